# Optimizing a Trainium2 kernel written in Bass

```python
import math
import jax
import jax.numpy as jnp
from jax import lax
import numpy as np

D_MODEL = 2048
BATCH = 2
SEQ = 4096
DEPTH = 2
DEC_BATCH = 128
DEC_SEQ = 8
PAST_LEN = 8192
PAGE_SIZE = 128

N_AB_LAYERS = (DEPTH + 1) // 2
N_SSM_LAYERS = DEPTH // 2

MLA_HEADS = 8
QK_NOPE_DIM = 128
QK_ROPE_DIM = 64
V_HEAD_DIM = 128
Q_LORA_RANK = 768
KV_LORA_RANK = 512
ROPE_THETA = 10000.0
ATTN_SCALE = 1.0 / math.sqrt(QK_NOPE_DIM + QK_ROPE_DIM)
ATTN_Q_BLOCK = 128

SCONV_GROUPS = 8
SCONV_DIM = D_MODEL // 2
SCONV_WIDTH = 3

SSM_EXPAND = 2
SSM_D_INNER = SSM_EXPAND * D_MODEL
SSM_HEAD_DIM = 64
SSM_HEADS = SSM_D_INNER // SSM_HEAD_DIM
SSM_GROUPS = 8
SSM_STATE = 128
SSM_CONV_WIDTH = 4
SSM_CONV_DIM = SSM_D_INNER + 2 * SSM_GROUPS * SSM_STATE
SSM_CHUNK = 128
DT_MIN = 0.001
DT_MAX = 0.1

D_FF = 5632
NORM_EPS = 1e-6

AB_IN_DIM = Q_LORA_RANK + KV_LORA_RANK + QK_ROPE_DIM + 3 * SCONV_DIM
AB_OUT_DIM = MLA_HEADS * V_HEAD_DIM + SCONV_DIM
SSM_IN_DIM = 2 * SSM_D_INNER + 2 * SSM_GROUPS * SSM_STATE + SSM_HEADS

kernel_name = 'macaron_mla_shortconv_mamba2_step'


def rmsnorm(x, g):
    x32 = x.astype(jnp.float32)
    y = x32 * lax.rsqrt(jnp.mean(x32 * x32, axis=-1, keepdims=True) + NORM_EPS)
    return (y * g.astype(jnp.float32)).astype(x.dtype)


def swiglu(x, w_gate, w_up, w_down):
    return (jax.nn.silu(x @ w_gate) * (x @ w_up)) @ w_down


def rope_tables(pos):
    inv_freq = jnp.power(ROPE_THETA, -jnp.arange(0, QK_ROPE_DIM, 2, dtype=jnp.float32) / QK_ROPE_DIM)
    ang = pos.astype(jnp.float32)[:, None] * inv_freq[None, :]
    return jnp.cos(ang), jnp.sin(ang)


def apply_rope(x, cos, sin):
    half = x.shape[-1] // 2
    x1, x2 = x[..., :half], x[..., half:]
    c = cos.astype(x.dtype)
    s = sin.astype(x.dtype)
    return jnp.concatenate([x1 * c - x2 * s, x2 * c + x1 * s], axis=-1)


def causal_dwconv(u, hist, w):
    width = w.shape[0]
    t = u.shape[1]
    full = jnp.concatenate([hist.astype(u.dtype), u], axis=1)
    y = full[:, 0:t] * w[0]
    for k in range(1, width):
        y = y + full[:, k:k + t] * w[k]
    return y, full[:, t:]


def mla_prefill(q_nope, q_rope, c_kv, k_rope, w_uk, w_uv):
    b, s_len, h, _ = q_nope.shape
    k_nope = jnp.einsum('bsc,chn->bshn', c_kv, w_uk)
    v = jnp.einsum('bsc,chv->bshv', c_kv, w_uv)
    nb = s_len // ATTN_Q_BLOCK
    qn = q_nope.reshape(b, nb, ATTN_Q_BLOCK, h, QK_NOPE_DIM).transpose(1, 0, 2, 3, 4)
    qr = q_rope.reshape(b, nb, ATTN_Q_BLOCK, h, QK_ROPE_DIM).transpose(1, 0, 2, 3, 4)
    key_pos = jnp.arange(s_len)

    def block(args):
        qn_b, qr_b, blk = args
        sc = jnp.einsum('bqhn,bshn->bhqs', qn_b, k_nope) + jnp.einsum('bqhr,bsr->bhqs', qr_b, k_rope)
        sc = sc.astype(jnp.float32) * ATTN_SCALE
        q_pos = blk * ATTN_Q_BLOCK + jnp.arange(ATTN_Q_BLOCK)
        mask = key_pos[None, :] <= q_pos[:, None]
        pr = jax.nn.softmax(jnp.where(mask, sc, -jnp.inf), axis=-1).astype(v.dtype)
        return jnp.einsum('bhqs,bshv->bqhv', pr, v)

    o = lax.map(block, (qn, qr, jnp.arange(nb)))
    return o.transpose(1, 0, 2, 3, 4).reshape(b, s_len, h, V_HEAD_DIM)


def mla_decode(q_nope, q_rope, c_kv, k_rope, ckv_past, kr_past, w_uk, w_uv):
    t = q_nope.shape[1]
    n_past = ckv_past.shape[1]
    q_lat = jnp.einsum('bthn,chn->bthc', q_nope, w_uk)
    s_past = jnp.einsum('bthc,bsc->bhts', q_lat, ckv_past) + jnp.einsum('bthr,bsr->bhts', q_rope, kr_past)
    s_new = jnp.einsum('bthc,bsc->bhts', q_lat, c_kv) + jnp.einsum('bthr,bsr->bhts', q_rope, k_rope)
    causal = jnp.tril(jnp.ones((t, t), dtype=bool))
    s_new = jnp.where(causal, s_new.astype(jnp.float32) * ATTN_SCALE, -jnp.inf)
    sc = jnp.concatenate([s_past.astype(jnp.float32) * ATTN_SCALE, s_new], axis=-1)
    pr = jax.nn.softmax(sc, axis=-1).astype(c_kv.dtype)
    o_lat = jnp.einsum('bhts,bsc->bthc', pr[..., :n_past], ckv_past) + jnp.einsum('bhts,bsc->bthc', pr[..., n_past:], c_kv)
    return jnp.einsum('bthc,chv->bthv', o_lat, w_uv)


def mixer_mla_sconv(h, cos, sin, p, i, past, sconv_hist):
    b, t, _ = h.shape
    proj = h @ p['ab_w_in'][i]
    o1 = Q_LORA_RANK
    o2 = o1 + KV_LORA_RANK
    o3 = o2 + QK_ROPE_DIM
    o4 = o3 + SCONV_DIM
    o5 = o4 + SCONV_DIM
    c_q, c_kv, k_r, gate_b, gate_c, v_in = jnp.split(proj, [o1, o2, o3, o4, o5], axis=-1)
    q = (rmsnorm(c_q, p['mla_q_norm'][i]) @ p['mla_w_uq'][i]).reshape(b, t, MLA_HEADS, QK_NOPE_DIM + QK_ROPE_DIM)
    q_nope = q[..., :QK_NOPE_DIM]
    q_rope = apply_rope(q[..., QK_NOPE_DIM:], cos[:, None, :], sin[:, None, :])
    c_kv = rmsnorm(c_kv, p['mla_kv_norm'][i])
    k_rope = apply_rope(k_r, cos, sin)
    if past is None:
        attn = mla_prefill(q_nope, q_rope, c_kv, k_rope, p['mla_w_uk'][i], p['mla_w_uv'][i])
    else:
        attn = mla_decode(q_nope, q_rope, c_kv, k_rope, past[0], past[1], p['mla_w_uk'][i], p['mla_w_uv'][i])
    u = gate_c * v_in
    yc, sc_state = causal_dwconv(u, sconv_hist, p['sconv_w'][i])
    sc_out = gate_b * yc
    merged = jnp.concatenate([attn.reshape(b, t, MLA_HEADS * V_HEAD_DIM), sc_out], axis=-1)
    return merged @ p['ab_w_out'][i], c_kv, k_rope, sc_state


def ssd_chunked(x, dt, a, bm, cm, h0):
    f32 = jnp.float32
    b, t, n_heads, hp = x.shape
    g, n = bm.shape[2], bm.shape[3]
    r = n_heads // g
    q = math.gcd(t, SSM_CHUNK)
    nc = t // q
    xr = x.astype(f32).reshape(b, nc, q, g, r, hp)
    dtr = dt.reshape(b, nc, q, g, r)
    br = bm.astype(f32).reshape(b, nc, q, g, n)
    cr = cm.astype(f32).reshape(b, nc, q, g, n)
    a_cum = jnp.cumsum(dtr * a.reshape(g, r), axis=2)
    causal = jnp.tril(jnp.ones((q, q), dtype=bool))[None, None, :, :, None, None]
    seg = a_cum[:, :, :, None] - a_cum[:, :, None, :]
    decay_in = jnp.exp(jnp.where(causal, seg, -jnp.inf))
    xdt = xr * dtr[..., None]
    cb = jnp.einsum('bcign,bcjgn->bcijg', cr, br)
    y_diag = jnp.einsum('bcijg,bcijgr,bcjgrp->bcigrp', cb, decay_in, xdt)
    decay_end = jnp.exp(a_cum[:, :, -1:] - a_cum)
    chunk_states = jnp.einsum('bcjgn,bcjgr,bcjgrp->bcgrpn', br, decay_end, xdt)
    chunk_decay = jnp.exp(a_cum[:, :, -1])

    def step(hc, inp):
        dec, st = inp
        return dec[..., None, None] * hc + st, hc

    h_final, h_before = lax.scan(step, h0.astype(f32).reshape(b, g, r, hp, n),
                                 (chunk_decay.transpose(1, 0, 2, 3), chunk_states.transpose(1, 0, 2, 3, 4, 5)))
    h_before = h_before.transpose(1, 0, 2, 3, 4, 5)
    y_off = jnp.einsum('bcign,bcgrpn,bcigr->bcigrp', cr, h_before, jnp.exp(a_cum))
    y = (y_diag + y_off).reshape(b, t, n_heads, hp).astype(x.dtype)
    return y, h_final.reshape(b, n_heads, hp, n).astype(h0.dtype)


def mixer_ssd(h, p, i, conv_hist, h0):
    b, t, _ = h.shape
    proj = h @ p['ssm_w_in'][i]
    z, xbc, dt_raw = jnp.split(proj, [SSM_D_INNER, SSM_D_INNER + SSM_CONV_DIM], axis=-1)
    xbc, conv_state = causal_dwconv(xbc, conv_hist, p['ssm_conv_w'][i])
    xbc = jax.nn.silu(xbc + p['ssm_conv_b'][i])
    xs, bm, cm = jnp.split(xbc, [SSM_D_INNER, SSM_D_INNER + SSM_GROUPS * SSM_STATE], axis=-1)
    xs = xs.reshape(b, t, SSM_HEADS, SSM_HEAD_DIM)
    bm = bm.reshape(b, t, SSM_GROUPS, SSM_STATE)
    cm = cm.reshape(b, t, SSM_GROUPS, SSM_STATE)
    dt = jax.nn.softplus(dt_raw.astype(jnp.float32) + p['ssm_dt_bias'][i].astype(jnp.float32))
    a = -jnp.exp(p['ssm_a_log'][i].astype(jnp.float32))
    y, h_new = ssd_chunked(xs, dt, a, bm, cm, h0)
    y = y + p['ssm_d'][i][:, None].astype(y.dtype) * xs
    gated = y.reshape(b, t, SSM_D_INNER) * jax.nn.silu(z)
    gated = rmsnorm(gated.reshape(b, t, SSM_GROUPS, SSM_D_INNER // SSM_GROUPS),
                    p['ssm_norm'][i].reshape(SSM_GROUPS, SSM_D_INNER // SSM_GROUPS)).reshape(b, t, SSM_D_INNER)
    return gated @ p['ssm_w_out'][i], conv_state, h_new


def trunk(x, pos, p, paged, sconv_hist, ssm_conv_hist, ssm_h0):
    cos, sin = rope_tables(pos)
    ckv_rows, kr_rows, sc_states, ssc_states, ssm_states = [], [], [], [], []
    for layer in range(DEPTH):
        x = x + 0.5 * swiglu(rmsnorm(x, p['norm_ffn'][layer, 0]), p['ffn_w_gate'][layer, 0],
                             p['ffn_w_up'][layer, 0], p['ffn_w_down'][layer, 0])
        h = rmsnorm(x, p['norm_mix'][layer])
        i = layer // 2
        if layer % 2 == 0:
            past = None
            if paged is not None:
                cache_ckv, cache_krope, page_table = paged
                nb, n_pages = page_table.shape
                past = (cache_ckv[i, page_table].reshape(nb, n_pages * PAGE_SIZE, KV_LORA_RANK),
                        cache_krope[i, page_table].reshape(nb, n_pages * PAGE_SIZE, QK_ROPE_DIM))
            mix, c_kv, k_rope, sc_state = mixer_mla_sconv(h, cos, sin, p, i, past, sconv_hist[i])
            ckv_rows.append(c_kv)
            kr_rows.append(k_rope)
            sc_states.append(sc_state)
        else:
            mix, ssc_state, ssm_state = mixer_ssd(h, p, i, ssm_conv_hist[i], ssm_h0[i])
            ssc_states.append(ssc_state)
            ssm_states.append(ssm_state)
        x = x + mix
        x = x + 0.5 * swiglu(rmsnorm(x, p['norm_ffn'][layer, 1]), p['ffn_w_gate'][layer, 1],
                             p['ffn_w_up'][layer, 1], p['ffn_w_down'][layer, 1])
    y = rmsnorm(x, p['norm_final'])
    return y, jnp.stack(ckv_rows), jnp.stack(kr_rows), jnp.stack(sc_states), jnp.stack(ssc_states), jnp.stack(ssm_states)


def setup_inputs(seed: int = 0) -> dict:
    key = jax.random.key(seed)
    ks = jax.random.split(key, 40)
    f32 = jnp.float32

    def nrm(k, shape, scale):
        return jax.random.normal(k, shape, f32) * scale

    def gain(k, shape):
        return 1.0 + 0.02 * jax.random.normal(k, shape, f32)

    n_pages = PAST_LEN // PAGE_SIZE
    n_used = DEC_BATCH * n_pages
    n_pool = n_used + max(1, n_used // 4)
    page_table = jax.random.permutation(ks[0], n_pool)[:n_used].reshape(DEC_BATCH, n_pages).astype(jnp.int32)
    dt0 = jnp.exp(jax.random.uniform(ks[1], (N_SSM_LAYERS, SSM_HEADS), f32, math.log(DT_MIN), math.log(DT_MAX)))
    dt_bias = dt0 + jnp.log(-jnp.expm1(-dt0))
    a_log = jnp.log(jax.random.uniform(ks[2], (N_SSM_LAYERS, SSM_HEADS), f32, 1.0, 16.0))
    return {
        'x_prompt': nrm(ks[3], (BATCH, SEQ, D_MODEL), 1.0),
        'x_sample': nrm(ks[4], (DEC_BATCH, DEC_SEQ, D_MODEL), 1.0),
        'cache_ckv': nrm(ks[5], (N_AB_LAYERS, n_pool, PAGE_SIZE, KV_LORA_RANK), 1.0),
        'cache_krope': nrm(ks[6], (N_AB_LAYERS, n_pool, PAGE_SIZE, QK_ROPE_DIM), 1.0),
        'state_sconv': nrm(ks[7], (N_AB_LAYERS, DEC_BATCH, SCONV_WIDTH - 1, SCONV_DIM), 1.0),
        'state_ssm_conv': nrm(ks[8], (N_SSM_LAYERS, DEC_BATCH, SSM_CONV_WIDTH - 1, SSM_CONV_DIM), 1.0),
        'state_ssm': nrm(ks[9], (N_SSM_LAYERS, DEC_BATCH, SSM_HEADS, SSM_HEAD_DIM, SSM_STATE), 0.5),
        'page_table': page_table,
        'norm_ffn': gain(ks[10], (DEPTH, 2, D_MODEL)),
        'ffn_w_gate': nrm(ks[11], (DEPTH, 2, D_MODEL, D_FF), D_MODEL ** -0.5),
        'ffn_w_up': nrm(ks[12], (DEPTH, 2, D_MODEL, D_FF), D_MODEL ** -0.5),
        'ffn_w_down': nrm(ks[13], (DEPTH, 2, D_FF, D_MODEL), D_FF ** -0.5),
        'norm_mix': gain(ks[14], (DEPTH, D_MODEL)),
        'ab_w_in': nrm(ks[15], (N_AB_LAYERS, D_MODEL, AB_IN_DIM), D_MODEL ** -0.5),
        'mla_q_norm': gain(ks[16], (N_AB_LAYERS, Q_LORA_RANK)),
        'mla_w_uq': nrm(ks[17], (N_AB_LAYERS, Q_LORA_RANK, MLA_HEADS * (QK_NOPE_DIM + QK_ROPE_DIM)), Q_LORA_RANK ** -0.5),
        'mla_kv_norm': gain(ks[18], (N_AB_LAYERS, KV_LORA_RANK)),
        'mla_w_uk': nrm(ks[19], (N_AB_LAYERS, KV_LORA_RANK, MLA_HEADS, QK_NOPE_DIM), KV_LORA_RANK ** -0.5),
        'mla_w_uv': nrm(ks[20], (N_AB_LAYERS, KV_LORA_RANK, MLA_HEADS, V_HEAD_DIM), KV_LORA_RANK ** -0.5),
        'sconv_w': nrm(ks[21], (N_AB_LAYERS, SCONV_WIDTH, SCONV_DIM), SCONV_WIDTH ** -0.5),
        'ab_w_out': nrm(ks[22], (N_AB_LAYERS, AB_OUT_DIM, D_MODEL), AB_OUT_DIM ** -0.5),
        'ssm_w_in': nrm(ks[23], (N_SSM_LAYERS, D_MODEL, SSM_IN_DIM), D_MODEL ** -0.5),
        'ssm_conv_w': nrm(ks[24], (N_SSM_LAYERS, SSM_CONV_WIDTH, SSM_CONV_DIM), SSM_CONV_WIDTH ** -0.5),
        'ssm_conv_b': nrm(ks[25], (N_SSM_LAYERS, SSM_CONV_DIM), 0.02),
        'ssm_dt_bias': dt_bias,
        'ssm_a_log': a_log,
        'ssm_d': 1.0 + 0.1 * jax.random.normal(ks[26], (N_SSM_LAYERS, SSM_HEADS), f32),
        'ssm_norm': gain(ks[27], (N_SSM_LAYERS, SSM_D_INNER)),
        'ssm_w_out': nrm(ks[28], (N_SSM_LAYERS, SSM_D_INNER, D_MODEL), SSM_D_INNER ** -0.5),
        'norm_final': gain(ks[29], (D_MODEL,)),
    }


def reference(x_prompt, x_sample, cache_ckv, cache_krope, state_sconv, state_ssm_conv, state_ssm, page_table,
              norm_ffn, ffn_w_gate, ffn_w_up, ffn_w_down, norm_mix, ab_w_in, mla_q_norm, mla_w_uq, mla_kv_norm,
              mla_w_uk, mla_w_uv, sconv_w, ab_w_out, ssm_w_in, ssm_conv_w, ssm_conv_b, ssm_dt_bias, ssm_a_log,
              ssm_d, ssm_norm, ssm_w_out, norm_final):
    p = dict(norm_ffn=norm_ffn, ffn_w_gate=ffn_w_gate, ffn_w_up=ffn_w_up, ffn_w_down=ffn_w_down,
             norm_mix=norm_mix, ab_w_in=ab_w_in, mla_q_norm=mla_q_norm, mla_w_uq=mla_w_uq,
             mla_kv_norm=mla_kv_norm, mla_w_uk=mla_w_uk, mla_w_uv=mla_w_uv, sconv_w=sconv_w,
             ab_w_out=ab_w_out, ssm_w_in=ssm_w_in, ssm_conv_w=ssm_conv_w, ssm_conv_b=ssm_conv_b,
             ssm_dt_bias=ssm_dt_bias, ssm_a_log=ssm_a_log, ssm_d=ssm_d, ssm_norm=ssm_norm,
             ssm_w_out=ssm_w_out, norm_final=norm_final)
    bp, sp, _ = x_prompt.shape
    pos_p = jnp.arange(sp, dtype=jnp.int32)
    zero_sc = jnp.zeros((N_AB_LAYERS, bp, SCONV_WIDTH - 1, SCONV_DIM), x_prompt.dtype)
    zero_ssc = jnp.zeros((N_SSM_LAYERS, bp, SSM_CONV_WIDTH - 1, SSM_CONV_DIM), x_prompt.dtype)
    zero_h = jnp.zeros((N_SSM_LAYERS, bp, SSM_HEADS, SSM_HEAD_DIM, SSM_STATE), x_prompt.dtype)
    y_prompt, p_ckv, p_krope, p_sconv, p_ssm_conv, p_ssm = trunk(x_prompt, pos_p, p, None, zero_sc, zero_ssc, zero_h)
    past_len = page_table.shape[1] * PAGE_SIZE
    pos_s = past_len + jnp.arange(x_sample.shape[1], dtype=jnp.int32)
    y_sample, s_ckv, s_krope, s_sconv, s_ssm_conv, s_ssm = trunk(
        x_sample, pos_s, p, (cache_ckv, cache_krope, page_table), state_sconv, state_ssm_conv, state_ssm)
    return (y_prompt, y_sample, p_ckv, p_krope, p_sconv, p_ssm_conv, p_ssm, s_ckv, s_krope, s_sconv, s_ssm_conv, s_ssm)
```

```python
import math
import numpy as np
import concourse.bass as bass
import concourse.mybir as mybir
from concourse.bass_utils import run_bass_kernel_spmd

F32 = mybir.dt.float32
BF16 = mybir.dt.bfloat16
I32 = mybir.dt.int32
AF = mybir.ActivationFunctionType
ALU = mybir.AluOpType
EPOCH = 16000

D = 2048
KC = 16
FF = 5632
NTOK = 1152
NP_ = 1024
ND = 128
TB = [(0, 512), (512, 512), (1024, 128)]
EPS = 1e-6
QL = 768
KVL = 512
ROPE = 64
SCD = 1024
AB_IN = 4416
O1, O2, O3, O4, O5 = 768, 1280, 1344, 2368, 3392
SCALE = 1.0 / math.sqrt(192.0)
NEG = -30000.0


class Chan:
    __slots__ = ("sem", "count", "name", "inc")

    def __init__(self, name, inc=16):
        self.name = name
        self.sem = None
        self.count = 0
        self.inc = inc


class Buf:
    __slots__ = ("name", "ap", "w", "rd", "chan", "dma_w", "dma_r", "pre_chan", "ckey")

    def __init__(self, name, ap, ckey=None):
        self.name = name
        self.ckey = ckey
        self.ap = ap
        self.w = None
        self.rd = []
        self.chan = None
        self.dma_w = False
        self.dma_r = False
        self.pre_chan = []

    def __getitem__(self, k):
        return self.ap[k]


class Op:
    __slots__ = ("eng", "fn", "deps", "cwaits", "dma", "chan", "signal", "tok")

    def __init__(self, eng, fn):
        self.eng = eng
        self.fn = fn
        self.deps = []
        self.cwaits = []
        self.dma = False
        self.chan = None
        self.signal = False
        self.tok = None


class Sched:
    ENGS = ("pe", "act", "dve", "pool", "sp")

    def __init__(self, nc):
        self.nc = nc
        self.ops = {e: [] for e in self.ENGS}
        self.chans = []
        self.chan_reg = {}
        self.nops = 0

    def new_chan(self, name, inc=16):
        c = self.chan_reg.get(name)
        if c is None:
            c = Chan(name, inc)
            self.chans.append(c)
            self.chan_reg[name] = c
        return c

    def op(self, eng, fn, reads=(), writes=(), dma=False, group=False, chan_buf=None, inc=16):
        o = Op(eng, fn)
        deps = {}
        cw = {}

        def addc(c, v):
            old = cw.get(id(c))
            if old is None or old[1] < v:
                cw[id(c)] = (c, v)

        for b in reads:
            if b.w is not None:
                deps[id(b.w)] = b.w
            if b.dma_w:
                addc(b.chan, b.chan.count)
            for (c, v) in b.pre_chan:
                addc(c, v)
        for b in writes:
            if b.w is not None:
                deps[id(b.w)] = b.w
            for r in b.rd:
                deps[id(r)] = r
            if (b.dma_w or b.dma_r) and not (group and dma):
                addc(b.chan, b.chan.count)
            for (c, v) in b.pre_chan:
                addc(c, v)
        for d in deps.values():
            if d.eng == "pe" and eng == "pe":
                continue
            d.signal = True
            o.deps.append(d)
        if dma:
            cb = chan_buf if chan_buf is not None else (list(writes) + list(reads))[0]
            if cb.chan is None:
                cb.chan = self.new_chan(cb.ckey or cb.name, inc)
            ch = cb.chan
            if not group and ch.count > 0:
                addc(ch, ch.count)
        o.cwaits = list(cw.values())
        if dma:
            o.dma = True
            ch.count += ch.inc
            o.chan = ch
            for b in writes:
                if b.chan is None:
                    b.chan = ch
                if not group:
                    b.w = None
                    b.rd = []
                    b.pre_chan = []
                    b.dma_w = False
                    b.dma_r = False
                if b.chan is ch:
                    b.dma_w = True
                else:
                    b.pre_chan.append((ch, ch.count))
            for b in reads:
                if b.chan is None:
                    b.chan = ch
                if b.chan is ch:
                    b.dma_r = True
                else:
                    b.pre_chan.append((ch, ch.count))
        else:
            for b in writes:
                b.w = o
                b.rd = []
                b.dma_w = False
                b.dma_r = False
                b.pre_chan = []
            for b in reads:
                b.rd.append(o)
        self.ops[eng].append(o)
        self.nops += 1
        return o

    def emit(self, final_wait_eng="sp"):
        nc = self.nc
        esems = {}
        pos = {}
        for e in self.ENGS:
            for i, o in enumerate(self.ops[e]):
                pos[id(o)] = i
                o.signal = False
        for e in self.ENGS:
            seenp = {}
            for o in self.ops[e]:
                best = {}
                for d in o.deps:
                    if d.dma:
                        continue
                    pi = pos[id(d)]
                    if pi > seenp.get(d.eng, -1) and pi > best.get(d.eng, (-1, None))[0]:
                        best[d.eng] = (pi, d)
                for de, (pi, d) in best.items():
                    d.signal = True
                    seenp[de] = pi
        for e in self.ENGS:
            n = 0
            for o in self.ops[e]:
                if o.signal and not o.dma:
                    o.tok = (e, n // EPOCH, n % EPOCH + 1)
                    n += 1
                else:
                    o.tok = None
            nep = (n + EPOCH - 1) // EPOCH
            esems[e] = [nc.alloc_semaphore(f"es_{e}_{i}") for i in range(max(nep, 1))]
            print("signals", e, n)
        for i, c in enumerate(self.chans):
            c.sem = nc.alloc_semaphore(f"ch{i}")
        sched = self

        def run(ename, eng):
            seen = {}
            for o in sched.ops[ename]:
                for d in o.deps:
                    if d.tok is None:
                        continue
                    (e, ep, v) = d.tok
                    key = (e, ep)
                    if seen.get(key, 0) < v:
                        eng.wait_ge(esems[e][ep], v)
                        seen[key] = v
                for (c, v) in o.cwaits:
                    key = id(c)
                    if seen.get(key, 0) < v:
                        eng.wait_ge(c.sem, v)
                        seen[key] = v
                ins = o.fn(eng)
                if o.dma:
                    if o.chan.inc == 16:
                        ins.then_inc(o.chan.sem, 16)
                    else:
                        ins.then_inc(o.chan.sem)
                elif o.signal:
                    (e, ep, v) = o.tok
                    ins.then_inc(esems[e][ep], 1)
            if ename == final_wait_eng:
                for c in sched.chans:
                    if c.count > 0 and seen.get(id(c), 0) < c.count:
                        eng.wait_ge(c.sem, c.count)

        with nc.Block() as block:
            @block.tensor
            def _(t):
                run("pe", t)

            @block.scalar
            def _(a):
                run("act", a)

            @block.vector
            def _(v):
                run("dve", v)

            @block.gpsimd
            def _(g):
                run("pool", g)

            @block.sync
            def _(s):
                run("sp", s)


class Arena:
    def __init__(self, nc, ncols, name="arena"):
        self.t = nc.alloc_sbuf_tensor(name, [128, ncols], F32)
        self.ncols = ncols
        self.live = []
        self.dead = []

    def _find(self, cols32):
        pos = 0
        for (s, e, _) in sorted(self.live, key=lambda x: x[0]):
            if s - pos >= cols32:
                return pos
            pos = max(pos, e)
        assert pos + cols32 <= self.ncols, f"arena full: need {cols32} at {pos} of {self.ncols}"
        return pos

    def alloc(self, name, cols, dtype=F32, ckey=None):
        cols32 = (cols + 1) // 2 if dtype == BF16 else cols
        cols32 = (cols32 + 7) // 8 * 8
        start = self._find(cols32)
        end = start + cols32
        ap = self.t[:][:, start:end]
        if dtype == F32:
            ap = ap[:, :cols]
        else:
            ap = ap.bitcast(dtype)[:, :cols]
        b = Buf(name, ap, ckey)
        keep = []
        for (s, e, ob) in self.dead:
            if s < end and start < e:
                if ob.w is not None:
                    b.rd.append(ob.w)
                b.rd.extend(ob.rd)
                if (ob.dma_w or ob.dma_r) and ob.chan is not None:
                    b.pre_chan.append((ob.chan, ob.chan.count))
                b.pre_chan.extend(ob.pre_chan)
                if s < start or e > end:
                    keep.append((s, e, ob))
            else:
                keep.append((s, e, ob))
        self.dead = keep
        self.live.append((start, end, b))
        return b

    def free(self, *bufs):
        for b in bufs:
            for i, (s, e, ob) in enumerate(self.live):
                if ob is b:
                    self.live.pop(i)
                    self.dead.append((s, e, ob))
                    break
            else:
                raise KeyError(b.name)

    def used(self):
        return max([e for (_, e, _) in self.live] + [0])


class Prog:
    def __init__(self, mode="full", npool=10240, stop=99):
        self.mode = mode
        self.stop = stop
        self.npool = npool
        nc = self.nc = bass.Bass("TRN2", target_bir_lowering=False)
        self.S = Sched(nc)
        self.A = Arena(nc, 52000)
        self.PS = [Buf(f"ps{i}", nc.alloc_psum_tensor(f"ps{i}", [128, 512], F32)[:]) for i in range(8)]
        self.psi = 0
        self.din = {}
        self.dout = {}

    def inp(self, name, shape, dt=F32):
        if name in self.din:
            return self.din[name].ap()
        t = self.nc.dram_tensor(name, list(shape), dt, kind="ExternalInput")
        self.din[name] = t
        return t.ap()

    def outp(self, name, shape, dt=F32):
        t = self.nc.dram_tensor(name, list(shape), dt, kind="ExternalOutput")
        self.dout[name] = t
        return t.ap()

    def nextps(self):
        rot = getattr(self, "rot", None) or list(range(8))
        b = self.PS[rot[self.psi % len(rot)]]
        self.psi += 1
        return b

    def mm(self, ps, out_ap, lhsT, lhsT_ap, rhs, rhs_ap, start, stop):
        self.S.op("pe", lambda e: e.matmul(out_ap, lhsT_ap, rhs_ap, start=start, stop=stop),
                  reads=[lhsT, rhs], writes=[ps])

    def dma(self, eng, out_ap, in_ap, reads=(), writes=(), **kw):
        return self.S.op(eng, lambda e: e.dma_start(out=out_ap, in_=in_ap), reads=reads, writes=writes, dma=True, **kw)

    def dma_nc(self, eng, out_ap, in_ap, reads=(), writes=(), **kw):
        return self.S.op(eng, lambda e: e.dma_start(out=out_ap, in_=in_ap, allow_slow_non_contiguous=True),
                         reads=reads, writes=writes, dma=True, **kw)

    def rmsnorm_fm(self, src, dst, gain, gcol0, dim, blocks, src_cols=None):
        S, A = self.S, self.A
        nk = len(src)
        sq = [A.alloc(f"sq{i}", 512, BF16) for i in range(2)]
        rstd = A.alloc("rstd", 512)
        for (s, n) in blocks:
            ps = self.nextps()
            for k in range(nk):
                q = sq[k % 2]
                S.op("act", lambda e, k=k, q=q, s=s, n=n: e.activation(out=q[:, :n], in_=src[k][:, s:s + n], func=AF.Square),
                     reads=[src[k]], writes=[q])
                self.mm(ps, ps[:, :n], self.ones, self.ones[:, :], q, q[:, :n], k == 0, k == nk - 1)
            S.op("act", lambda e, ps=ps, n=n: e.activation(out=rstd[:, :n], in_=ps[:, :n], func=AF.Sqrt, bias=self.epsb[:, :], scale=1.0 / dim),
                 reads=[ps, self.epsb], writes=[rstd])
            S.op("dve", lambda e, n=n: e.reciprocal(out=rstd[:, :n], in_=rstd[:, :n]), reads=[rstd], writes=[rstd])
            for k in range(nk):
                S.op("dve", lambda e, k=k, s=s, n=n: e.scalar_tensor_tensor(
                    out=dst[k][:, s:s + n], in0=src[k][:, s:s + n], scalar=gain[:, gcol0 + k:gcol0 + k + 1],
                    in1=rstd[:, :n], op0=ALU.mult, op1=ALU.mult), reads=[src[k], gain, rstd], writes=[dst[k]])
        A.free(sq[0], sq[1], rstd)

    def ffn(self, widx, hT, xT):
        S, A = self.S, self.A
        wgv = self.w_gate[widx].rearrange("(k p) n -> p k n", p=128)
        wuv = self.w_up[widx].rearrange("(k p) n -> p k n", p=128)
        wdv = self.w_down[widx]
        NFC = FF // 128
        NPAIR = NFC // 2
        GP = 2
        gu = [[A.alloc(f"gu{m}{sl}", KC * 256, BF16) for sl in range(2)] for m in range(2)]
        dn = [A.alloc(f"dn{sl}", D, BF16) for sl in range(6)]
        act = [A.alloc(f"act{c}", NTOK, BF16) for c in range(4)]
        sg = [A.alloc(f"sg{i}", 512, BF16) for i in range(2)]

        def load_gu(j):
            sl = j % 2
            for m, wv in ((0, wgv), (1, wuv)):
                b = gu[m][sl]
                self.dma("pool", b[:, :].rearrange("p (k n) -> p k n", k=KC), wv[:, :, j * 256:(j + 1) * 256], writes=[b])

        def load_dn(f):
            b = dn[f % 6]
            self.dma("pool", b[:, :], wdv[f * 128:(f + 1) * 128, :], writes=[b])

        load_gu(0)
        sgi = 0
        for j in range(NPAIR):
            if j + 1 < NPAIR:
                load_gu(j + 1)
            g0 = (j // GP) * GP
            gend = min(g0 + GP, NPAIR)
            if j == g0:
                for jj in range(g0, gend):
                    load_dn(2 * jj)
                    load_dn(2 * jj + 1)
            for c2 in range(2):
                f = 2 * j + c2
                ci = f % 4
                for (s, n) in TB:
                    pg = self.nextps()
                    pu = self.nextps()
                    for m, ps in ((0, pg), (1, pu)):
                        w = gu[m][j % 2]
                        for k in range(KC):
                            c0 = k * 256 + c2 * 128
                            self.mm(ps, ps[:, :n], w, w[:, c0:c0 + 128], hT[k], hT[k][:, s:s + n], k == 0, k == KC - 1)
                    t = sg[sgi % 2]
                    sgi += 1
                    S.op("act", lambda e, t=t, pg=pg, n=n: e.activation(out=t[:, :n], in_=pg[:, :n], func=AF.Silu),
                         reads=[pg], writes=[t])
                    S.op("dve", lambda e, t=t, pu=pu, ci=ci, s=s, n=n: e.tensor_tensor(
                        out=act[ci][:, s:s + n], in0=t[:, :n], in1=pu[:, :n], op=ALU.mult), reads=[t, pu], writes=[act[ci]])
            if j == gend - 1:
                fs = list(range(2 * g0, 2 * j + 2))
                for dc in range(KC):
                    for (s, n) in TB:
                        ps = self.nextps()
                        for i, f in enumerate(fs):
                            self.mm(ps, ps[:, :n], dn[f % 6], dn[f % 6][:, dc * 128:(dc + 1) * 128],
                                    act[f % 4], act[f % 4][:, s:s + n], i == 0, i == len(fs) - 1)
                        S.op("dve", lambda e, ps=ps, dc=dc, s=s, n=n: e.scalar_tensor_tensor(
                            out=xT[dc][:, s:s + n], in0=ps[:, :n], scalar=0.5, in1=xT[dc][:, s:s + n],
                            op0=ALU.mult, op1=ALU.add), reads=[ps, xT[dc]], writes=[xT[dc]])
        for m in range(2):
            A.free(*gu[m])
        A.free(*dn)
        A.free(*act)
        A.free(*sg)

    def build(self):
        nc, S, A = self.nc, self.S, self.A
        xT_d = self.inp("xT", [128, KC * NTOK])
        pv_d = self.inp("pv", [128, PV_N])
        mode = self.mode
        if mode == "full":
            self.w_gate = self.inp("ffn_w_gate", [4, D, FF])
            self.w_up = self.inp("ffn_w_up", [4, D, FF])
            self.w_down = self.inp("ffn_w_down", [4, FF, D])
        o_yT = self.outp("o_yT", [128, KC * NTOK])
        if mode in ("full", "l0mix"):
            w_in = self.inp("ab_w_in", [D, AB_IN])
            kvn_d = self.inp("kvn", [128, KVL])
            rope_tm_d = self.inp("rope_tm", [128, 9 * 128])
            sc_hist_d = self.inp("sc_hist", [128, 8 * 32])
            o_ckv = self.outp("o_ckv", [NTOK, KVL])
            o_kr = self.outp("o_kr", [NTOK, ROPE])
            o_sc = self.outp("o_sc", [128, 8 * 34])
        if mode in ("full", "l1mix"):
            o_ssc = self.outp("o_ssc", [128, 48 * 51])
            o_pssm = self.outp("o_pssm", [128, 32 * 128])
            o_sssm = self.outp("o_sssm", [16 * 128, 32 * 128])

        xT = [A.alloc(f"xT{k}", NTOK, ckey="xT") for k in range(KC)]
        pv = A.alloc("pv", PV_N, ckey="once")
        self.ones = A.alloc("ones", 128, BF16)
        self.epsb = A.alloc("epsb", 1)
        xv = xT_d.rearrange("p (k t) -> p k t", k=KC)
        for k in range(KC):
            self.dma("sp", xT[k][:, :], xv[:, k, :], writes=[xT[k]], group=(k > 0))
        self.dma("sp", pv[:, :], pv_d[:, :], writes=[pv])
        S.op("pool", lambda e: e.memset(self.ones[:, :], 1.0), writes=[self.ones])
        S.op("pool", lambda e: e.memset(self.epsb[:, :], EPS), writes=[self.epsb])

        hT = [A.alloc(f"hT{k}", NTOK, BF16) for k in range(KC)]
        full = self.mode == "full"
        cv = self.cv = A.alloc("cv", 32, ckey="once")
        self.dma("sp", cv[:, :], self.inp("cv", [128, 32])[:, :], writes=[cv])

        if full:
            self.rmsnorm_fm(xT, hT, pv, PV_NORM_FFN + 0 * 16, D, TB)
            self.ffn(0, hT, xT)
        if self.mode in ("full", "l0mix"):
            self.rmsnorm_fm(xT, hT, pv, PV_NORM_MIX + 0, D, TB)
            self.spill_x(xT)
            self.merged = [None] * 16
            st = self.stop
            if st >= 1:
                self.l0_inproj(hT, xT, pv, w_in, kvn_d, rope_tm_d, sc_hist_d, o_ckv, o_kr, o_sc)
            if st >= 2:
                self.l0_latents_fm(hT, pv, w_in, self.inp("rope_fm", [128, 2 * NTOK]))
            A.free(*hT)
            if st >= 3:
                self.l0_exchange()
            if st >= 4:
                w_uq_d = self.inp("mla_w_uq", [QL, 1536])
                w_uk_d = self.inp("mla_w_uk", [KVL, 1024])
                w_uv_d = self.inp("mla_w_uv", [KVL, 1024])
                self.l0_prefill_attn(cv, w_uq_d, w_uk_d, w_uv_d, self.inp("cmask", [128, 2048]))
            if st >= 5:
                self.l0_decode_attn(cv, self.inp("cache_ckv", [self.npool * 16, 4096]), self.inp("cache_krope", [self.npool * 16, 512]),
                                    self.inp("ptx", [128, 128], I32), self.inp("w_ukT", [1024, KVL]), w_uv_d,
                                    self.inp("dmask", [128, 1024]), self.inp("ident", [128, 128]))
            if st >= 6:
                self.l0_sconv_patch(pv, cv)
            if st >= 6:
                xT = self.reload_x()
                self.out_proj(self.inp("ab_w_out", [D, D]), 16, xT)
                A.free(*self.merged)
            hT = [A.alloc(f"hT{k}", NTOK, BF16) for k in range(KC)]
        if full:
            self.rmsnorm_fm(xT, hT, pv, PV_NORM_FFN + 1 * 16, D, TB)
            self.ffn(1, hT, xT)
            self.rmsnorm_fm(xT, hT, pv, PV_NORM_FFN + 2 * 16, D, TB)
            self.ffn(2, hT, xT)
        if self.mode in ("full", "l1mix"):
            self.rmsnorm_fm(xT, hT, pv, PV_NORM_MIX + 16, D, TB)
            self.ssd_mixer(hT, xT, o_ssc, o_pssm, o_sssm)
        if full:
            self.rmsnorm_fm(xT, hT, pv, PV_NORM_FFN + 3 * 16, D, TB)
            self.ffn(3, hT, xT)
            self.rmsnorm_fm(xT, xT, pv, PV_NORM_FIN, D, TB)

        yv = o_yT.rearrange("p (k t) -> p k t", k=KC)
        for k in range(KC):
            if self.stop < 6:
                break
            self.dma("sp", yv[:, k, :], xT[k][:, :], reads=[xT[k]])
        print("nops", S.nops, {e: len(v) for e, v in S.ops.items()}, "arena used", A.used(), "chans", len(S.chans))
        S.emit()
        return nc

    def l0_inproj(self, hT, xT, pv, w_in, kvn_d, rope_tm_d, sc_hist_d, o_ckv, o_kr, o_sc):
        S, A = self.S, self.A
        wv = w_in.rearrange("(k p) n -> p k n", p=128)
        wkv = A.alloc("wkv", KC * KVL, BF16)
        wkr = A.alloc("wkr", KC * 128, BF16)
        kvn = A.alloc("kvn", KVL, ckey="once")
        ropet = A.alloc("ropet", 9 * 128, ckey="once")
        self.dma("pool", wkv[:, :].rearrange("p (k n) -> p k n", k=KC), wv[:, :, O1:O2], writes=[wkv])
        wkr3 = wkr[:, :].rearrange("p (k n) -> p k n", k=KC)
        self.dma("pool", wkr3[:, :, 0:64], wv[:, :, O2:O3], writes=[wkr])
        self.dma("pool", wkr3[:, :, 64:96], wv[:, :, O2 + 32:O3], writes=[wkr], group=True)
        self.dma("pool", wkr3[:, :, 96:128], wv[:, :, O2:O2 + 32], writes=[wkr], group=True)
        self.dma("sp", kvn[:, :], kvn_d[:, :], writes=[kvn])
        self.dma("sp", ropet[:, :], rope_tm_d[:, :], writes=[ropet])
        ckv_o = [A.alloc(f"ckvo{i}", KVL, ckey="ost") for i in range(2)]
        kr_o = [A.alloc(f"kro{i}", ROPE, ckey="ost") for i in range(2)]
        junk = A.alloc("junk", KVL, BF16)
        ssq = A.alloc("ssq", 2)
        t64 = A.alloc("t64", 64)
        for ti in range(9):
            c0 = ti * 128
            pk = self.nextps()
            pr = self.nextps()
            for k in range(KC):
                self.mm(pk, pk[:, :KVL], hT[k], hT[k][:, c0:c0 + 128], wkv, wkv[:, k * KVL:(k + 1) * KVL], k == 0, k == KC - 1)
            for k in range(KC):
                self.mm(pr, pr[:, :128], hT[k], hT[k][:, c0:c0 + 128], wkr, wkr[:, k * 128:(k + 1) * 128], k == 0, k == KC - 1)
            co = ckv_o[ti % 2]
            ko = kr_o[ti % 2]
            S.op("act", lambda e, pk=pk: e.activation(out=junk[:, :], in_=pk[:, :KVL], func=AF.Square, accum_out=ssq[:, 0:1]),
                 reads=[pk], writes=[junk, ssq])
            S.op("act", lambda e: e.activation(out=ssq[:, 1:2], in_=ssq[:, 0:1], func=AF.Sqrt, bias=self.epsb[:, :], scale=1.0 / KVL),
                 reads=[ssq, self.epsb], writes=[ssq])
            S.op("dve", lambda e: e.reciprocal(out=ssq[:, 1:2], in_=ssq[:, 1:2]), reads=[ssq], writes=[ssq])
            S.op("dve", lambda e, pk=pk, co=co: e.scalar_tensor_tensor(out=co[:, :], in0=pk[:, :KVL], scalar=ssq[:, 1:2], in1=kvn[:, :],
                                                                       op0=ALU.mult, op1=ALU.mult), reads=[pk, ssq, kvn], writes=[co])
            self.dma("sp", o_ckv[c0:c0 + 128, :], co[:, :], reads=[co])
            if ti == 8:
                self.ckv_dec = A.alloc("ckv_dec", KVL, BF16)
                S.op("pool", lambda e, co=co: e.tensor_copy(out=self.ckv_dec[:, :], in_=co[:, :]), reads=[co], writes=[self.ckv_dec])
            S.op("dve", lambda e, pr=pr, ti=ti: e.tensor_tensor(out=t64[:, :], in0=pr[:, 64:128], in1=ropet[:, ti * 128 + 64:ti * 128 + 128], op=ALU.mult),
                 reads=[pr, ropet], writes=[t64])
            S.op("dve", lambda e, pr=pr, ti=ti, ko=ko: e.tensor_tensor(out=ko[:, :], in0=pr[:, 0:64], in1=ropet[:, ti * 128:ti * 128 + 64], op=ALU.mult),
                 reads=[pr, ropet], writes=[ko])
            S.op("dve", lambda e, ko=ko: e.tensor_tensor(out=ko[:, :], in0=ko[:, :], in1=t64[:, :], op=ALU.add),
                 reads=[ko, t64], writes=[ko])
            self.dma("sp", o_kr[c0:c0 + 128, :], ko[:, :], reads=[ko])
        A.free(wkv, wkr, kvn, ropet, junk, ssq, t64, *ckv_o, *kr_o)

        PVW = PV_SCW
        hist = A.alloc("schist", 8 * 32, ckey="once")
        self.dma("sp", hist[:, :], sc_hist_d[:, :], writes=[hist])
        osc = A.alloc("osc", 8 * 34, ckey="stage")
        wsl = [A.alloc(f"wsc{i}", KC * 128, BF16) for i in range(3)]
        uext = A.alloc("uext", 2 + NP_ + 16 * 10)
        gb = A.alloc("gbt", NTOK)
        acc = A.alloc("acct", NTOK)
        vt = A.alloc("vt", 512)
        self.merged = [None] * 16
        self.sc_sav = A.alloc("sc_sav", 32)
        for cch in range(8):
            for m, off in enumerate((O3, O4, O5)):
                b = wsl[m]
                self.dma("pool", b[:, :].rearrange("p (k n) -> p k n", k=KC), wv[:, :, off + cch * 128:off + (cch + 1) * 128], writes=[b])
            S.op("pool", lambda e: e.memset(uext[:, 0:2], 0.0), writes=[uext])
            udec = uext[:, 2 + NP_:].rearrange("p (b t) -> p b t", b=16)
            S.op("pool", lambda e, cch=cch, udec=udec: e.tensor_copy(out=udec[:, :, 0:2], in_=hist[:, cch * 32:(cch + 1) * 32].rearrange("p (b t) -> p b t", b=16)),
                 reads=[hist], writes=[uext])
            for (s, n) in TB:
                pb = self.nextps()
                pc = self.nextps()
                pvv = self.nextps()
                for m, ps in enumerate((pb, pc, pvv)):
                    for k in range(KC):
                        self.mm(ps, ps[:, :n], wsl[m], wsl[m][:, k * 128:(k + 1) * 128], hT[k], hT[k][:, s:s + n], k == 0, k == KC - 1)
                S.op("act", lambda e, pb=pb, s=s, n=n: e.activation(out=gb[:, s:s + n], in_=pb[:, :n], func=AF.Copy), reads=[pb], writes=[gb])
                S.op("act", lambda e, pvv=pvv, n=n: e.activation(out=vt[:, :n], in_=pvv[:, :n], func=AF.Copy), reads=[pvv], writes=[vt])
                if s < NP_:
                    S.op("dve", lambda e, pc=pc, s=s, n=n: e.tensor_tensor(out=uext[:, 2 + s:2 + s + n], in0=pc[:, :n], in1=vt[:, :n], op=ALU.mult),
                         reads=[pc, vt], writes=[uext])
                else:
                    S.op("dve", lambda e, pc=pc, n=n, udec=udec: e.tensor_tensor(
                        out=udec[:, :, 2:10], in0=pc[:, :n].rearrange("p (b t) -> p b t", b=16),
                        in1=vt[:, :n].rearrange("p (b t) -> p b t", b=16), op=ALU.mult), reads=[pc, vt], writes=[uext])
            w0 = pv[:, PVW + cch * 3 + 0:PVW + cch * 3 + 1]
            w1 = pv[:, PVW + cch * 3 + 1:PVW + cch * 3 + 2]
            w2 = pv[:, PVW + cch * 3 + 2:PVW + cch * 3 + 3]
            S.op("dve", lambda e, w2=w2: e.tensor_scalar(out=acc[:, 0:NP_], in0=uext[:, 2:2 + NP_], scalar1=w2, scalar2=None, op0=ALU.mult),
                 reads=[uext, pv], writes=[acc])
            S.op("dve", lambda e, w1=w1: e.scalar_tensor_tensor(out=acc[:, 0:NP_], in0=uext[:, 1:1 + NP_], scalar=w1, in1=acc[:, 0:NP_], op0=ALU.mult, op1=ALU.add),
                 reads=[uext, pv, acc], writes=[acc])
            S.op("dve", lambda e, w0=w0: e.scalar_tensor_tensor(out=acc[:, 0:NP_], in0=uext[:, 0:NP_], scalar=w0, in1=acc[:, 0:NP_], op0=ALU.mult, op1=ALU.add),
                 reads=[uext, pv, acc], writes=[acc])
            accd = acc[:, NP_:].rearrange("p (b t) -> p b t", b=16)
            S.op("dve", lambda e, w2=w2, udec=udec, accd=accd: e.tensor_scalar(out=accd, in0=udec[:, :, 2:10], scalar1=w2, scalar2=None, op0=ALU.mult),
                 reads=[uext, pv], writes=[acc])
            S.op("dve", lambda e, w1=w1, udec=udec, accd=accd: e.scalar_tensor_tensor(out=accd, in0=udec[:, :, 1:9], scalar=w1, in1=accd, op0=ALU.mult, op1=ALU.add),
                 reads=[uext, pv, acc], writes=[acc])
            S.op("dve", lambda e, w0=w0, udec=udec, accd=accd: e.scalar_tensor_tensor(out=accd, in0=udec[:, :, 0:8], scalar=w0, in1=accd, op0=ALU.mult, op1=ALU.add),
                 reads=[uext, pv, acc], writes=[acc])
            S.op("pool", lambda e, cch=cch: e.tensor_copy(out=osc[:, cch * 34:cch * 34 + 2], in_=uext[:, NP_:NP_ + 2]), reads=[uext], writes=[osc])
            S.op("pool", lambda e, cch=cch, udec=udec: e.tensor_copy(out=osc[:, cch * 34 + 2:cch * 34 + 34].rearrange("p (b t) -> p b t", b=16), in_=udec[:, :, 8:10]),
                 reads=[uext], writes=[osc])
            S.op("pool", lambda e, cch=cch: e.tensor_copy(out=self.sc_sav[:, cch * 4:cch * 4 + 2], in_=acc[:, 0:2]), reads=[acc], writes=[self.sc_sav])
            S.op("pool", lambda e, cch=cch: e.tensor_copy(out=self.sc_sav[:, cch * 4 + 2:cch * 4 + 4], in_=gb[:, 0:2]), reads=[gb], writes=[self.sc_sav])
            mg = A.alloc(f"mg{8 + cch}", NTOK, BF16)
            self.merged[8 + cch] = mg
            S.op("dve", lambda e, mg=mg: e.tensor_tensor(out=mg[:, :], in0=acc[:, :], in1=gb[:, :], op=ALU.mult), reads=[acc, gb], writes=[mg])
        self.dma("sp", o_sc[:, :], osc[:, :], reads=[osc])
        A.free(hist, uext, gb, acc, vt, *wsl)
        self.osc = osc


    def spill_x(self, xT):
        if not hasattr(self, "xsp_t"):
            self.xsp_t = self.nc.dram_tensor("x_sp", [128, KC * NTOK], F32)
            self.xsp = Buf("xsp", None)
        v = self.xsp_t.ap().rearrange("p (k t) -> p k t", k=KC)
        for k in range(KC):
            self.dma("sp", v[:, k, :], xT[k][:, :], reads=[xT[k]], writes=[self.xsp], chan_buf=xT[k], group=True)
        self.A.free(*xT)

    def reload_x(self):
        v = self.xsp_t.ap().rearrange("p (k t) -> p k t", k=KC)
        xT = [self.A.alloc(f"xT{k}", NTOK, ckey="xT") for k in range(KC)]
        for k in range(KC):
            self.dma("sp", xT[k][:, :], v[:, k, :], reads=[self.xsp], writes=[xT[k]], chan_buf=xT[k], group=(k > 0))
        return xT

    def copy_evac(self, i, out_ap, in_ap, reads, writes):
        if i % 2 == 0:
            self.S.op("act", lambda e: e.activation(out=out_ap, in_=in_ap, func=AF.Copy), reads=reads, writes=writes)
        else:
            self.S.op("dve", lambda e: e.tensor_copy(out=out_ap, in_=in_ap), reads=reads, writes=writes)

    def l0_latents_fm(self, hT, pv, w_in, rope_fm_d):
        S, A = self.S, self.A
        wv = w_in.rearrange("(k p) n -> p k n", p=128)
        wkv = A.alloc("wkv2", KC * KVL, BF16)
        wkr = A.alloc("wkr2", KC * 128, BF16)
        self.dma("pool", wkv[:, :].rearrange("p (k n) -> p k n", k=KC), wv[:, :, O1:O2], writes=[wkv])
        wkr3 = wkr[:, :].rearrange("p (k n) -> p k n", k=KC)
        self.dma("pool", wkr3[:, :, 0:64], wv[:, :, O2:O3], writes=[wkr])
        self.dma("pool", wkr3[:, :, 64:96], wv[:, :, O2 + 32:O3], writes=[wkr], group=True)
        self.dma("pool", wkr3[:, :, 96:128], wv[:, :, O2:O2 + 32], writes=[wkr], group=True)
        self.rope_fm = A.alloc("rope_fm", 2 * NTOK, ckey="once")
        self.dma("sp", self.rope_fm[:, :], rope_fm_d[:, :], writes=[self.rope_fm])
        raw = [A.alloc(f"ckvraw{cc}", NTOK) for cc in range(4)]
        ei = 0
        for cc in range(4):
            for (s, n) in TB:
                ps = self.nextps()
                for k in range(KC):
                    c0 = k * KVL + cc * 128
                    self.mm(ps, ps[:, :n], wkv, wkv[:, c0:c0 + 128], hT[k], hT[k][:, s:s + n], k == 0, k == KC - 1)
                self.copy_evac(ei, raw[cc][:, s:s + n], ps[:, :n], [ps], [raw[cc]])
                ei += 1
        self.ckvT = [A.alloc(f"ckvT{cc}", NTOK, BF16, ckey="stage") for cc in range(4)]
        self.rmsnorm_fm(raw, self.ckvT, pv, PV_KVN, KVL, TB)
        A.free(*raw)
        self.kropeT = A.alloc("kropeT", NTOK, BF16, ckey="stage")
        t1 = A.alloc("kr_t1", 512)
        t2 = A.alloc("kr_t2", 512)
        rf = self.rope_fm
        for (s, n) in TB:
            p1 = self.nextps()
            p2 = self.nextps()
            for k in range(KC):
                self.mm(p1, p1[:64, :n], wkr, wkr[:, k * 128:k * 128 + 64], hT[k], hT[k][:, s:s + n], k == 0, k == KC - 1)
            for k in range(KC):
                self.mm(p2, p2[:64, :n], wkr, wkr[:, k * 128 + 64:k * 128 + 128], hT[k], hT[k][:, s:s + n], k == 0, k == KC - 1)
            S.op("dve", lambda e, p2=p2, s=s, n=n: e.tensor_tensor(out=t1[:64, :n], in0=p2[:64, :n], in1=rf[:64, NTOK + s:NTOK + s + n], op=ALU.mult),
                 reads=[p2, rf], writes=[t1])
            S.op("dve", lambda e, p1=p1, s=s, n=n: e.tensor_tensor(out=t2[:64, :n], in0=p1[:64, :n], in1=rf[:64, s:s + n], op=ALU.mult),
                 reads=[p1, rf], writes=[t2])
            S.op("dve", lambda e, s=s, n=n: e.tensor_tensor(out=self.kropeT[:64, s:s + n], in0=t1[:64, :n], in1=t2[:64, :n], op=ALU.add),
                 reads=[t1, t2], writes=[self.kropeT])
        A.free(t1, t2, wkv, wkr)
        wq = [A.alloc(f"wcq{i}", KC * 256, BF16) for i in range(2)]
        raw = [A.alloc(f"cqraw{i}", NTOK) for i in range(6)]
        for j in range(3):
            b = wq[j % 2]
            self.dma("pool", b[:, :].rearrange("p (k n) -> p k n", k=KC), wv[:, :, j * 256:(j + 1) * 256], writes=[b])
            for c2 in range(2):
                ch = 2 * j + c2
                for (s, n) in TB:
                    ps = self.nextps()
                    for k in range(KC):
                        c0 = k * 256 + c2 * 128
                        self.mm(ps, ps[:, :n], b, b[:, c0:c0 + 128], hT[k], hT[k][:, s:s + n], k == 0, k == KC - 1)
                    self.copy_evac(ei, raw[ch][:, s:s + n], ps[:, :n], [ps], [raw[ch]])
                    ei += 1
        self.cqn = [A.alloc(f"cqn{i}", NTOK, BF16) for i in range(6)]
        self.rmsnorm_fm(raw, self.cqn, pv, PV_QNORM, QL, TB)
        A.free(*raw)
        A.free(*wq)

    def allgather(self, name, width, fill):
        nc = self.nc
        tin = nc.dram_tensor(name + "_in", [128, width], F32)
        tout = nc.dram_tensor(name + "_out", [4 * 128, width], F32)
        exi = Buf(name + "_in", None)
        exo = Buf(name + "_out", None, ckey="cc")
        fill(tin.ap(), exi)
        self.S.op("pool", lambda e: e.collective_compute("AllGather", ALU.bypass, replica_groups=[[0, 1, 2, 3], [4, 5, 6, 7]],
                                                         ins=[tin.ap().opt()], outs=[tout.ap().opt()]),
                  reads=[exi], writes=[exo], dma=True, chan_buf=exo, inc=1)
        return tout.ap(), exo

    def l0_exchange(self):
        S, A, nc = self.S, self.A, self.nc

        def fill_kv(c0):
            def f(ein, exi):
                for i in range(2):
                    cc = c0 + i
                    dst = ein[:, i * 512:(i + 1) * 512].bitcast(BF16)
                    self.dma("sp", dst, self.ckvT[cc][:, 0:NP_], reads=[self.ckvT[cc]], writes=[exi], chan_buf=self.ckvT[cc], group=True)
            return f

        def fill_c(ein, exi):
            dst = ein[0:64, 0:512].bitcast(BF16)
            self.dma("sp", dst, self.kropeT[0:64, 0:NP_], reads=[self.kropeT], writes=[exi], chan_buf=self.kropeT, group=True)
            osc3 = self.osc[:, :].rearrange("p (c t) -> p c t", c=8)
            self.dma_nc("sp", ein[:, 512:528].rearrange("p (c t) -> p c t", c=8), osc3[:, :, 0:2], reads=[self.osc], writes=[exi], chan_buf=self.osc, group=True)

        eoA, exoA = self.allgather("ex0a", 1024, fill_kv(0))
        eoB, exoB = self.allgather("ex0b", 1024, fill_kv(2))
        eoC, exoC = self.allgather("ex0c", 528, fill_c)
        self.ckvT_all = [[A.alloc(f"ckvA{r}_{cc}", NP_, BF16, ckey="exl") for cc in range(4)] for r in range(4)]
        self.kropeT_all = [A.alloc(f"krA{r}", NP_, BF16, ckey="exl") for r in range(4)]
        self.uh_all = A.alloc("uh_all", 64, ckey="exl")
        for r in range(4):
            for cc in range(4):
                b = self.ckvT_all[r][cc]
                eo, exo = (eoA, exoA) if cc < 2 else (eoB, exoB)
                i = cc % 2
                self.dma("sp", b[:, :], eo[r * 128:(r + 1) * 128, i * 512:(i + 1) * 512].bitcast(BF16), reads=[exo], writes=[b], chan_buf=b)
            b = self.kropeT_all[r]
            self.dma("sp", b[0:64, :], eoC[r * 128:r * 128 + 64, 0:512].bitcast(BF16), reads=[exoC], writes=[b], chan_buf=b)
            self.dma("sp", self.uh_all[:, r * 16:(r + 1) * 16], eoC[r * 128:(r + 1) * 128, 512:528], reads=[exoC], writes=[self.uh_all],
                     chan_buf=self.uh_all, group=(r > 0))

    def l0_prefill_attn(self, cv, w_uq_d, w_uk_d, w_uv_d, cmask_d):
        S, A = self.S, self.A
        self.rot = [0, 1, 2, 3]
        PO = [self.PS[4], self.PS[6]]
        PL = [self.PS[5], self.PS[7]]
        cmask = A.alloc("cmask", 4 * 512, BF16, ckey="oncep")
        self.dma("pool", cmask[:, :], cmask_d[:, :], writes=[cmask])
        wq = [A.alloc(f"wqh{i}", 6 * 256, BF16) for i in range(2)]
        wuk = [A.alloc(f"wuk{i}", 4 * 128, BF16) for i in range(2)]
        wuv = [A.alloc(f"wuv{i}", 4 * 128, BF16) for i in range(2)]
        qn = A.alloc("qn_h", NTOK, BF16)
        qr = A.alloc("qr_h", NTOK, BF16)
        tq1 = A.alloc("tq1", 512)
        tq2 = A.alloc("tq2", 512)
        KT = A.alloc("KT_h", 4096, BF16)
        V = A.alloc("V_h", 4096, BF16)
        pT = [A.alloc(f"pT{i}", 512, BF16) for i in range(3)]
        Mt = [A.alloc(f"Mt{i}", 512, BF16) for i in range(2)]
        rl = A.alloc("rl", 512)
        self.qn_dec = A.alloc("qn_dec", 8 * 128, BF16)
        self.qr_dec = A.alloc("qr_dec", 8 * 128, BF16)
        self.merged[0:8] = [A.alloc(f"mg{h}", NTOK, BF16) for h in range(8)]
        wqv = w_uq_d.rearrange("(k p) n -> p k n", p=128)
        wukv = w_uk_d.rearrange("(c p) n -> p c n", p=128)
        wuvv = w_uv_d.rearrange("(c p) n -> p c n", p=128)
        rf = self.rope_fm
        pti = 0
        mti = 0
        ei = 0
        for h in range(8):
            w = wq[h % 2]
            w3 = w[:, :].rearrange("p (k n) -> p k n", k=6)
            b0 = h * 192
            self.dma("pool", w3[:, :, 0:192], wqv[:, :, b0:b0 + 192], writes=[w])
            self.dma("pool", w3[:, :, 192:224], wqv[:, :, b0 + 160:b0 + 192], writes=[w], group=True)
            self.dma("pool", w3[:, :, 224:256], wqv[:, :, b0 + 128:b0 + 160], writes=[w], group=True)
            uk = wuk[h % 2]
            uv = wuv[h % 2]
            self.dma("pool", uk[:, :].rearrange("p (c n) -> p c n", c=4), wukv[:, :, h * 128:(h + 1) * 128], writes=[uk])
            self.dma("pool", uv[:, :].rearrange("p (c n) -> p c n", c=4), wuvv[:, :, h * 128:(h + 1) * 128], writes=[uv])
            for (s, n) in TB:
                pn = self.nextps()
                p1 = self.nextps()
                p2 = self.nextps()
                for k in range(6):
                    self.mm(pn, pn[:, :n], w, w[:, k * 256:k * 256 + 128], self.cqn[k], self.cqn[k][:, s:s + n], k == 0, k == 5)
                for k in range(6):
                    self.mm(p1, p1[:64, :n], w, w[:, k * 256 + 128:k * 256 + 192], self.cqn[k], self.cqn[k][:, s:s + n], k == 0, k == 5)
                for k in range(6):
                    self.mm(p2, p2[:64, :n], w, w[:, k * 256 + 192:k * 256 + 256], self.cqn[k], self.cqn[k][:, s:s + n], k == 0, k == 5)
                S.op("act", lambda e, pn=pn, s=s, n=n: e.activation(out=qn[:, s:s + n], in_=pn[:, :n], func=AF.Copy), reads=[pn], writes=[qn])
                S.op("dve", lambda e, p2=p2, s=s, n=n: e.tensor_tensor(out=tq1[:64, :n], in0=p2[:64, :n], in1=rf[:64, NTOK + s:NTOK + s + n], op=ALU.mult),
                     reads=[p2, rf], writes=[tq1])
                S.op("dve", lambda e, p1=p1, s=s, n=n: e.tensor_tensor(out=tq2[:64, :n], in0=p1[:64, :n], in1=rf[:64, s:s + n], op=ALU.mult),
                     reads=[p1, rf], writes=[tq2])
                S.op("dve", lambda e, s=s, n=n: e.tensor_tensor(out=qr[:64, s:s + n], in0=tq1[:64, :n], in1=tq2[:64, :n], op=ALU.add),
                     reads=[tq1, tq2], writes=[qr])
            S.op("pool", lambda e, h=h: e.tensor_copy(out=self.qn_dec[:, h * 128:(h + 1) * 128], in_=qn[:, NP_:NTOK]), reads=[qn], writes=[self.qn_dec])
            S.op("pool", lambda e, h=h: e.tensor_copy(out=self.qr_dec[:64, h * 128:(h + 1) * 128], in_=qr[:64, NP_:NTOK]), reads=[qr], writes=[self.qr_dec])
            for r in range(4):
                for half in range(2):
                    ps = self.nextps()
                    for cc in range(4):
                        a = self.ckvT_all[r][cc]
                        self.mm(ps, ps[:, :512], uk, uk[:, cc * 128:(cc + 1) * 128], a, a[:, half * 512:(half + 1) * 512], cc == 0, cc == 3)
                    o0 = r * 1024 + half * 512
                    self.copy_evac(ei, KT[:, o0:o0 + 512], ps[:, :512], [ps], [KT])
                    ei += 1
                for q4 in range(2):
                    ps = self.nextps()
                    for sl in range(4):
                        st = q4 * 4 + sl
                        for cc in range(4):
                            a = self.ckvT_all[r][cc]
                            self.mm(ps, ps[:, sl * 128:(sl + 1) * 128], a, a[:, st * 128:(st + 1) * 128], uv, uv[:, cc * 128:(cc + 1) * 128], cc == 0, cc == 3)
                    o0 = (r * 8 + q4 * 4) * 128
                    self.copy_evac(ei, V[:, o0:o0 + 512], ps[:, :512], [ps], [V])
                    ei += 1
            for qb in range(2):
                q0 = qb * 512
                po, pl = PO[qb], PL[qb]
                first = True
                for kb in range(4):
                    for st in range(8):
                        rel = st - 4 * qb
                        ps = self.nextps()
                        kc0 = (kb * 8 + st) * 128
                        self.mm(ps, ps[:, :512], KT, KT[:, kc0:kc0 + 128], qn, qn[:, q0:q0 + 512], True, False)
                        ka = self.kropeT_all[kb]
                        self.mm(ps, ps[:, :512], ka, ka[:64, st * 128:(st + 1) * 128], qr, qr[:64, q0:q0 + 512], False, True)
                        bcol = (4 + kb) if rel > 3 else kb
                        p = pT[pti % 3]
                        pti += 1
                        S.op("act", lambda e, p=p, ps=ps, bcol=bcol: e.activation(out=p[:, :], in_=ps[:, :512], func=AF.Exp, bias=cv[:, bcol:bcol + 1], scale=SCALE),
                             reads=[ps, cv], writes=[p])
                        if 0 <= rel <= 3:
                            m = Mt[mti % 2]
                            mti += 1
                            S.op("pool", lambda e, m=m, rel=rel, kb=kb: e.tensor_scalar(out=m[:, :], in0=cmask[:, rel * 512:(rel + 1) * 512], scalar1=cv[:, 8 + kb:9 + kb],
                                                                                      scalar2=cv[:, 12 + kb:13 + kb], op0=ALU.mult, op1=ALU.add), reads=[cmask, cv], writes=[m])
                            S.op("dve", lambda e, p=p, m=m: e.tensor_tensor(out=p[:, :], in0=p[:, :], in1=m[:, :], op=ALU.mult), reads=[p, m], writes=[p])
                        last = (kb == 3 and st == 7)
                        self.mm(po, po[:, :512], V, V[:, kc0:kc0 + 128], p, p[:, :], first, last)
                        self.mm(pl, pl[:, :512], self.ones, self.ones[:, :], p, p[:, :], first, last)
                        first = False
                S.op("dve", lambda e, pl=pl: e.reciprocal(out=rl[:, :], in_=pl[:, :512]), reads=[pl], writes=[rl])
                mg = self.merged[h]
                S.op("dve", lambda e, po=po, mg=mg, q0=q0: e.tensor_tensor(out=mg[:, q0:q0 + 512], in0=po[:, :512], in1=rl[:, :], op=ALU.mult),
                     reads=[po, rl], writes=[mg])
        self.rot = None
        A.free(cmask, qn, qr, tq1, tq2, KT, V, rl, *wq, *wuk, *wuv, *pT, *Mt)
        for r in range(4):
            A.free(*self.ckvT_all[r])
        A.free(*self.kropeT_all)

    def l0_sconv_patch(self, pv, cv):
        S, A = self.S, self.A
        uh = A.alloc("uh", 16)
        S.op("dve", lambda e: e.tensor_scalar(out=uh[:, :], in0=self.uh_all[:, 0:16], scalar1=cv[:, 17:18], scalar2=None, op0=ALU.mult),
             reads=[self.uh_all, cv], writes=[uh])
        for r in range(1, 4):
            S.op("dve", lambda e, r=r: e.scalar_tensor_tensor(out=uh[:, :], in0=self.uh_all[:, r * 16:(r + 1) * 16], scalar=cv[:, 17 + r:18 + r], in1=uh[:, :],
                                                              op0=ALU.mult, op1=ALU.add), reads=[self.uh_all, cv, uh], writes=[uh])
        uh3 = uh[:, :].rearrange("p (c t) -> p c t", c=8)
        w3 = pv[:, PV_SCW:PV_SCW + 24].rearrange("p (c k) -> p c k", k=3)
        sv = self.sc_sav[:, :].rearrange("p (c t) -> p c t", c=8)
        fx = A.alloc("scfix", 32)
        f3 = fx[:, :].rearrange("p (c t) -> p c t", c=8)
        S.op("dve", lambda e: e.tensor_tensor(out=f3[:, :, 0], in0=w3[:, :, 0], in1=uh3[:, :, 0], op=ALU.mult), reads=[pv, uh], writes=[fx])
        S.op("dve", lambda e: e.tensor_tensor(out=f3[:, :, 1], in0=w3[:, :, 1], in1=uh3[:, :, 1], op=ALU.mult), reads=[pv, uh, fx], writes=[fx])
        S.op("dve", lambda e: e.tensor_tensor(out=f3[:, :, 0], in0=f3[:, :, 0], in1=f3[:, :, 1], op=ALU.add), reads=[fx], writes=[fx])
        S.op("dve", lambda e: e.tensor_tensor(out=f3[:, :, 1], in0=w3[:, :, 0], in1=uh3[:, :, 1], op=ALU.mult), reads=[pv, uh, fx], writes=[fx])
        S.op("dve", lambda e: e.tensor_tensor(out=f3[:, :, 0:2], in0=f3[:, :, 0:2], in1=sv[:, :, 0:2], op=ALU.add), reads=[fx, self.sc_sav], writes=[fx])
        S.op("dve", lambda e: e.tensor_tensor(out=f3[:, :, 2:4], in0=f3[:, :, 0:2], in1=sv[:, :, 2:4], op=ALU.mult), reads=[fx, self.sc_sav], writes=[fx])
        for cch in range(8):
            mg = self.merged[8 + cch]
            S.op("dve", lambda e, mg=mg, cch=cch: e.tensor_copy(out=mg[:, 0:2], in_=f3[:, cch, 2:4]), reads=[fx], writes=[mg])
        A.free(uh, fx, self.uh_all, self.sc_sav)

    def out_proj(self, w_out_d, nmc, xT):
        S, A = self.S, self.A
        wv = w_out_d.rearrange("(m p) n -> p m n", p=128)
        wo = [A.alloc(f"wo{i}", nmc * 256, BF16) for i in range(2)]
        for dcp in range(8):
            b = wo[dcp % 2]
            self.dma("pool", b[:, :].rearrange("p (m n) -> p m n", m=nmc), wv[:, :, dcp * 256:(dcp + 1) * 256], writes=[b])
            for c2 in range(2):
                dc = 2 * dcp + c2
                for (s, n) in TB:
                    ps = self.nextps()
                    for mc in range(nmc):
                        c0 = mc * 256 + c2 * 128
                        self.mm(ps, ps[:, :n], b, b[:, c0:c0 + 128], self.merged[mc], self.merged[mc][:, s:s + n], mc == 0, mc == nmc - 1)
                    S.op("dve", lambda e, ps=ps, dc=dc, s=s, n=n: e.tensor_tensor(out=xT[dc][:, s:s + n], in0=ps[:, :n], in1=xT[dc][:, s:s + n], op=ALU.add),
                         reads=[ps, xT[dc]], writes=[xT[dc]])
        A.free(*wo)

    def l0_decode_attn(self, cv, cache_ckv_d, cache_kr_d, ptx_d, w_ukT_d, w_uv_d, dmask_d, ident_d):
        S, A = self.S, self.A
        DS = 9
        identb = A.alloc("identb", 128, BF16, ckey="oncep")
        self.dma("pool", identb[:, :], ident_d[:, :], writes=[identb])
        onesf = A.alloc("onesf", 1)
        S.op("pool", lambda e: e.memset(onesf[:, :], 1.0), writes=[onesf])
        dmask = A.alloc("dmask", 16 * 64, ckey="once")
        self.dma("sp", dmask[:, :], dmask_d[:, :], writes=[dmask])
        wukT = A.alloc("wukT", 8 * 512, BF16)
        self.dma("pool", wukT[:, :].rearrange("p (h c) -> p h c", h=8), w_ukT_d.rearrange("(h n) c -> n h c", h=8), writes=[wukT])
        wuv = A.alloc("wuvd", 4 * 1024, BF16)
        self.dma("pool", wuv[:, :].rearrange("p (c n) -> p c n", c=4), w_uv_d.rearrange("(c p) n -> p c n", p=128), writes=[wuv])
        ptx = A.alloc("ptx", 128, I32, ckey="once")
        self.dma("sp", ptx[:, :], ptx_d[:, :], writes=[ptx])
        ptf = A.alloc("ptf", 128)
        idx = A.alloc("idx", 128, I32)
        S.op("dve", lambda e: e.tensor_copy(out=ptf[:, :], in_=ptx[:, :]), reads=[ptx], writes=[ptf])
        S.op("dve", lambda e: e.tensor_scalar(out=ptf[:, :], in0=ptf[:, :], scalar1=16.0, scalar2=cv[:, 16:17], op0=ALU.mult, op1=ALU.add),
             reads=[ptf, cv], writes=[ptf])
        S.op("dve", lambda e: e.tensor_copy(out=idx[:, :], in_=ptf[:, :]), reads=[ptf], writes=[idx])
        qlat = [A.alloc(f"qlat{cc}", 1024, BF16) for cc in range(4)]
        qrd = A.alloc("qrd", 1024, BF16)
        ei = 0
        for h in range(8):
            for cc in range(4):
                ps = self.nextps()
                self.mm(ps, ps[:, :128], wukT, wukT[:, h * 512 + cc * 128:h * 512 + (cc + 1) * 128], self.qn_dec, self.qn_dec[:, h * 128:(h + 1) * 128], True, True)
                dst = qlat[cc][:, :].rearrange("p (b h t) -> p b h t", b=16, h=8)[:, :, h, :]
                self.copy_evac(ei, dst, ps[:, :128].rearrange("p (b t) -> p b t", b=16), [ps], [qlat[cc]])
                ei += 1
            dst = qrd[:64, :].rearrange("p (b h t) -> p b h t", b=16, h=8)[:, :, h, :]
            S.op("pool", lambda e, dst=dst, h=h: e.tensor_copy(out=dst, in_=self.qr_dec[:64, h * 128:(h + 1) * 128].rearrange("p (b t) -> p b t", b=16)),
                 reads=[self.qr_dec], writes=[qrd])
        kv = [A.alloc(f"kvg{i}", 8 * 512, BF16) for i in range(2)]
        kr = [A.alloc(f"krg{i}", 8 * 64, BF16) for i in range(2)]
        ckT = [A.alloc(f"ckT{cc}", 1024, BF16) for cc in range(4)]
        krT = A.alloc("krT", 1024, BF16)
        pT = [A.alloc(f"pTd{i}", 512, BF16) for i in range(2)]
        lt = A.alloc("lt", 64)
        lacc = A.alloc("lacc", 64)
        pN = A.alloc("pN", 64)
        pNb = A.alloc("pNb", 64, BF16)
        rl = A.alloc("rld", 1)
        rlb = A.alloc("rlb", 64)
        on = A.alloc("on", 512, BF16)
        S.op("pool", lambda e: e.memset(on[:, :], 0.0), writes=[on])
        olat = [A.alloc(f"olat{cc}", 1024, BF16) for cc in range(4)]
        cview = cache_ckv_d.rearrange("r (a e) -> r a e", a=2)
        PT = self.PS[0:4]
        PTR = self.PS[4]
        PSS = self.PS[5]
        PSO = self.PS[6]
        PM = self.PS[7]
        gi = 0
        for b in range(16 if DS >= 2 else 0):
            for g in range(8):
                kvb = kv[gi % 2]
                krb = kr[gi % 2]
                col = b * 8 + g
                S.op("pool", lambda e, kvb=kvb, col=col: e.indirect_dma_start(
                    out=kvb[:, :], out_offset=None, in_=cache_ckv_d,
                    in_offset=bass.IndirectOffsetOnAxis(ap=idx[:, col:col + 1], axis=0)), reads=[idx], writes=[kvb], dma=True, chan_buf=kvb)
                S.op("pool", lambda e, krb=krb, col=col: e.indirect_dma_start(
                    out=krb[:, :], out_offset=None, in_=cache_kr_d,
                    in_offset=bass.IndirectOffsetOnAxis(ap=idx[:, col:col + 1], axis=0)), reads=[idx], writes=[krb], dma=True, chan_buf=krb)
                for cc in range(4):
                    pb = PT[cc]
                    pbv = pb[:, :].bitcast(BF16)
                    for jj in range(8):
                        S.op("pe", lambda e, pbv=pbv, kvb=kvb, jj=jj, cc=cc: e.transpose(pbv[:, jj * 128:(jj + 1) * 128], kvb[:, jj * 512 + cc * 128:jj * 512 + (cc + 1) * 128], identb[:, :]),
                             reads=[kvb, identb], writes=[pb])
                    self.copy_evac(cc, ckT[cc][:, :], pbv[:, 0:1024], [pb], [ckT[cc]])
                pbv = PTR[:, :].bitcast(BF16)
                for jj in range(8):
                    S.op("pe", lambda e, pbv=pbv, krb=krb, jj=jj: e.transpose(pbv[:64, jj * 128:(jj + 1) * 128], krb[:, jj * 64:(jj + 1) * 64], identb[:, :]),
                         reads=[krb, identb], writes=[PTR])
                self.copy_evac(1, krT[:64, :], pbv[:64, 0:1024], [PTR], [krT])
                if DS < 3:
                    gi += 1
                    continue
                for jj in range(8):
                    for cc in range(4):
                        self.mm(PSS, PSS[:, jj * 64:(jj + 1) * 64], ckT[cc], ckT[cc][:, jj * 128:(jj + 1) * 128], qlat[cc], qlat[cc][:, b * 64:(b + 1) * 64], cc == 0, False)
                    self.mm(PSS, PSS[:, jj * 64:(jj + 1) * 64], krT, krT[:64, jj * 128:(jj + 1) * 128], qrd, qrd[:64, b * 64:(b + 1) * 64], False, True)
                p = pT[gi % 2]
                S.op("act", lambda e, p=p: e.activation(out=p[:, :], in_=PSS[:, :512], func=AF.Exp, scale=SCALE), reads=[PSS], writes=[p])
                for jj in range(8):
                    first = (g == 0 and jj == 0)
                    for cc in range(4):
                        self.mm(PSO, PSO[:, cc * 64:(cc + 1) * 64], kvb, kvb[:, jj * 512 + cc * 128:jj * 512 + (cc + 1) * 128], p, p[:, jj * 64:(jj + 1) * 64], first, False)
                    self.mm(PSO, PSO[:, 256:320], self.ones, self.ones[:, :], p, p[:, jj * 64:(jj + 1) * 64], first, False)
                gi += 1
            if DS < 4:
                continue
            for cc in range(4):
                self.mm(PM, PM[:, 0:64], self.ckvT[cc], self.ckvT[cc][:, NP_:NTOK], qlat[cc], qlat[cc][:, b * 64:(b + 1) * 64], cc == 0, False)
            self.mm(PM, PM[:, 0:64], self.kropeT, self.kropeT[:64, NP_:NTOK], qrd, qrd[:64, b * 64:(b + 1) * 64], False, True)
            S.op("act", lambda e: e.activation(out=pN[:, :], in_=PM[:, 0:64], func=AF.Exp, scale=SCALE), reads=[PM], writes=[pN])
            S.op("dve", lambda e, b=b: e.tensor_tensor(out=pN[:, :], in0=pN[:, :], in1=dmask[:, b * 64:(b + 1) * 64], op=ALU.mult), reads=[pN, dmask], writes=[pN])
            S.op("dve", lambda e: e.tensor_copy(out=pNb[:, :], in_=pN[:, :]), reads=[pN], writes=[pNb])
            for cc in range(4):
                self.mm(PSO, PSO[:, cc * 64:(cc + 1) * 64], self.ckv_dec, self.ckv_dec[:, cc * 128:(cc + 1) * 128], pNb, pNb[:, :], False, True)
            self.mm(PSO, PSO[:, 256:320], self.ones, self.ones[:, :], pNb, pNb[:, :], False, True)
            S.op("dve", lambda e: e.reciprocal(out=rlb[:, :], in_=PSO[:, 256:320]), reads=[PSO], writes=[rlb])
            for cc in range(4):
                dst = olat[cc][:, :].rearrange("p (h b t) -> p h b t", h=8, b=16)[:, :, b, :]
                S.op("dve", lambda e, cc=cc, dst=dst: e.tensor_tensor(out=dst, in0=PSO[:, cc * 64:(cc + 1) * 64].rearrange("p (h t) -> p h t", h=8),
                                                                      in1=rlb[:, :].rearrange("p (h t) -> p h t", h=8), op=ALU.mult),
                     reads=[PSO, rlb], writes=[olat[cc]])
        self.rot = [0, 1, 2, 3]
        for h in range(8 if DS >= 5 else 0):
            ps = self.nextps()
            for cc in range(4):
                self.mm(ps, ps[:, :128], wuv, wuv[:, cc * 1024 + h * 128:cc * 1024 + (h + 1) * 128], olat[cc], olat[cc][:, h * 128:(h + 1) * 128], cc == 0, cc == 3)
            mg = self.merged[h]
            self.copy_evac(h, mg[:, NP_:NTOK], ps[:, :128], [ps], [mg])
        self.rot = None
        A.free(identb, onesf, dmask, wukT, wuv, ptx, ptf, idx, qrd, krT, lt, lacc, pN, pNb, rl, rlb, on, *qlat, *kv, *kr, *ckT, *pT, *olat)
        A.free(self.qn_dec, self.qr_dec, self.ckv_dec, self.kropeT, *self.ckvT, *self.cqn, self.rope_fm)


    def ssd_mixer(self, hT, xT, o_ssc, o_pssm, o_sssm):
        S, A, nc = self.S, self.A, self.nc
        cv = self.cv
        w_in = self.inp("ssm_w_in", [D, 10304])
        w_out = self.inp("ssm_w_out", [4096, D])
        self.ssm_wv = w_in.rearrange("(k p) n -> p k n", p=128)
        rows_d = self.inp("ssm_rows", [128, 128 + 8192])
        self.ssm_rows_d = rows_d
        kc_d = self.inp("ssm_consts", [128, SC_N])
        self.ssm_hist_d = self.inp("ssm_hist", [128, 48 * 48])
        st_d = self.inp("state_ssm", [16 * 4096, 128])
        ident_d = self.inp("ident", [128, 128])
        self.o_ssc = o_ssc
        KCB = A.alloc("ssm_kc", SC_N, ckey="once")
        self.dma("sp", KCB[:, :], kc_d[:, :], writes=[KCB])
        self.KCB = KCB
        cwb = A.alloc("ssm_cwb", 192 + 48, ckey="once")
        self.dma("sp", cwb[:, 0:192], self.inp("ssm_cw", [128, 192])[:, :], writes=[cwb])
        self.dma("sp", cwb[:, 192:240], self.inp("ssm_cb", [128, 48])[:, :], writes=[cwb])
        self.cwb = cwb
        identb = A.alloc("identb1", 128, BF16, ckey="oncep")
        self.dma("pool", identb[:, :], ident_d[:, :], writes=[identb])
        self.identb = identb
        onecol = A.alloc("onecol", 1)
        S.op("pool", lambda e: e.memset(onecol[:, :], 1.0), writes=[onecol])
        rows01 = A.alloc("rows01", 128, ckey="once")
        self.dma("sp", rows01[:, :], rows_d[:, 0:128], writes=[rows01])
        arow = A.alloc("arow", 64)
        S.op("act", lambda e: e.activation(out=arow[:, :], in_=rows01[:, 64:128], func=AF.Exp), reads=[rows01], writes=[arow])
        S.op("dve", lambda e: e.tensor_scalar(out=arow[:, :], in0=arow[:, :], scalar1=-1.0, scalar2=None, op0=ALU.mult), reads=[arow], writes=[arow])

        hl = A.alloc("hl", 48, ckey="stage")
        for k in range(KC):
            S.op("dve", lambda e, k=k: e.tensor_copy(out=hl[:, k * 3:(k + 1) * 3], in_=hT[k][:, NP_ - 3:NP_]), reads=[hT[k]], writes=[hl])
        def fill_h(ein, exi):
            self.dma("sp", ein[:, :], hl[:, :], reads=[hl], writes=[exi], chan_buf=hl)
        eoh, exo = self.allgather("exh", 48, fill_h)
        hall = A.alloc("hall", 4 * 48, ckey="exl")
        for r in range(4):
            self.dma("sp", hall[:, r * 48:(r + 1) * 48], eoh[r * 128:(r + 1) * 128, :], reads=[exo], writes=[hall],
                     chan_buf=hall, group=(r > 0))
        hh = A.alloc("hh", 48)
        S.op("dve", lambda e: e.tensor_scalar(out=hh[:, :], in0=hall[:, 0:48], scalar1=cv[:, 17:18], scalar2=None, op0=ALU.mult),
             reads=[hall, cv], writes=[hh])
        for r in range(1, 4):
            S.op("dve", lambda e, r=r: e.scalar_tensor_tensor(out=hh[:, :], in0=hall[:, r * 48:(r + 1) * 48], scalar=cv[:, 17 + r:18 + r], in1=hh[:, :],
                                                              op0=ALU.mult, op1=ALU.add), reads=[hall, cv, hh], writes=[hh])
        hTh = A.alloc("hTh", 48, BF16)
        S.op("dve", lambda e: e.tensor_copy(out=hTh[:, :], in_=hh[:, :]), reads=[hh], writes=[hTh])
        self.hTh = hTh
        A.free(hl, hall, hh)

        U_p, U_d, T_d, ONESF = KCB[:, 0:128], KCB[:, 128:256], KCB[:, 256:384], KCB[:, 384:512]
        Wdt = A.alloc("wdt", KC * 64, BF16)
        self.dma("pool", Wdt[:, :].rearrange("p (k n) -> p k n", k=KC), self.ssm_wv[:, :, 10240:10304], writes=[Wdt])
        PA = {n: A.alloc("pa_" + n, 9 * 64) for n in ("dt", "da", "nacum", "dtd", "cdec")}
        self.PA = PA
        self.ea_d = A.alloc("ea_d", 64)
        self.totd = A.alloc("totd", 64)
        totc = A.alloc("totc", 64)
        t64 = A.alloc("t64s", 64)
        for ti in range(9):
            c0 = ti * 128
            sl = slice(ti * 64, (ti + 1) * 64)
            ps = self.nextps()
            for k in range(KC):
                self.mm(ps, ps[:, :64], hT[k], hT[k][:, c0:c0 + 128], Wdt, Wdt[:, k * 64:(k + 1) * 64], k == 0, k == KC - 1)
            S.op("dve", lambda e, ps=ps: e.tensor_tensor(out=t64[:, :], in0=ps[:, :64], in1=rows01[:, 0:64], op=ALU.add), reads=[ps, rows01], writes=[t64])
            S.op("act", lambda e: e.activation(out=t64[:, :], in_=t64[:, :], func=AF.Exp), reads=[t64], writes=[t64])
            S.op("act", lambda e, sl=sl: e.activation(out=PA["dt"][:, sl], in_=t64[:, :], func=AF.Ln, bias=onecol[:, :], scale=1.0),
                 reads=[t64, onecol], writes=[PA["dt"]])
            S.op("dve", lambda e, sl=sl: e.tensor_tensor(out=PA["da"][:, sl], in0=PA["dt"][:, sl], in1=arow[:, :], op=ALU.mult),
                 reads=[PA["dt"], arow], writes=[PA["da"]])
            Um = U_p if ti < 8 else U_d
            Tm = ONESF if ti < 8 else T_d
            ps2 = self.nextps()
            self.mm(ps2, ps2[:, 0:64], KCB, Um, PA["da"], PA["da"][:, sl], True, True)
            self.mm(ps2, ps2[:, 64:128], KCB, Tm, PA["da"], PA["da"][:, sl], True, True)
            S.op("dve", lambda e, ps2=ps2, sl=sl: e.tensor_scalar(out=PA["nacum"][:, sl], in0=ps2[:, 0:64], scalar1=-1.0, scalar2=None, op0=ALU.mult),
                 reads=[ps2], writes=[PA["nacum"]])
            S.op("dve", lambda e, ps2=ps2, sl=sl: e.tensor_tensor(out=t64[:, :], in0=ps2[:, 64:128], in1=PA["nacum"][:, sl], op=ALU.add),
                 reads=[ps2, PA["nacum"]], writes=[t64])
            S.op("act", lambda e: e.activation(out=t64[:, :], in_=t64[:, :], func=AF.Exp), reads=[t64], writes=[t64])
            S.op("dve", lambda e, sl=sl: e.tensor_tensor(out=PA["dtd"][:, sl], in0=t64[:, :], in1=PA["dt"][:, sl], op=ALU.mult),
                 reads=[t64, PA["dt"]], writes=[PA["dtd"]])
            S.op("act", lambda e, ps2=ps2, sl=sl: e.activation(out=PA["cdec"][:, sl], in_=ps2[:, 64:128], func=AF.Exp), reads=[ps2], writes=[PA["cdec"]])
            if ti == 0:
                S.op("dve", lambda e, ps2=ps2: e.tensor_copy(out=totc[:, :], in_=ps2[:, 64:128]), reads=[ps2], writes=[totc])
            elif ti < 8:
                S.op("dve", lambda e, ps2=ps2: e.tensor_tensor(out=totc[:, :], in0=totc[:, :], in1=ps2[:, 64:128], op=ALU.add), reads=[ps2, totc], writes=[totc])
            else:
                S.op("act", lambda e, ps2=ps2: e.activation(out=self.ea_d[:, :], in_=ps2[:, 0:64], func=AF.Exp), reads=[ps2], writes=[self.ea_d])
                S.op("dve", lambda e, ps2=ps2: e.tensor_copy(out=self.totd[:, :], in_=ps2[:, 64:128]), reads=[ps2], writes=[self.totd])
        S.op("act", lambda e: e.activation(out=totc[:, :], in_=totc[:, :], func=AF.Exp), reads=[totc], writes=[totc])
        A.free(Wdt, t64, rows01, arow)

        self.wsl = [A.alloc(f"ssw{i}", KC * 256, BF16) for i in range(2)]
        self.wji = 0
        self.rot = [0, 1, 2, 3, 4, 5]
        stT = A.alloc("stT", 512)
        st2 = [A.alloc(f"st2_{i}", 1024, ckey="stage") for i in range(2)]
        eo_st = []
        for g in range(8):
            xc = self.ssd_inproj_conv(g, hT, with_c=False)
            S.op("pool", lambda e: e.memset(stT[:, :], 0.0), writes=[stT])
            for ti in range(8):
                xtm, xdtd, xdt = self.ssd_tile_common(ti, g, xc, need_xdt=False)
                self.ssd_state_step(ti, g, xtm, xdtd, stT)
                A.free(xtm, xdtd, xdt)
            sb = st2[(g // 2) % 2]
            S.op("pool", lambda e, sb=sb, g=g: e.tensor_copy(out=sb[:, (g % 2) * 512:(g % 2 + 1) * 512], in_=stT[:, :]), reads=[stT], writes=[sb])
            A.free(*xc)
            if g % 2 == 1:
                def fill_s(ein, exi, sb=sb):
                    self.dma("sp", ein[:, :], sb[:, :], reads=[sb], writes=[exi], chan_buf=sb)
                eo_st.append(self.allgather(f"ex1_{g // 2}", 1024, fill_s))

        def fill_t(ein, exi):
            self.dma("sp", ein[:, :], totc[:, :], reads=[totc], writes=[exi], chan_buf=totc)
        eot, exot = self.allgather("ex1t", 64, fill_t)
        decall = A.alloc("decall", 4 * 64, ckey="exl")
        for r in range(4):
            self.dma("sp", decall[:, r * 64:(r + 1) * 64], eot[r * 128:(r + 1) * 128, :], reads=[exot], writes=[decall],
                     chan_buf=decall, group=(r > 0))
        dd = A.alloc("ddsel", 4 * 64)
        for r in range(4):
            S.op("dve", lambda e, r=r: e.tensor_scalar(out=dd[:, r * 64:(r + 1) * 64], in0=decall[:, r * 64:(r + 1) * 64], scalar1=cv[:, 24 + r:25 + r],
                                                       scalar2=cv[:, 28 + r:29 + r], op0=ALU.mult, op1=ALU.add), reads=[decall, cv], writes=[dd])
        A.free(decall, totc, *st2)

        stTb = A.alloc("stTb", 512, BF16)
        Sr = [A.alloc(f"Sr{i}", 512, ckey="exl") for i in range(2)]
        rowsg = A.alloc("rowsg", 1024, ckey="once")
        gT = A.alloc("gT", 4 * NTOK, BF16)
        wov = w_out.rearrange("(m p) n -> p m n", p=128)
        for g in range(8):
            xc = self.ssd_inproj_conv(g, hT, with_c=True)
            for i in range(2):
                b = self.wsl[i]
                self.dma("pool", b[:, :].rearrange("p (k n) -> p k n", k=KC), self.ssm_wv[:, :, g * 512 + i * 256:g * 512 + (i + 1) * 256], writes=[b])
            self.dma("sp", rowsg[:, 0:512], rows_d[:, 128 + g * 512:128 + (g + 1) * 512], writes=[rowsg])
            self.dma("sp", rowsg[:, 512:1024], rows_d[:, 128 + 4096 + g * 512:128 + 4096 + (g + 1) * 512], writes=[rowsg], group=True)
            S.op("pool", lambda e: e.memset(stT[:, :], 0.0), writes=[stT])
            st3 = stT[:, :].rearrange("p (r q) -> p r q", r=8)
            for r in range(4):
                sr = Sr[r % 2]
                eo, ex1o = eo_st[g // 2]
                self.dma("sp", sr[:, :], eo[r * 128:(r + 1) * 128, (g % 2) * 512:(g % 2 + 1) * 512], reads=[ex1o], writes=[sr], chan_buf=sr)
                S.op("dve", lambda e, r=r, g=g: e.tensor_tensor(out=st3, in0=st3, in1=dd[:, r * 64 + 8 * g:r * 64 + 8 * g + 8].unsqueeze(2).broadcast_to([128, 8, 64]),
                                                                op=ALU.mult), reads=[stT, dd], writes=[stT])
                S.op("dve", lambda e, r=r, sr=sr: e.scalar_tensor_tensor(out=stT[:, :], in0=sr[:, :], scalar=cv[:, 24 + r:25 + r], in1=stT[:, :],
                                                                         op0=ALU.mult, op1=ALU.add), reads=[sr, cv, stT], writes=[stT])
            S.op("act", lambda e: e.activation(out=stTb[:, :], in_=stT[:, :], func=AF.Copy), reads=[stT], writes=[stTb])
            for ti in range(9):
                xtm, xdtd, xdt = self.ssd_tile_common(ti, g, xc, need_xdt=True)
                yoff_tm = None
                if ti == 8:
                    yoff_tm = self.ssd_decode_states(g, xc, xtm, xdtd, st_d, o_sssm)
                self.ssd_tile_y(ti, g, xc, xtm, xdt, stTb, yoff_tm, rowsg, gT, hT)
                if ti < 8:
                    self.ssd_state_step(ti, g, xtm, xdtd, stT)
                    S.op("act", lambda e: e.activation(out=stTb[:, :], in_=stT[:, :], func=AF.Copy), reads=[stT], writes=[stTb])
                A.free(xtm, xdtd, xdt)
                if yoff_tm is not None:
                    A.free(yoff_tm)
                if ti == 7:
                    ps = self.nextps()
                    for q in range(4):
                        S.op("pe", lambda e, ps=ps, q=q: e.transpose(ps[:, q * 128:(q + 1) * 128], stT[:, q * 128:(q + 1) * 128], KCB[:, 512:640]),
                             reads=[stT, KCB], writes=[ps])
                    fo = A.alloc("pssm_o", 512, ckey="ost")
                    self.copy_evac(g, fo[:, :], ps[:, :512], [ps], [fo])
                    self.dma("sp", o_pssm[:, g * 512:(g + 1) * 512], fo[:, :], reads=[fo])
                    A.free(fo)
            A.free(*xc)
            gT3 = gT[:, :].rearrange("p (q t) -> p q t", q=4)
            for dp in range(4):
                b = self.wsl[self.wji % 2]
                self.wji += 1
                self.dma("pool", b[:, 0:2048].rearrange("p (q n) -> p q n", q=4), wov[:, g * 4:(g + 1) * 4, dp * 512:(dp + 1) * 512], writes=[b])
                for dcl in range(4):
                    dc = dp * 4 + dcl
                    for (s, n) in TB:
                        ps = self.nextps()
                        for q in range(4):
                            self.mm(ps, ps[:, :n], b, b[:, q * 512 + dcl * 128:q * 512 + (dcl + 1) * 128], gT, gT3[:, q, s:s + n], q == 0, q == 3)
                        S.op("dve", lambda e, ps=ps, dc=dc, s=s, n=n: e.tensor_tensor(out=xT[dc][:, s:s + n], in0=ps[:, :n], in1=xT[dc][:, s:s + n], op=ALU.add),
                             reads=[ps, xT[dc]], writes=[xT[dc]])
        self.rot = None
        A.free(stT, stTb, rowsg, gT, dd, *Sr, *self.wsl, KCB, cwb, identb, onecol, hTh, self.ea_d, self.totd, *PA.values())

    def ssd_inproj_conv(self, g, hT, with_c):
        S, A = self.S, self.A
        wv = self.ssm_wv
        cwb = self.cwb
        jobs = [[g * 4, g * 4 + 1], [g * 4 + 2, g * 4 + 3], [32 + g] + ([40 + g] if with_c else [])]
        uext = [A.alloc(f"suext{i}", 3 + NP_ + 16 * 11) for i in range(2)]
        acc = A.alloc("sacc", NTOK)
        out = []
        ci = 0
        for chs in jobs:
            b = self.wsl[self.wji % 2]
            self.wji += 1
            b3 = b[:, :].rearrange("p (k n) -> p k n", k=KC)
            if len(chs) == 2 and chs[1] == chs[0] + 1:
                self.dma("pool", b3[:, :, 0:256], wv[:, :, 4096 + chs[0] * 128:4096 + chs[0] * 128 + 256], writes=[b])
            else:
                for i, ch in enumerate(chs):
                    self.dma("pool", b3[:, :, i * 128:(i + 1) * 128], wv[:, :, 4096 + ch * 128:4096 + (ch + 1) * 128], writes=[b], group=(i > 0))
            for i, ch in enumerate(chs):
                ue = uext[ci % 2]
                ci += 1
                udec = ue[:, 3 + NP_:].rearrange("p (b t) -> p b t", b=16)
                self.dma_nc("sp", udec[:, :, 0:3], self.ssm_hist_d[:, ch * 48:(ch + 1) * 48].rearrange("p (b t) -> p b t", b=16), writes=[ue])
                ps = self.nextps()
                for k in range(KC):
                    self.mm(ps, ps[:, :3], b, b[:, k * 256 + i * 128:k * 256 + (i + 1) * 128], self.hTh, self.hTh[:, k * 3:(k + 1) * 3], k == 0, k == KC - 1)
                S.op("act", lambda e, ps=ps, ue=ue: e.activation(out=ue[:, 0:3], in_=ps[:, :3], func=AF.Copy), reads=[ps], writes=[ue])
                ei = 0
                for (s, n) in TB:
                    ps = self.nextps()
                    for k in range(KC):
                        self.mm(ps, ps[:, :n], b, b[:, k * 256 + i * 128:k * 256 + (i + 1) * 128], hT[k], hT[k][:, s:s + n], k == 0, k == KC - 1)
                    if s < NP_:
                        self.copy_evac(ei, ue[:, 3 + s:3 + s + n], ps[:, :n], [ps], [ue])
                    else:
                        self.copy_evac(ei, udec[:, :, 3:11], ps[:, :n].rearrange("p (b t) -> p b t", b=16), [ps], [ue])
                    ei += 1
                wc = [cwb[:, ch * 4 + k:ch * 4 + k + 1] for k in range(4)]
                accd = acc[:, NP_:].rearrange("p (b t) -> p b t", b=16)
                S.op("dve", lambda e, ue=ue, wc=wc: e.tensor_scalar(out=acc[:, 0:NP_], in0=ue[:, 3:3 + NP_], scalar1=wc[3], scalar2=None, op0=ALU.mult),
                     reads=[ue, cwb], writes=[acc])
                S.op("dve", lambda e, udec=udec, wc=wc, accd=accd: e.tensor_scalar(out=accd, in0=udec[:, :, 3:11], scalar1=wc[3], scalar2=None, op0=ALU.mult),
                     reads=[ue, cwb], writes=[acc])
                for k in range(3):
                    S.op("dve", lambda e, ue=ue, wc=wc, k=k: e.scalar_tensor_tensor(out=acc[:, 0:NP_], in0=ue[:, k:k + NP_], scalar=wc[k], in1=acc[:, 0:NP_],
                                                                                    op0=ALU.mult, op1=ALU.add), reads=[ue, cwb, acc], writes=[acc])
                    S.op("dve", lambda e, udec=udec, wc=wc, k=k, accd=accd: e.scalar_tensor_tensor(out=accd, in0=udec[:, :, k:k + 8], scalar=wc[k], in1=accd,
                                                                                                   op0=ALU.mult, op1=ALU.add), reads=[ue, cwb, acc], writes=[acc])
                xo = A.alloc(f"sxc{len(out)}", NTOK, BF16)
                S.op("act", lambda e, xo=xo, ch=ch: e.activation(out=xo[:, :], in_=acc[:, :], func=AF.Silu, bias=cwb[:, 192 + ch:193 + ch], scale=1.0),
                     reads=[acc, cwb], writes=[xo])
                out.append(xo)
                if with_c:
                    osv = self.o_ssc[:, ch * 51:(ch + 1) * 51]
                    self.dma_nc("sp", osv[:, 0:3], ue[:, NP_:NP_ + 3], reads=[ue], chan_buf=ue)
                    self.dma_nc("sp", osv[:, 3:51].rearrange("p (b t) -> p b t", b=16), udec[:, :, 8:11], reads=[ue], chan_buf=ue)
        A.free(acc, *uext)
        return out

    def ssd_tile_common(self, ti, g, xc, need_xdt):
        S, A = self.S, self.A
        PA = self.PA
        c0 = ti * 128
        ps = self.nextps()
        pb = ps[:, :].bitcast(BF16)
        for q in range(5):
            S.op("pe", lambda e, pb=pb, q=q: e.transpose(pb[:, q * 128:(q + 1) * 128], xc[q][:, c0:c0 + 128], self.identb[:, :]),
                 reads=[xc[q], self.identb], writes=[ps])
        xtm = A.alloc("xtm", 640, BF16)
        S.op("act", lambda e: e.activation(out=xtm[:, :], in_=pb[:, 0:640], func=AF.Copy), reads=[ps], writes=[xtm])
        x3 = xtm[:, 0:512].rearrange("p (r q) -> p r q", r=8)
        h0 = ti * 64 + 8 * g
        xdtd = A.alloc("xdtd", 512, BF16)
        S.op("dve", lambda e: e.tensor_tensor(out=xdtd[:, :].rearrange("p (r q) -> p r q", r=8), in0=x3,
                                              in1=PA["dtd"][:, h0:h0 + 8].unsqueeze(2).broadcast_to([128, 8, 64]), op=ALU.mult),
             reads=[xtm, PA["dtd"]], writes=[xdtd])
        xdt = A.alloc("xdt", 512, BF16)
        if need_xdt:
            S.op("pool", lambda e: e.tensor_tensor(out=xdt[:, :].rearrange("p (r q) -> p r q", r=8), in0=x3,
                                                   in1=PA["dt"][:, h0:h0 + 8].unsqueeze(2).broadcast_to([128, 8, 64]), op=ALU.mult),
                 reads=[xtm, PA["dt"]], writes=[xdt])
        return xtm, xdtd, xdt

    def ssd_state_step(self, ti, g, xtm, xdtd, stT):
        S = self.S
        PA = self.PA
        h0 = ti * 64 + 8 * g
        ps = self.nextps()
        self.mm(ps, ps[:, :512], xtm, xtm[:, 512:640], xdtd, xdtd[:, :], True, True)
        st3 = stT[:, :].rearrange("p (r q) -> p r q", r=8)
        S.op("dve", lambda e: e.tensor_tensor(out=st3, in0=st3, in1=PA["cdec"][:, h0:h0 + 8].unsqueeze(2).broadcast_to([128, 8, 64]), op=ALU.mult),
             reads=[stT, PA["cdec"]], writes=[stT])
        S.op("dve", lambda e, ps=ps: e.tensor_tensor(out=stT[:, :], in0=stT[:, :], in1=ps[:, :512], op=ALU.add), reads=[stT, ps], writes=[stT])

    def ssd_tile_y(self, ti, g, xc, xtm, xdt, stTb, yoff_tm, rowsg, gT, hT):
        S, A = self.S, self.A
        PA, KCB = self.PA, self.KCB
        c0 = ti * 128
        dec_t = ti == 8
        Um = KCB[:, 128:256] if dec_t else KCB[:, 0:128]
        NEGM = KCB[:, 1152:1664] if dec_t else KCB[:, 640:1152]
        ONESF, IDF = KCB[:, 384:512], KCB[:, 512:640]
        xcB, xcC = xc[4], xc[5]
        pY = self.PS[7]
        pc = self.nextps()
        self.mm(pc, pc[:, :128], xcB, xcB[:, c0:c0 + 128], xcC, xcC[:, c0:c0 + 128], True, True)
        cbT = A.alloc("cbT", 128, BF16)
        S.op("act", lambda e: e.activation(out=cbT[:, :], in_=pc[:, :128], func=AF.Copy), reads=[pc], writes=[cbT])
        R = A.alloc("Rda", 512)
        ea = A.alloc("ea", 512, BF16)
        CsT = A.alloc("CsT", 512, BF16)
        dec = A.alloc("decm", 512, BF16)
        MT = A.alloc("MT", 512, BF16)
        for hq in range(2):
            hb = ti * 64 + 8 * g + 4 * hq
            S.op("pool", lambda e, hb=hb: e.tensor_tensor(out=R[:, :].rearrange("p (r i) -> p r i", r=4), in0=Um.unsqueeze(1).broadcast_to([128, 4, 128]),
                                                          in1=PA["da"][:, hb:hb + 4].unsqueeze(2).broadcast_to([128, 4, 128]), op=ALU.mult),
                 reads=[KCB, PA["da"]], writes=[R])
            pA = self.nextps()
            self.mm(pA, pA[:, :512], KCB, ONESF, R, R[:, :], True, False)
            if not dec_t:
                S.op("act", lambda e, pA=pA: e.activation(out=ea[:, :], in_=pA[:, :512], func=AF.Exp), reads=[pA], writes=[ea])
                S.op("dve", lambda e: e.tensor_tensor(out=CsT[:, :].rearrange("p (r i) -> p r i", r=4), in0=ea[:, :].rearrange("p (r i) -> p r i", r=4),
                                                      in1=xcC[:, c0:c0 + 128].unsqueeze(1).broadcast_to([128, 4, 128]), op=ALU.mult),
                     reads=[ea, xcC], writes=[CsT])
            self.mm(pA, pA[:, :512], KCB, IDF, KCB, NEGM, False, True)
            for r4 in range(4):
                S.op("act", lambda e, pA=pA, r4=r4, hb=hb: e.activation(out=dec[:, r4 * 128:(r4 + 1) * 128], in_=pA[:, r4 * 128:(r4 + 1) * 128], func=AF.Exp,
                                                                        bias=PA["nacum"][:, hb + r4:hb + r4 + 1], scale=1.0),
                     reads=[pA, PA["nacum"]], writes=[dec])
            S.op("dve", lambda e: e.tensor_tensor(out=MT[:, :].rearrange("p (r i) -> p r i", r=4), in0=dec[:, :].rearrange("p (r i) -> p r i", r=4),
                                                  in1=cbT[:, :].unsqueeze(1).broadcast_to([128, 4, 128]), op=ALU.mult), reads=[dec, cbT], writes=[MT])
            for r4 in range(4):
                h = 4 * hq + r4
                self.mm(pY, pY[:, h * 64:(h + 1) * 64], MT, MT[:, r4 * 128:(r4 + 1) * 128], xdt, xdt[:, h * 64:(h + 1) * 64], True, dec_t)
                if not dec_t:
                    self.mm(pY, pY[:, h * 64:(h + 1) * 64], CsT, CsT[:, r4 * 128:(r4 + 1) * 128], stTb, stTb[:, h * 64:(h + 1) * 64], False, True)
        A.free(R, ea, CsT, dec, MT, cbT)
        ysb = A.alloc("ysb", 512)
        S.op("pool", lambda e: e.tensor_tensor(out=ysb[:, :], in0=xtm[:, 0:512], in1=rowsg[:, 0:512], op=ALU.mult), reads=[xtm, rowsg], writes=[ysb])
        S.op("dve", lambda e: e.tensor_tensor(out=ysb[:, :], in0=ysb[:, :], in1=pY[:, :512], op=ALU.add), reads=[ysb, pY], writes=[ysb])
        if yoff_tm is not None:
            S.op("dve", lambda e: e.tensor_tensor(out=ysb[:, :], in0=ysb[:, :], in1=yoff_tm[:, :], op=ALU.add), reads=[ysb, yoff_tm], writes=[ysb])
        pZ = self.nextps()
        for i in range(2):
            w = self.wsl[i]
            for k in range(KC):
                self.mm(pZ, pZ[:, i * 256:(i + 1) * 256], hT[k], hT[k][:, c0:c0 + 128], w, w[:, k * 256:(k + 1) * 256], k == 0, k == KC - 1)
        sz = A.alloc("sz", 512)
        S.op("act", lambda e: e.activation(out=sz[:, :], in_=pZ[:, :512], func=AF.Silu), reads=[pZ], writes=[sz])
        S.op("dve", lambda e: e.tensor_tensor(out=ysb[:, :], in0=ysb[:, :], in1=sz[:, :], op=ALU.mult), reads=[ysb, sz], writes=[ysb])
        ssq = A.alloc("ssq1", 2)
        S.op("act", lambda e: e.activation(out=sz[:, :], in_=ysb[:, :], func=AF.Square, accum_out=ssq[:, 0:1]), reads=[ysb], writes=[sz, ssq])
        S.op("act", lambda e: e.activation(out=ssq[:, 1:2], in_=ssq[:, 0:1], func=AF.Sqrt, bias=self.epsb[:, :], scale=1.0 / 512),
             reads=[ssq, self.epsb], writes=[ssq])
        S.op("dve", lambda e: e.reciprocal(out=ssq[:, 1:2], in_=ssq[:, 1:2]), reads=[ssq], writes=[ssq])
        gn = A.alloc("gn", 512, BF16)
        S.op("dve", lambda e: e.scalar_tensor_tensor(out=gn[:, :], in0=ysb[:, :], scalar=ssq[:, 1:2], in1=rowsg[:, 512:1024], op0=ALU.mult, op1=ALU.mult),
             reads=[ysb, ssq, rowsg], writes=[gn])
        pt = self.nextps()
        ptb = pt[:, :].bitcast(BF16)
        for q in range(4):
            S.op("pe", lambda e, q=q: e.transpose(ptb[:, q * 128:(q + 1) * 128], gn[:, q * 128:(q + 1) * 128], self.identb[:, :]),
                 reads=[gn, self.identb], writes=[pt])
        gT3 = gT[:, :].rearrange("p (q t) -> p q t", q=4)
        S.op("act", lambda e: e.activation(out=gT3[:, :, c0:c0 + 128], in_=ptb[:, 0:512].rearrange("p (q t) -> p q t", q=4), func=AF.Copy),
             reads=[pt], writes=[gT])
        A.free(ysb, sz, ssq, gn)

    def ssd_decode_states(self, g, xc, xtm, xdtd, st_d, o_sssm):
        S, A = self.S, self.A
        KCB = self.KCB
        IDF, SEQM, SEL = KCB[:, 512:640], KCB[:, 1664:1680], KCB[:, 1680:1696]
        xcC = xc[5]
        pYo = self.PS[6]
        stv = st_d.rearrange("(b t p) n -> p b t n", b=16, t=32)
        aexp = A.alloc("aexp", 512)
        S.op("dve", lambda e: e.tensor_copy(out=aexp[:, :].rearrange("p (r q) -> p r q", r=8), in_=self.totd[:, 8 * g:8 * g + 8].unsqueeze(2).broadcast_to([128, 8, 64])),
             reads=[self.totd], writes=[aexp])
        pD = self.nextps()
        for q in range(4):
            self.mm(pD, pD[:, q * 16:(q + 1) * 16], aexp, aexp[:, q * 128:(q + 1) * 128], KCB, SEL, True, True)
        decfm = A.alloc("decfm", 64)
        S.op("act", lambda e: e.activation(out=decfm[:, :], in_=pD[:, 0:64], func=AF.Exp), reads=[pD], writes=[decfm])
        h0b = [A.alloc(f"h0b{i}", 512, ckey=f"h0b{i}") for i in range(2)]
        nsb = [A.alloc(f"nsb{i}", 512, ckey=f"nsb{i}") for i in range(2)]
        h0T = A.alloc("h0T", 512, BF16)
        Bm = A.alloc("Bm", 128, BF16)
        for b in range(16):
            h0 = h0b[b % 2]
            ns = nsb[b % 2]
            self.dma("sp", h0[:, :].rearrange("p (t n) -> p t n", t=4), stv[:, b, g * 4:(g + 1) * 4, :], writes=[h0])
            pH = self.nextps()
            for q in range(4):
                S.op("pe", lambda e, pH=pH, q=q, h0=h0: e.transpose(pH[:, q * 128:(q + 1) * 128], h0[:, q * 128:(q + 1) * 128], IDF),
                     reads=[h0, KCB], writes=[pH])
            self.copy_evac(b, h0T[:, :], pH[:, :512], [pH], [h0T])
            for q in range(4):
                self.mm(pYo, pYo[:, q * 128 + b * 8:q * 128 + b * 8 + 8], h0T, h0T[:, q * 128:(q + 1) * 128], xcC, xcC[:, NP_ + b * 8:NP_ + b * 8 + 8], True, True)
            S.op("pool", lambda e, b=b: e.tensor_scalar(out=Bm[:, :], in0=xtm[:, 512:640], scalar1=SEQM[:, b:b + 1], scalar2=None, op0=ALU.mult),
                 reads=[xtm, KCB], writes=[Bm])
            pN = self.nextps()
            for q in range(4):
                self.mm(pN, pN[:, q * 128:(q + 1) * 128], xdtd, xdtd[:, q * 128:(q + 1) * 128], Bm, Bm[:, :], True, True)
            for q in range(4):
                S.op("dve", lambda e, q=q, b=b, h0=h0, ns=ns, pN=pN: e.scalar_tensor_tensor(
                    out=ns[:, q * 128:(q + 1) * 128], in0=h0[:, q * 128:(q + 1) * 128], scalar=decfm[:, q * 16 + b:q * 16 + b + 1],
                    in1=pN[:, q * 128:(q + 1) * 128], op0=ALU.mult, op1=ALU.add), reads=[h0, decfm, pN], writes=[ns])
            self.dma("sp", o_sssm[b * 128:(b + 1) * 128, g * 512:(g + 1) * 512], ns[:, :], reads=[ns])
        yoT = A.alloc("yoT", 512, BF16)
        S.op("act", lambda e: e.activation(out=yoT[:, :], in_=pYo[:, :512], func=AF.Copy), reads=[pYo], writes=[yoT])
        pt = self.nextps()
        ptb = pt[:, :].bitcast(BF16)
        for q in range(4):
            S.op("pe", lambda e, q=q: e.transpose(ptb[:, q * 128:(q + 1) * 128], yoT[:, q * 128:(q + 1) * 128], self.identb[:, :]),
                 reads=[yoT, self.identb], writes=[pt])
        yoff = A.alloc("yoff_tm", 512)
        S.op("dve", lambda e: e.tensor_tensor(out=yoff[:, :].rearrange("p (r q) -> p r q", r=8), in0=ptb[:, 0:512].rearrange("p (r q) -> p r q", r=8),
                                              in1=self.ea_d[:, 8 * g:8 * g + 8].unsqueeze(2).broadcast_to([128, 8, 64]), op=ALU.mult),
             reads=[pt, self.ea_d], writes=[yoff])
        A.free(aexp, decfm, h0T, Bm, yoT, *h0b, *nsb)
        return yoff


PV_NORM_FFN = 0
PV_NORM_MIX = 64
PV_NORM_FIN = 96
PV_QNORM = 112
PV_SCW = 120
PV_KVN = 144
PV_N = 160


def _fm(v, nchunk):
    return np.ascontiguousarray(np.asarray(v, np.float32).reshape(nchunk, 128).T)


def _rope_tables(pos):
    inv = np.power(np.float32(10000.0), -np.arange(0, 64, 2, dtype=np.float32) / np.float32(64))
    ang = pos.astype(np.float32)[:, None] * inv[None, :]
    c = np.cos(ang).astype(np.float32)
    s = np.sin(ang).astype(np.float32)
    cos2 = np.concatenate([c, c], axis=1)
    sin2 = np.concatenate([-s, s], axis=1)
    return cos2, sin2


def _bc(v, n=128):
    v = np.asarray(v, np.float32).reshape(1, -1)
    return np.ascontiguousarray(np.broadcast_to(v, (n, v.shape[1])))


SC_N = 1696


def _ssd_consts():
    f = np.float32
    k = np.arange(128)[:, None]
    i = np.arange(128)[None, :]
    same = (k // 8 == i // 8)
    U_p = (k <= i).astype(f)
    U_d = (same & (k <= i)).astype(f)
    T_d = same.astype(f)
    ones = np.ones((128, 128), f)
    idf = np.eye(128, dtype=f)
    negp = np.where(i < k, NEG, 0.0).astype(f)
    negd = np.where(same & (i >= k), 0.0, NEG).astype(f)
    seqm = (k // 8 == np.arange(16)[None, :]).astype(f)
    sel = (k == 8 * np.arange(16)[None, :]).astype(f)
    return np.ascontiguousarray(np.concatenate([U_p, U_d, T_d, ones, idf, np.tile(negp, (1, 4)), np.tile(negd, (1, 4)), seqm, sel], axis=1))


def _ssd_host_shared(inp):
    f = np.float32
    g = lambda k: np.asarray(inp[k], f)
    cwv = g("ssm_conv_w")[0]
    rows = np.concatenate([g("ssm_dt_bias")[0], g("ssm_a_log")[0], np.repeat(g("ssm_d")[0], 64), g("ssm_norm")[0]])
    return {"ssm_w_in": g("ssm_w_in")[0], "ssm_w_out": g("ssm_w_out")[0],
            "ssm_cw": np.ascontiguousarray(cwv.reshape(4, 48, 128).transpose(2, 1, 0).reshape(128, 192)),
            "ssm_cb": _fm(g("ssm_conv_b")[0], 48), "ssm_rows": _bc(rows), "ssm_consts": _ssd_consts()}


def _ssd_host_core(inp, c):
    f = np.float32
    hs = np.asarray(inp["state_ssm_conv"], f)[0, 16 * c:16 * c + 16]
    hist = np.ascontiguousarray(hs.reshape(16, 3, 48, 128).transpose(3, 2, 0, 1).reshape(128, 48 * 48))
    st = np.asarray(inp["state_ssm"], f)[0, 16 * c:16 * c + 16].reshape(16 * 4096, 128)
    return {"ssm_hist": hist, "state_ssm": st}


def run_step(inp, mode="full", stop=99):
    f = np.float32
    g = lambda k: np.asarray(inp[k], f)
    npool = int(np.asarray(inp["cache_ckv"]).shape[1]) if "cache_ckv" in inp else 10240
    prog = Prog(mode=mode, npool=npool, stop=stop)
    nc = prog.build()
    need = set(prog.din)
    shared = {}
    pvh = np.zeros((128, PV_N), f)
    nf = g("norm_ffn").reshape(4, D)
    for i in range(4):
        pvh[:, PV_NORM_FFN + 16 * i:PV_NORM_FFN + 16 * (i + 1)] = _fm(nf[i], 16)
    nm = g("norm_mix")
    for i in range(2):
        pvh[:, PV_NORM_MIX + 16 * i:PV_NORM_MIX + 16 * (i + 1)] = _fm(nm[i], 16)
    pvh[:, PV_NORM_FIN:PV_NORM_FIN + 16] = _fm(g("norm_final"), 16)
    pvh[:, PV_QNORM:PV_QNORM + 6] = _fm(g("mla_q_norm")[0], 6)
    pvh[:, PV_KVN:PV_KVN + 4] = _fm(g("mla_kv_norm")[0], 4)
    scw = g("sconv_w")[0]
    pvh[:, PV_SCW:PV_SCW + 24] = scw.reshape(3, 8, 128).transpose(2, 1, 0).reshape(128, 24)
    shared["pv"] = pvh
    if "ffn_w_gate" in need:
        shared["ffn_w_gate"] = g("ffn_w_gate").reshape(4, D, FF)
        shared["ffn_w_up"] = g("ffn_w_up").reshape(4, D, FF)
        shared["ffn_w_down"] = g("ffn_w_down").reshape(4, FF, D)
    if "ab_w_in" in need:
        shared["ab_w_in"] = g("ab_w_in")[0]
        shared["kvn"] = _bc(g("mla_kv_norm")[0])
        w_uk = g("mla_w_uk")[0]
        shared["mla_w_uq"] = g("mla_w_uq")[0]
        shared["mla_w_uk"] = w_uk.reshape(KVL, 1024)
        shared["mla_w_uv"] = g("mla_w_uv")[0].reshape(KVL, 1024)
        shared["w_ukT"] = np.ascontiguousarray(w_uk.transpose(1, 2, 0).reshape(1024, KVL))
        shared["ab_w_out"] = g("ab_w_out")[0]
        shared["cache_ckv"] = g("cache_ckv").reshape(npool * 16, 4096)
        shared["cache_krope"] = g("cache_krope").reshape(npool * 16, 512)
        sidx = np.arange(128)[:, None]
        qidx = np.arange(512)[None, :]
        shared["cmask"] = np.concatenate([(qidx >= r * 128 + sidx).astype(f) for r in range(4)], axis=1)
        kb_, kt_ = np.arange(128) // 8, np.arange(128) % 8
        dmask = np.zeros((128, 16, 8, 8), f)
        for b in range(16):
            for t in range(8):
                dmask[:, b, :, t] = ((kb_ == b) & (kt_ <= t)).astype(f)[:, None]
        shared["dmask"] = dmask.reshape(128, 1024)
        ssc = g("state_sconv")[0]
        pt = np.asarray(inp["page_table"], np.int32)
    shared["ident"] = np.eye(128, dtype=f)
    if "ssm_w_in" in need:
        shared.update(_ssd_host_shared(inp))
    xp = g("x_prompt")
    xs = g("x_sample")
    in_maps = []
    for c in range(8):
        b, j = c // 4, c % 4
        xt = np.concatenate([xp[b, j * 1024:(j + 1) * 1024], xs[16 * c:16 * c + 16].reshape(128, D)], axis=0)
        m = dict(shared)
        m["xT"] = np.ascontiguousarray(xt.T.reshape(KC, 128, NTOK).transpose(1, 0, 2).reshape(128, KC * NTOK))
        cvh = np.zeros((128, 32), f)
        for kb in range(4):
            cvh[:, kb] = 0.0 if kb <= j else NEG
            cvh[:, 4 + kb] = 0.0 if kb < j else NEG
            cvh[:, 8 + kb] = 1.0 if kb == j else 0.0
            cvh[:, 12 + kb] = 1.0 if kb < j else 0.0
            cvh[:, 17 + kb] = 1.0 if kb == j - 1 else 0.0
            cvh[:, 24 + kb] = 1.0 if kb < j else 0.0
            cvh[:, 28 + kb] = 0.0 if kb < j else 1.0
        cvh[:, 16] = np.arange(128) % 16
        m["cv"] = cvh
        if "ab_w_in" in need:
            pos = np.concatenate([np.arange(j * 1024, (j + 1) * 1024), np.tile(8192 + np.arange(8), 16)])
            cos2, sin2 = _rope_tables(pos)
            rt = np.concatenate([cos2, sin2], axis=1)
            m["rope_tm"] = np.ascontiguousarray(rt.reshape(9, 128, 128).transpose(1, 0, 2).reshape(128, 9 * 128))
            rope_fm = np.zeros((128, 2 * NTOK), f)
            rope_fm[:64, :NTOK] = cos2.T
            rope_fm[:64, NTOK:] = sin2.T
            m["rope_fm"] = rope_fm
            h = ssc[16 * c:16 * c + 16]
            m["sc_hist"] = np.ascontiguousarray(h.reshape(16, 2, 8, 128).transpose(3, 2, 0, 1).reshape(128, 8 * 32))
            ptc = pt[16 * c:16 * c + 16]
            m["ptx"] = np.ascontiguousarray(ptc.reshape(16, 8, 8)[:, :, np.arange(128) // 16].transpose(2, 0, 1).reshape(128, 128)).astype(np.int32)
        if "ssm_w_in" in need:
            m.update(_ssd_host_core(inp, c))
        in_maps.append({k: v for k, v in m.items() if k in need})
    res = run_bass_kernel_spmd(nc, in_maps, core_ids=list(range(8)))
    R = res.results
    y_prompt = np.zeros((2, 4096, D), f)
    y_sample = np.zeros((128, 8, D), f)
    p_ckv = np.zeros((1, 2, 4096, KVL), f)
    p_kr = np.zeros((1, 2, 4096, ROPE), f)
    s_ckv = np.zeros((1, 128, 8, KVL), f)
    s_kr = np.zeros((1, 128, 8, ROPE), f)
    p_sc = np.zeros((1, 2, 2, SCD), f)
    s_sc = np.zeros((1, 128, 2, SCD), f)
    p_ssc = np.zeros((1, 2, 3, 6144), f)
    s_ssc = np.zeros((1, 128, 3, 6144), f)
    p_ssm = np.zeros((1, 2, 64, 64, 128), f)
    s_ssm = np.zeros((1, 128, 64, 64, 128), f)
    for c in range(8):
        b, j = c // 4, c % 4
        r = R[c]
        yt = r["o_yT"].reshape(128, KC, NTOK).transpose(2, 1, 0).reshape(NTOK, D)
        y_prompt[b, j * 1024:(j + 1) * 1024] = yt[:1024]
        y_sample[16 * c:16 * c + 16] = yt[1024:].reshape(16, 8, D)
        if "o_ssc" in r:
            ossc = r["o_ssc"].reshape(128, 48, 51)
            if j == 3:
                p_ssc[0, b] = ossc[:, :, 0:3].transpose(2, 1, 0).reshape(3, 6144)
                p_ssm[0, b] = r["o_pssm"].reshape(2, 64, 32, 128).transpose(2, 0, 1, 3).reshape(64, 64, 128)
            s_ssc[0, 16 * c:16 * c + 16] = ossc[:, :, 3:51].reshape(128, 48, 16, 3).transpose(2, 3, 1, 0).reshape(16, 3, 6144)
            s_ssm[0, 16 * c:16 * c + 16] = r["o_sssm"].reshape(16, 2, 64, 32, 128).transpose(0, 3, 1, 2, 4).reshape(16, 64, 64, 128)
        if "o_ckv" not in r:
            continue
        p_ckv[0, b, j * 1024:(j + 1) * 1024] = r["o_ckv"][:1024]
        s_ckv[0, 16 * c:16 * c + 16] = r["o_ckv"][1024:].reshape(16, 8, KVL)
        p_kr[0, b, j * 1024:(j + 1) * 1024] = r["o_kr"][:1024]
        s_kr[0, 16 * c:16 * c + 16] = r["o_kr"][1024:].reshape(16, 8, ROPE)
        osc = r["o_sc"].reshape(128, 8, 34)
        if j == 3:
            p_sc[0, b] = osc[:, :, 0:2].transpose(2, 1, 0).reshape(2, SCD)
        s_sc[0, 16 * c:16 * c + 16] = osc[:, :, 2:34].reshape(128, 8, 16, 2).transpose(2, 3, 1, 0).reshape(16, 2, SCD)
    return (y_prompt, y_sample, p_ckv, p_kr, p_sc, p_ssc, p_ssm, s_ckv, s_kr, s_sc, s_ssc, s_ssm)


def kernel(x_prompt, x_sample, cache_ckv, cache_krope, state_sconv, state_ssm_conv, state_ssm, page_table,
           norm_ffn, ffn_w_gate, ffn_w_up, ffn_w_down, norm_mix, ab_w_in, mla_q_norm, mla_w_uq, mla_kv_norm,
           mla_w_uk, mla_w_uv, sconv_w, ab_w_out, ssm_w_in, ssm_conv_w, ssm_conv_b, ssm_dt_bias, ssm_a_log,
           ssm_d, ssm_norm, ssm_w_out, norm_final):
    inp = dict(x_prompt=x_prompt, x_sample=x_sample, cache_ckv=cache_ckv, cache_krope=cache_krope, state_sconv=state_sconv,
               state_ssm_conv=state_ssm_conv, state_ssm=state_ssm, page_table=page_table, norm_ffn=norm_ffn,
               ffn_w_gate=ffn_w_gate, ffn_w_up=ffn_w_up, ffn_w_down=ffn_w_down, norm_mix=norm_mix, ab_w_in=ab_w_in,
               mla_q_norm=mla_q_norm, mla_w_uq=mla_w_uq, mla_kv_norm=mla_kv_norm, mla_w_uk=mla_w_uk, mla_w_uv=mla_w_uv,
               sconv_w=sconv_w, ab_w_out=ab_w_out, ssm_w_in=ssm_w_in, ssm_conv_w=ssm_conv_w, ssm_conv_b=ssm_conv_b,
               ssm_dt_bias=ssm_dt_bias, ssm_a_log=ssm_a_log, ssm_d=ssm_d, ssm_norm=ssm_norm, ssm_w_out=ssm_w_out,
               norm_final=norm_final)
    return run_step(inp, "full")
```

```python
import math
import numpy as np
import concourse.bass as bass
import concourse.mybir as mybir
from concourse.bass_utils import run_bass_kernel_spmd

F32 = mybir.dt.float32
BF16 = mybir.dt.bfloat16
I32 = mybir.dt.int32
AF = mybir.ActivationFunctionType
ALU = mybir.AluOpType
EPOCH = 16000

D = 2048
KC = 16
FF = 5632
NTOK = 1152
NP_ = 1024
ND = 128
TB = [(0, 512), (512, 512), (1024, 128)]
EPS = 1e-6
QL = 768
KVL = 512
ROPE = 64
SCD = 1024
AB_IN = 4416
O1, O2, O3, O4, O5 = 768, 1280, 1344, 2368, 3392
SCALE = 1.0 / math.sqrt(192.0)
NEG = -30000.0


class Chan:
    __slots__ = ("sem", "count", "name", "inc")

    def __init__(self, name, inc=16):
        self.name = name
        self.sem = None
        self.count = 0
        self.inc = inc


class Buf:
    __slots__ = ("name", "ap", "w", "rd", "chan", "dma_w", "dma_r", "pre_chan", "ckey")

    def __init__(self, name, ap, ckey=None):
        self.name = name
        self.ckey = ckey
        self.ap = ap
        self.w = None
        self.rd = []
        self.chan = None
        self.dma_w = False
        self.dma_r = False
        self.pre_chan = []

    def __getitem__(self, k):
        return self.ap[k]


class Op:
    __slots__ = ("eng", "fn", "deps", "cwaits", "dma", "chan", "signal", "tok")

    def __init__(self, eng, fn):
        self.eng = eng
        self.fn = fn
        self.deps = []
        self.cwaits = []
        self.dma = False
        self.chan = None
        self.signal = False
        self.tok = None


class Sched:
    ENGS = ("pe", "act", "dve", "pool", "sp")

    def __init__(self, nc):
        self.nc = nc
        self.ops = {e: [] for e in self.ENGS}
        self.chans = []
        self.chan_reg = {}
        self.nops = 0

    def new_chan(self, name, inc=16):
        c = self.chan_reg.get(name)
        if c is None:
            c = Chan(name, inc)
            self.chans.append(c)
            self.chan_reg[name] = c
        return c

    def op(self, eng, fn, reads=(), writes=(), dma=False, group=False, chan_buf=None, inc=16):
        o = Op(eng, fn)
        deps = {}
        cw = {}

        def addc(c, v):
            old = cw.get(id(c))
            if old is None or old[1] < v:
                cw[id(c)] = (c, v)

        for b in reads:
            if b.w is not None:
                deps[id(b.w)] = b.w
            if b.dma_w:
                addc(b.chan, b.chan.count)
            for (c, v) in b.pre_chan:
                addc(c, v)
        for b in writes:
            if b.w is not None:
                deps[id(b.w)] = b.w
            for r in b.rd:
                deps[id(r)] = r
            if (b.dma_w or b.dma_r) and not (group and dma):
                addc(b.chan, b.chan.count)
            for (c, v) in b.pre_chan:
                addc(c, v)
        for d in deps.values():
            if d.eng == "pe" and eng == "pe":
                continue
            d.signal = True
            o.deps.append(d)
        if dma:
            cb = chan_buf if chan_buf is not None else (list(writes) + list(reads))[0]
            if cb.chan is None:
                cb.chan = self.new_chan(cb.ckey or cb.name, inc)
            ch = cb.chan
            if not group and ch.count > 0:
                addc(ch, ch.count)
        o.cwaits = list(cw.values())
        if dma:
            o.dma = True
            ch.count += ch.inc
            o.chan = ch
            for b in writes:
                if b.chan is None:
                    b.chan = ch
                if not group:
                    b.w = None
                    b.rd = []
                    b.pre_chan = []
                    b.dma_w = False
                    b.dma_r = False
                if b.chan is ch:
                    b.dma_w = True
                else:
                    b.pre_chan.append((ch, ch.count))
            for b in reads:
                if b.chan is None:
                    b.chan = ch
                if b.chan is ch:
                    b.dma_r = True
                else:
                    b.pre_chan.append((ch, ch.count))
        else:
            for b in writes:
                b.w = o
                b.rd = []
                b.dma_w = False
                b.dma_r = False
                b.pre_chan = []
            for b in reads:
                b.rd.append(o)
        self.ops[eng].append(o)
        self.nops += 1
        return o

    def emit(self, final_wait_eng="sp"):
        nc = self.nc
        esems = {}
        pos = {}
        for e in self.ENGS:
            for i, o in enumerate(self.ops[e]):
                pos[id(o)] = i
                o.signal = False
        for e in self.ENGS:
            seenp = {}
            for o in self.ops[e]:
                best = {}
                for d in o.deps:
                    if d.dma:
                        continue
                    pi = pos[id(d)]
                    if pi > seenp.get(d.eng, -1) and pi > best.get(d.eng, (-1, None))[0]:
                        best[d.eng] = (pi, d)
                for de, (pi, d) in best.items():
                    d.signal = True
                    seenp[de] = pi
        for e in self.ENGS:
            n = 0
            for o in self.ops[e]:
                if o.signal and not o.dma:
                    o.tok = (e, n // EPOCH, n % EPOCH + 1)
                    n += 1
                else:
                    o.tok = None
            nep = (n + EPOCH - 1) // EPOCH
            esems[e] = [nc.alloc_semaphore(f"es_{e}_{i}") for i in range(max(nep, 1))]
            print("signals", e, n)
        for i, c in enumerate(self.chans):
            c.sem = nc.alloc_semaphore(f"ch{i}")
        sched = self

        def run(ename, eng):
            seen = {}
            for o in sched.ops[ename]:
                for d in o.deps:
                    if d.tok is None:
                        continue
                    (e, ep, v) = d.tok
                    key = (e, ep)
                    if seen.get(key, 0) < v:
                        eng.wait_ge(esems[e][ep], v)
                        seen[key] = v
                for (c, v) in o.cwaits:
                    key = id(c)
                    if seen.get(key, 0) < v:
                        eng.wait_ge(c.sem, v)
                        seen[key] = v
                ins = o.fn(eng)
                if o.dma:
                    if o.chan.inc == 16:
                        ins.then_inc(o.chan.sem, 16)
                    else:
                        ins.then_inc(o.chan.sem)
                elif o.signal:
                    (e, ep, v) = o.tok
                    ins.then_inc(esems[e][ep], 1)
            if ename == final_wait_eng:
                for c in sched.chans:
                    if c.count > 0 and seen.get(id(c), 0) < c.count:
                        eng.wait_ge(c.sem, c.count)

        with nc.Block() as block:
            @block.tensor
            def _(t):
                run("pe", t)

            @block.scalar
            def _(a):
                run("act", a)

            @block.vector
            def _(v):
                run("dve", v)

            @block.gpsimd
            def _(g):
                run("pool", g)

            @block.sync
            def _(s):
                run("sp", s)


class Arena:
    def __init__(self, nc, ncols, name="arena"):
        self.t = nc.alloc_sbuf_tensor(name, [128, ncols], F32)
        self.ncols = ncols
        self.live = []
        self.dead = []

    def _find(self, cols32):
        pos = 0
        for (s, e, _) in sorted(self.live, key=lambda x: x[0]):
            if s - pos >= cols32:
                return pos
            pos = max(pos, e)
        assert pos + cols32 <= self.ncols, f"arena full: need {cols32} at {pos} of {self.ncols}"
        return pos

    def alloc(self, name, cols, dtype=F32, ckey=None):
        cols32 = (cols + 1) // 2 if dtype == BF16 else cols
        cols32 = (cols32 + 7) // 8 * 8
        start = self._find(cols32)
        end = start + cols32
        ap = self.t[:][:, start:end]
        if dtype == F32:
            ap = ap[:, :cols]
        else:
            ap = ap.bitcast(dtype)[:, :cols]
        b = Buf(name, ap, ckey)
        keep = []
        for (s, e, ob) in self.dead:
            if s < end and start < e:
                if ob.w is not None:
                    b.rd.append(ob.w)
                b.rd.extend(ob.rd)
                if (ob.dma_w or ob.dma_r) and ob.chan is not None:
                    b.pre_chan.append((ob.chan, ob.chan.count))
                b.pre_chan.extend(ob.pre_chan)
                if s < start or e > end:
                    keep.append((s, e, ob))
            else:
                keep.append((s, e, ob))
        self.dead = keep
        self.live.append((start, end, b))
        return b

    def free(self, *bufs):
        for b in bufs:
            for i, (s, e, ob) in enumerate(self.live):
                if ob is b:
                    self.live.pop(i)
                    self.dead.append((s, e, ob))
                    break
            else:
                raise KeyError(b.name)

    def used(self):
        return max([e for (_, e, _) in self.live] + [0])


class Prog:
    def __init__(self, mode="full", npool=10240, stop=99):
        self.mode = mode
        self.stop = stop
        self.npool = npool
        nc = self.nc = bass.Bass("TRN2", target_bir_lowering=False)
        self.S = Sched(nc)
        self.A = Arena(nc, 52000)
        self.PS = [Buf(f"ps{i}", nc.alloc_psum_tensor(f"ps{i}", [128, 512], F32)[:]) for i in range(8)]
        self.psi = 0
        self.din = {}
        self.dout = {}

    def inp(self, name, shape, dt=F32):
        if name in self.din:
            return self.din[name].ap()
        t = self.nc.dram_tensor(name, list(shape), dt, kind="ExternalInput")
        self.din[name] = t
        return t.ap()

    def outp(self, name, shape, dt=F32):
        t = self.nc.dram_tensor(name, list(shape), dt, kind="ExternalOutput")
        self.dout[name] = t
        return t.ap()

    def nextps(self):
        rot = getattr(self, "rot", None) or list(range(8))
        b = self.PS[rot[self.psi % len(rot)]]
        self.psi += 1
        return b

    def mm(self, ps, out_ap, lhsT, lhsT_ap, rhs, rhs_ap, start, stop):
        self.S.op("pe", lambda e: e.matmul(out_ap, lhsT_ap, rhs_ap, start=start, stop=stop),
                  reads=[lhsT, rhs], writes=[ps])

    def dma(self, eng, out_ap, in_ap, reads=(), writes=(), **kw):
        return self.S.op(eng, lambda e: e.dma_start(out=out_ap, in_=in_ap), reads=reads, writes=writes, dma=True, **kw)

    def dma_nc(self, eng, out_ap, in_ap, reads=(), writes=(), **kw):
        return self.S.op(eng, lambda e: e.dma_start(out=out_ap, in_=in_ap, allow_slow_non_contiguous=True),
                         reads=reads, writes=writes, dma=True, **kw)

    def rmsnorm_fm(self, src, dst, gain, gcol0, dim, blocks, src_cols=None):
        S, A = self.S, self.A
        nk = len(src)
        sq = [A.alloc(f"sq{i}", 512, BF16) for i in range(2)]
        rstd = A.alloc("rstd", 512)
        for (s, n) in blocks:
            ps = self.nextps()
            for k in range(nk):
                q = sq[k % 2]
                S.op("act", lambda e, k=k, q=q, s=s, n=n: e.activation(out=q[:, :n], in_=src[k][:, s:s + n], func=AF.Square),
                     reads=[src[k]], writes=[q])
                self.mm(ps, ps[:, :n], self.ones, self.ones[:, :], q, q[:, :n], k == 0, k == nk - 1)
            S.op("act", lambda e, ps=ps, n=n: e.activation(out=rstd[:, :n], in_=ps[:, :n], func=AF.Sqrt, bias=self.epsb[:, :], scale=1.0 / dim),
                 reads=[ps, self.epsb], writes=[rstd])
            S.op("dve", lambda e, n=n: e.reciprocal(out=rstd[:, :n], in_=rstd[:, :n]), reads=[rstd], writes=[rstd])
            for k in range(nk):
                S.op("dve", lambda e, k=k, s=s, n=n: e.scalar_tensor_tensor(
                    out=dst[k][:, s:s + n], in0=src[k][:, s:s + n], scalar=gain[:, gcol0 + k:gcol0 + k + 1],
                    in1=rstd[:, :n], op0=ALU.mult, op1=ALU.mult), reads=[src[k], gain, rstd], writes=[dst[k]])
        A.free(sq[0], sq[1], rstd)

    def ffn(self, widx, hT, xT):
        S, A = self.S, self.A
        wgv = self.w_gate[widx].rearrange("(k p) n -> p k n", p=128)
        wuv = self.w_up[widx].rearrange("(k p) n -> p k n", p=128)
        wdv = self.w_down[widx]
        NFC = FF // 128
        NPAIR = NFC // 2
        GP = 2
        gu = [[A.alloc(f"gu{m}{sl}", KC * 256, BF16) for sl in range(2)] for m in range(2)]
        dn = [A.alloc(f"dn{sl}", D, BF16) for sl in range(6)]
        act = [A.alloc(f"act{c}", NTOK, BF16) for c in range(4)]
        sg = [A.alloc(f"sg{i}", 512, BF16) for i in range(2)]

        def load_gu(j):
            sl = j % 2
            for m, wv in ((0, wgv), (1, wuv)):
                b = gu[m][sl]
                self.dma("pool", b[:, :].rearrange("p (k n) -> p k n", k=KC), wv[:, :, j * 256:(j + 1) * 256], writes=[b])

        def load_dn(f):
            b = dn[f % 6]
            self.dma("pool", b[:, :], wdv[f * 128:(f + 1) * 128, :], writes=[b])

        load_gu(0)
        sgi = 0
        for j in range(NPAIR):
            if j + 1 < NPAIR:
                load_gu(j + 1)
            g0 = (j // GP) * GP
            gend = min(g0 + GP, NPAIR)
            if j == g0:
                for jj in range(g0, gend):
                    load_dn(2 * jj)
                    load_dn(2 * jj + 1)
            for c2 in range(2):
                f = 2 * j + c2
                ci = f % 4
                for (s, n) in TB:
                    pg = self.nextps()
                    pu = self.nextps()
                    for m, ps in ((0, pg), (1, pu)):
                        w = gu[m][j % 2]
                        for k in range(KC):
                            c0 = k * 256 + c2 * 128
                            self.mm(ps, ps[:, :n], w, w[:, c0:c0 + 128], hT[k], hT[k][:, s:s + n], k == 0, k == KC - 1)
                    t = sg[sgi % 2]
                    sgi += 1
                    S.op("act", lambda e, t=t, pg=pg, n=n: e.activation(out=t[:, :n], in_=pg[:, :n], func=AF.Silu),
                         reads=[pg], writes=[t])
                    S.op("dve", lambda e, t=t, pu=pu, ci=ci, s=s, n=n: e.tensor_tensor(
                        out=act[ci][:, s:s + n], in0=t[:, :n], in1=pu[:, :n], op=ALU.mult), reads=[t, pu], writes=[act[ci]])
            if j == gend - 1:
                fs = list(range(2 * g0, 2 * j + 2))
                for dc in range(KC):
                    for (s, n) in TB:
                        ps = self.nextps()
                        for i, f in enumerate(fs):
                            self.mm(ps, ps[:, :n], dn[f % 6], dn[f % 6][:, dc * 128:(dc + 1) * 128],
                                    act[f % 4], act[f % 4][:, s:s + n], i == 0, i == len(fs) - 1)
                        S.op("dve", lambda e, ps=ps, dc=dc, s=s, n=n: e.scalar_tensor_tensor(
                            out=xT[dc][:, s:s + n], in0=ps[:, :n], scalar=0.5, in1=xT[dc][:, s:s + n],
                            op0=ALU.mult, op1=ALU.add), reads=[ps, xT[dc]], writes=[xT[dc]])
        for m in range(2):
            A.free(*gu[m])
        A.free(*dn)
        A.free(*act)
        A.free(*sg)

    def build(self):
        nc, S, A = self.nc, self.S, self.A
        xT_d = self.inp("xT", [128, KC * NTOK])
        pv_d = self.inp("pv", [128, PV_N])
        mode = self.mode
        if mode == "full":
            self.w_gate = self.inp("ffn_w_gate", [4, D, FF])
            self.w_up = self.inp("ffn_w_up", [4, D, FF])
            self.w_down = self.inp("ffn_w_down", [4, FF, D])
        o_yT = self.outp("o_yT", [128, KC * NTOK])
        if mode in ("full", "l0mix"):
            w_in = self.inp("ab_w_in", [D, AB_IN])
            kvn_d = self.inp("kvn", [128, KVL])
            rope_tm_d = self.inp("rope_tm", [128, 9 * 128])
            sc_hist_d = self.inp("sc_hist", [128, 8 * 32])
            o_ckv = self.outp("o_ckv", [NTOK, KVL])
            o_kr = self.outp("o_kr", [NTOK, ROPE])
            o_sc = self.outp("o_sc", [128, 8 * 34])
        if mode in ("full", "l1mix"):
            o_ssc = self.outp("o_ssc", [128, 48 * 51])
            o_pssm = self.outp("o_pssm", [128, 32 * 128])
            o_sssm = self.outp("o_sssm", [16 * 128, 32 * 128])

        xT = [A.alloc(f"xT{k}", NTOK, ckey="xT") for k in range(KC)]
        pv = A.alloc("pv", PV_N, ckey="once")
        self.ones = A.alloc("ones", 128, BF16)
        self.epsb = A.alloc("epsb", 1)
        xv = xT_d.rearrange("p (k t) -> p k t", k=KC)
        for k in range(KC):
            self.dma("sp", xT[k][:, :], xv[:, k, :], writes=[xT[k]], group=(k > 0))
        self.dma("sp", pv[:, :], pv_d[:, :], writes=[pv])
        S.op("pool", lambda e: e.memset(self.ones[:, :], 1.0), writes=[self.ones])
        S.op("pool", lambda e: e.memset(self.epsb[:, :], EPS), writes=[self.epsb])

        hT = [A.alloc(f"hT{k}", NTOK, BF16) for k in range(KC)]
        full = self.mode == "full"
        cv = self.cv = A.alloc("cv", 32, ckey="once")
        self.dma("sp", cv[:, :], self.inp("cv", [128, 32])[:, :], writes=[cv])

        if full:
            self.rmsnorm_fm(xT, hT, pv, PV_NORM_FFN + 0 * 16, D, TB)
            self.ffn(0, hT, xT)
        if self.mode in ("full", "l0mix"):
            self.rmsnorm_fm(xT, hT, pv, PV_NORM_MIX + 0, D, TB)
            self.spill_x(xT)
            self.merged = [None] * 16
            st = self.stop
            if st >= 1:
                self.l0_inproj(hT, xT, pv, w_in, kvn_d, rope_tm_d, sc_hist_d, o_ckv, o_kr, o_sc)
            if st >= 2:
                self.l0_latents_fm(hT, pv, w_in, self.inp("rope_fm", [128, 2 * NTOK]))
            A.free(*hT)
            if st >= 3:
                self.l0_exchange()
            if st >= 4:
                w_uq_d = self.inp("mla_w_uq", [QL, 1536])
                w_uk_d = self.inp("mla_w_uk", [KVL, 1024])
                w_uv_d = self.inp("mla_w_uv", [KVL, 1024])
                self.l0_prefill_attn(cv, w_uq_d, w_uk_d, w_uv_d, self.inp("cmask", [128, 2048]))
            if st >= 5:
                self.l0_decode_attn(cv, self.inp("cache_ckv", [self.npool * 16, 4096]), self.inp("cache_krope", [self.npool * 16, 512]),
                                    self.inp("ptx", [128, 128], I32), self.inp("w_ukT", [1024, KVL]), w_uv_d,
                                    self.inp("dmask", [128, 1024]), self.inp("ident", [128, 128]))
            if st >= 6:
                self.l0_sconv_patch(pv, cv)
            if st >= 6:
                xT = self.reload_x()
                self.out_proj(self.inp("ab_w_out", [D, D]), 16, xT)
                A.free(*self.merged)
            hT = [A.alloc(f"hT{k}", NTOK, BF16) for k in range(KC)]
        if full:
            self.rmsnorm_fm(xT, hT, pv, PV_NORM_FFN + 1 * 16, D, TB)
            self.ffn(1, hT, xT)
            self.rmsnorm_fm(xT, hT, pv, PV_NORM_FFN + 2 * 16, D, TB)
            self.ffn(2, hT, xT)
        if self.mode in ("full", "l1mix"):
            self.rmsnorm_fm(xT, hT, pv, PV_NORM_MIX + 16, D, TB)
            self.ssd_mixer(hT, xT, o_ssc, o_pssm, o_sssm)
        if full:
            self.rmsnorm_fm(xT, hT, pv, PV_NORM_FFN + 3 * 16, D, TB)
            self.ffn(3, hT, xT)
            self.rmsnorm_fm(xT, xT, pv, PV_NORM_FIN, D, TB)

        yv = o_yT.rearrange("p (k t) -> p k t", k=KC)
        for k in range(KC):
            if self.stop < 6:
                break
            self.dma("sp", yv[:, k, :], xT[k][:, :], reads=[xT[k]])
        print("nops", S.nops, {e: len(v) for e, v in S.ops.items()}, "arena used", A.used(), "chans", len(S.chans))
        S.emit()
        return nc

    def l0_inproj(self, hT, xT, pv, w_in, kvn_d, rope_tm_d, sc_hist_d, o_ckv, o_kr, o_sc):
        S, A = self.S, self.A
        wv = w_in.rearrange("(k p) n -> p k n", p=128)
        wkv = A.alloc("wkv", KC * KVL, BF16)
        wkr = A.alloc("wkr", KC * 128, BF16)
        kvn = A.alloc("kvn", KVL, ckey="once")
        ropet = A.alloc("ropet", 9 * 128, ckey="once")
        self.dma("pool", wkv[:, :].rearrange("p (k n) -> p k n", k=KC), wv[:, :, O1:O2], writes=[wkv])
        wkr3 = wkr[:, :].rearrange("p (k n) -> p k n", k=KC)
        self.dma("pool", wkr3[:, :, 0:64], wv[:, :, O2:O3], writes=[wkr])
        self.dma("pool", wkr3[:, :, 64:96], wv[:, :, O2 + 32:O3], writes=[wkr], group=True)
        self.dma("pool", wkr3[:, :, 96:128], wv[:, :, O2:O2 + 32], writes=[wkr], group=True)
        self.dma("sp", kvn[:, :], kvn_d[:, :], writes=[kvn])
        self.dma("sp", ropet[:, :], rope_tm_d[:, :], writes=[ropet])
        ckv_o = [A.alloc(f"ckvo{i}", KVL, ckey="ost") for i in range(2)]
        kr_o = [A.alloc(f"kro{i}", ROPE, ckey="ost") for i in range(2)]
        junk = A.alloc("junk", KVL, BF16)
        ssq = A.alloc("ssq", 2)
        t64 = A.alloc("t64", 64)
        for ti in range(9):
            c0 = ti * 128
            pk = self.nextps()
            pr = self.nextps()
            for k in range(KC):
                self.mm(pk, pk[:, :KVL], hT[k], hT[k][:, c0:c0 + 128], wkv, wkv[:, k * KVL:(k + 1) * KVL], k == 0, k == KC - 1)
            for k in range(KC):
                self.mm(pr, pr[:, :128], hT[k], hT[k][:, c0:c0 + 128], wkr, wkr[:, k * 128:(k + 1) * 128], k == 0, k == KC - 1)
            co = ckv_o[ti % 2]
            ko = kr_o[ti % 2]
            S.op("act", lambda e, pk=pk: e.activation(out=junk[:, :], in_=pk[:, :KVL], func=AF.Square, accum_out=ssq[:, 0:1]),
                 reads=[pk], writes=[junk, ssq])
            S.op("act", lambda e: e.activation(out=ssq[:, 1:2], in_=ssq[:, 0:1], func=AF.Sqrt, bias=self.epsb[:, :], scale=1.0 / KVL),
                 reads=[ssq, self.epsb], writes=[ssq])
            S.op("dve", lambda e: e.reciprocal(out=ssq[:, 1:2], in_=ssq[:, 1:2]), reads=[ssq], writes=[ssq])
            S.op("dve", lambda e, pk=pk, co=co: e.scalar_tensor_tensor(out=co[:, :], in0=pk[:, :KVL], scalar=ssq[:, 1:2], in1=kvn[:, :],
                                                                       op0=ALU.mult, op1=ALU.mult), reads=[pk, ssq, kvn], writes=[co])
            self.dma("sp", o_ckv[c0:c0 + 128, :], co[:, :], reads=[co])
            if ti == 8:
                self.ckv_dec = A.alloc("ckv_dec", KVL, BF16)
                S.op("pool", lambda e, co=co: e.tensor_copy(out=self.ckv_dec[:, :], in_=co[:, :]), reads=[co], writes=[self.ckv_dec])
            S.op("dve", lambda e, pr=pr, ti=ti: e.tensor_tensor(out=t64[:, :], in0=pr[:, 64:128], in1=ropet[:, ti * 128 + 64:ti * 128 + 128], op=ALU.mult),
                 reads=[pr, ropet], writes=[t64])
            S.op("dve", lambda e, pr=pr, ti=ti, ko=ko: e.tensor_tensor(out=ko[:, :], in0=pr[:, 0:64], in1=ropet[:, ti * 128:ti * 128 + 64], op=ALU.mult),
                 reads=[pr, ropet], writes=[ko])
            S.op("dve", lambda e, ko=ko: e.tensor_tensor(out=ko[:, :], in0=ko[:, :], in1=t64[:, :], op=ALU.add),
                 reads=[ko, t64], writes=[ko])
            self.dma("sp", o_kr[c0:c0 + 128, :], ko[:, :], reads=[ko])
        A.free(wkv, wkr, kvn, ropet, junk, ssq, t64, *ckv_o, *kr_o)

        PVW = PV_SCW
        hist = A.alloc("schist", 8 * 32, ckey="once")
        self.dma("sp", hist[:, :], sc_hist_d[:, :], writes=[hist])
        osc = A.alloc("osc", 8 * 34, ckey="stage")
        wsl = [A.alloc(f"wsc{i}", KC * 128, BF16) for i in range(3)]
        uext = A.alloc("uext", 2 + NP_ + 16 * 10)
        gb = A.alloc("gbt", NTOK)
        acc = A.alloc("acct", NTOK)
        vt = A.alloc("vt", 512)
        self.merged = [None] * 16
        self.sc_sav = A.alloc("sc_sav", 32)
        for cch in range(8):
            for m, off in enumerate((O3, O4, O5)):
                b = wsl[m]
                self.dma("pool", b[:, :].rearrange("p (k n) -> p k n", k=KC), wv[:, :, off + cch * 128:off + (cch + 1) * 128], writes=[b])
            S.op("pool", lambda e: e.memset(uext[:, 0:2], 0.0), writes=[uext])
            udec = uext[:, 2 + NP_:].rearrange("p (b t) -> p b t", b=16)
            S.op("pool", lambda e, cch=cch, udec=udec: e.tensor_copy(out=udec[:, :, 0:2], in_=hist[:, cch * 32:(cch + 1) * 32].rearrange("p (b t) -> p b t", b=16)),
                 reads=[hist], writes=[uext])
            for (s, n) in TB:
                pb = self.nextps()
                pc = self.nextps()
                pvv = self.nextps()
                for m, ps in enumerate((pb, pc, pvv)):
                    for k in range(KC):
                        self.mm(ps, ps[:, :n], wsl[m], wsl[m][:, k * 128:(k + 1) * 128], hT[k], hT[k][:, s:s + n], k == 0, k == KC - 1)
                S.op("act", lambda e, pb=pb, s=s, n=n: e.activation(out=gb[:, s:s + n], in_=pb[:, :n], func=AF.Copy), reads=[pb], writes=[gb])
                S.op("act", lambda e, pvv=pvv, n=n: e.activation(out=vt[:, :n], in_=pvv[:, :n], func=AF.Copy), reads=[pvv], writes=[vt])
                if s < NP_:
                    S.op("dve", lambda e, pc=pc, s=s, n=n: e.tensor_tensor(out=uext[:, 2 + s:2 + s + n], in0=pc[:, :n], in1=vt[:, :n], op=ALU.mult),
                         reads=[pc, vt], writes=[uext])
                else:
                    S.op("dve", lambda e, pc=pc, n=n, udec=udec: e.tensor_tensor(
                        out=udec[:, :, 2:10], in0=pc[:, :n].rearrange("p (b t) -> p b t", b=16),
                        in1=vt[:, :n].rearrange("p (b t) -> p b t", b=16), op=ALU.mult), reads=[pc, vt], writes=[uext])
            w0 = pv[:, PVW + cch * 3 + 0:PVW + cch * 3 + 1]
            w1 = pv[:, PVW + cch * 3 + 1:PVW + cch * 3 + 2]
            w2 = pv[:, PVW + cch * 3 + 2:PVW + cch * 3 + 3]
            S.op("dve", lambda e, w2=w2: e.tensor_scalar(out=acc[:, 0:NP_], in0=uext[:, 2:2 + NP_], scalar1=w2, scalar2=None, op0=ALU.mult),
                 reads=[uext, pv], writes=[acc])
            S.op("dve", lambda e, w1=w1: e.scalar_tensor_tensor(out=acc[:, 0:NP_], in0=uext[:, 1:1 + NP_], scalar=w1, in1=acc[:, 0:NP_], op0=ALU.mult, op1=ALU.add),
                 reads=[uext, pv, acc], writes=[acc])
            S.op("dve", lambda e, w0=w0: e.scalar_tensor_tensor(out=acc[:, 0:NP_], in0=uext[:, 0:NP_], scalar=w0, in1=acc[:, 0:NP_], op0=ALU.mult, op1=ALU.add),
                 reads=[uext, pv, acc], writes=[acc])
            accd = acc[:, NP_:].rearrange("p (b t) -> p b t", b=16)
            S.op("dve", lambda e, w2=w2, udec=udec, accd=accd: e.tensor_scalar(out=accd, in0=udec[:, :, 2:10], scalar1=w2, scalar2=None, op0=ALU.mult),
                 reads=[uext, pv], writes=[acc])
            S.op("dve", lambda e, w1=w1, udec=udec, accd=accd: e.scalar_tensor_tensor(out=accd, in0=udec[:, :, 1:9], scalar=w1, in1=accd, op0=ALU.mult, op1=ALU.add),
                 reads=[uext, pv, acc], writes=[acc])
            S.op("dve", lambda e, w0=w0, udec=udec, accd=accd: e.scalar_tensor_tensor(out=accd, in0=udec[:, :, 0:8], scalar=w0, in1=accd, op0=ALU.mult, op1=ALU.add),
                 reads=[uext, pv, acc], writes=[acc])
            S.op("pool", lambda e, cch=cch: e.tensor_copy(out=osc[:, cch * 34:cch * 34 + 2], in_=uext[:, NP_:NP_ + 2]), reads=[uext], writes=[osc])
            S.op("pool", lambda e, cch=cch, udec=udec: e.tensor_copy(out=osc[:, cch * 34 + 2:cch * 34 + 34].rearrange("p (b t) -> p b t", b=16), in_=udec[:, :, 8:10]),
                 reads=[uext], writes=[osc])
            S.op("pool", lambda e, cch=cch: e.tensor_copy(out=self.sc_sav[:, cch * 4:cch * 4 + 2], in_=acc[:, 0:2]), reads=[acc], writes=[self.sc_sav])
            S.op("pool", lambda e, cch=cch: e.tensor_copy(out=self.sc_sav[:, cch * 4 + 2:cch * 4 + 4], in_=gb[:, 0:2]), reads=[gb], writes=[self.sc_sav])
            mg = A.alloc(f"mg{8 + cch}", NTOK, BF16)
            self.merged[8 + cch] = mg
            S.op("dve", lambda e, mg=mg: e.tensor_tensor(out=mg[:, :], in0=acc[:, :], in1=gb[:, :], op=ALU.mult), reads=[acc, gb], writes=[mg])
        self.dma("sp", o_sc[:, :], osc[:, :], reads=[osc])
        A.free(hist, uext, gb, acc, vt, *wsl)
        self.osc = osc


    def spill_x(self, xT):
        if not hasattr(self, "xsp_t"):
            self.xsp_t = self.nc.dram_tensor("x_sp", [128, KC * NTOK], F32)
            self.xsp = Buf("xsp", None)
        v = self.xsp_t.ap().rearrange("p (k t) -> p k t", k=KC)
        for k in range(KC):
            self.dma("sp", v[:, k, :], xT[k][:, :], reads=[xT[k]], writes=[self.xsp], chan_buf=xT[k], group=True)
        self.A.free(*xT)

    def reload_x(self):
        v = self.xsp_t.ap().rearrange("p (k t) -> p k t", k=KC)
        xT = [self.A.alloc(f"xT{k}", NTOK, ckey="xT") for k in range(KC)]
        for k in range(KC):
            self.dma("sp", xT[k][:, :], v[:, k, :], reads=[self.xsp], writes=[xT[k]], chan_buf=xT[k], group=(k > 0))
        return xT

    def copy_evac(self, i, out_ap, in_ap, reads, writes):
        if i % 2 == 0:
            self.S.op("act", lambda e: e.activation(out=out_ap, in_=in_ap, func=AF.Copy), reads=reads, writes=writes)
        else:
            self.S.op("dve", lambda e: e.tensor_copy(out=out_ap, in_=in_ap), reads=reads, writes=writes)

    def l0_latents_fm(self, hT, pv, w_in, rope_fm_d):
        S, A = self.S, self.A
        wv = w_in.rearrange("(k p) n -> p k n", p=128)
        wkv = A.alloc("wkv2", KC * KVL, BF16)
        wkr = A.alloc("wkr2", KC * 128, BF16)
        self.dma("pool", wkv[:, :].rearrange("p (k n) -> p k n", k=KC), wv[:, :, O1:O2], writes=[wkv])
        wkr3 = wkr[:, :].rearrange("p (k n) -> p k n", k=KC)
        self.dma("pool", wkr3[:, :, 0:64], wv[:, :, O2:O3], writes=[wkr])
        self.dma("pool", wkr3[:, :, 64:96], wv[:, :, O2 + 32:O3], writes=[wkr], group=True)
        self.dma("pool", wkr3[:, :, 96:128], wv[:, :, O2:O2 + 32], writes=[wkr], group=True)
        self.rope_fm = A.alloc("rope_fm", 2 * NTOK, ckey="once")
        self.dma("sp", self.rope_fm[:, :], rope_fm_d[:, :], writes=[self.rope_fm])
        raw = [A.alloc(f"ckvraw{cc}", NTOK) for cc in range(4)]
        ei = 0
        for cc in range(4):
            for (s, n) in TB:
                ps = self.nextps()
                for k in range(KC):
                    c0 = k * KVL + cc * 128
                    self.mm(ps, ps[:, :n], wkv, wkv[:, c0:c0 + 128], hT[k], hT[k][:, s:s + n], k == 0, k == KC - 1)
                self.copy_evac(ei, raw[cc][:, s:s + n], ps[:, :n], [ps], [raw[cc]])
                ei += 1
        self.ckvT = [A.alloc(f"ckvT{cc}", NTOK, BF16, ckey="stage") for cc in range(4)]
        self.rmsnorm_fm(raw, self.ckvT, pv, PV_KVN, KVL, TB)
        A.free(*raw)
        self.kropeT = A.alloc("kropeT", NTOK, BF16, ckey="stage")
        t1 = A.alloc("kr_t1", 512)
        t2 = A.alloc("kr_t2", 512)
        rf = self.rope_fm
        for (s, n) in TB:
            p1 = self.nextps()
            p2 = self.nextps()
            for k in range(KC):
                self.mm(p1, p1[:64, :n], wkr, wkr[:, k * 128:k * 128 + 64], hT[k], hT[k][:, s:s + n], k == 0, k == KC - 1)
            for k in range(KC):
                self.mm(p2, p2[:64, :n], wkr, wkr[:, k * 128 + 64:k * 128 + 128], hT[k], hT[k][:, s:s + n], k == 0, k == KC - 1)
            S.op("dve", lambda e, p2=p2, s=s, n=n: e.tensor_tensor(out=t1[:64, :n], in0=p2[:64, :n], in1=rf[:64, NTOK + s:NTOK + s + n], op=ALU.mult),
                 reads=[p2, rf], writes=[t1])
            S.op("dve", lambda e, p1=p1, s=s, n=n: e.tensor_tensor(out=t2[:64, :n], in0=p1[:64, :n], in1=rf[:64, s:s + n], op=ALU.mult),
                 reads=[p1, rf], writes=[t2])
            S.op("dve", lambda e, s=s, n=n: e.tensor_tensor(out=self.kropeT[:64, s:s + n], in0=t1[:64, :n], in1=t2[:64, :n], op=ALU.add),
                 reads=[t1, t2], writes=[self.kropeT])
        A.free(t1, t2, wkv, wkr)
        wq = [A.alloc(f"wcq{i}", KC * 256, BF16) for i in range(2)]
        raw = [A.alloc(f"cqraw{i}", NTOK) for i in range(6)]
        for j in range(3):
            b = wq[j % 2]
            self.dma("pool", b[:, :].rearrange("p (k n) -> p k n", k=KC), wv[:, :, j * 256:(j + 1) * 256], writes=[b])
            for c2 in range(2):
                ch = 2 * j + c2
                for (s, n) in TB:
                    ps = self.nextps()
                    for k in range(KC):
                        c0 = k * 256 + c2 * 128
                        self.mm(ps, ps[:, :n], b, b[:, c0:c0 + 128], hT[k], hT[k][:, s:s + n], k == 0, k == KC - 1)
                    self.copy_evac(ei, raw[ch][:, s:s + n], ps[:, :n], [ps], [raw[ch]])
                    ei += 1
        self.cqn = [A.alloc(f"cqn{i}", NTOK, BF16) for i in range(6)]
        self.rmsnorm_fm(raw, self.cqn, pv, PV_QNORM, QL, TB)
        A.free(*raw)
        A.free(*wq)

    def allgather(self, name, width, fill):
        nc = self.nc
        tin = nc.dram_tensor(name + "_in", [128, width], F32)
        tout = nc.dram_tensor(name + "_out", [4 * 128, width], F32)
        exi = Buf(name + "_in", None)
        exo = Buf(name + "_out", None, ckey="cc")
        fill(tin.ap(), exi)
        self.S.op("pool", lambda e: e.collective_compute("AllGather", ALU.bypass, replica_groups=[[0, 1, 2, 3], [4, 5, 6, 7]],
                                                         ins=[tin.ap().opt()], outs=[tout.ap().opt()]),
                  reads=[exi], writes=[exo], dma=True, chan_buf=exo, inc=1)
        return tout.ap(), exo

    def l0_exchange(self):
        S, A, nc = self.S, self.A, self.nc

        def fill_kv(c0):
            def f(ein, exi):
                for i in range(2):
                    cc = c0 + i
                    dst = ein[:, i * 512:(i + 1) * 512].bitcast(BF16)
                    self.dma("sp", dst, self.ckvT[cc][:, 0:NP_], reads=[self.ckvT[cc]], writes=[exi], chan_buf=self.ckvT[cc], group=True)
            return f

        def fill_c(ein, exi):
            dst = ein[0:64, 0:512].bitcast(BF16)
            self.dma("sp", dst, self.kropeT[0:64, 0:NP_], reads=[self.kropeT], writes=[exi], chan_buf=self.kropeT, group=True)
            osc3 = self.osc[:, :].rearrange("p (c t) -> p c t", c=8)
            self.dma_nc("sp", ein[:, 512:528].rearrange("p (c t) -> p c t", c=8), osc3[:, :, 0:2], reads=[self.osc], writes=[exi], chan_buf=self.osc, group=True)

        eoA, exoA = self.allgather("ex0a", 1024, fill_kv(0))
        eoB, exoB = self.allgather("ex0b", 1024, fill_kv(2))
        eoC, exoC = self.allgather("ex0c", 528, fill_c)
        self.ckvT_all = [[A.alloc(f"ckvA{r}_{cc}", NP_, BF16, ckey="exl") for cc in range(4)] for r in range(4)]
        self.kropeT_all = [A.alloc(f"krA{r}", NP_, BF16, ckey="exl") for r in range(4)]
        self.uh_all = A.alloc("uh_all", 64, ckey="exl")
        for r in range(4):
            for cc in range(4):
                b = self.ckvT_all[r][cc]
                eo, exo = (eoA, exoA) if cc < 2 else (eoB, exoB)
                i = cc % 2
                self.dma("sp", b[:, :], eo[r * 128:(r + 1) * 128, i * 512:(i + 1) * 512].bitcast(BF16), reads=[exo], writes=[b], chan_buf=b)
            b = self.kropeT_all[r]
            self.dma("sp", b[0:64, :], eoC[r * 128:r * 128 + 64, 0:512].bitcast(BF16), reads=[exoC], writes=[b], chan_buf=b)
            self.dma("sp", self.uh_all[:, r * 16:(r + 1) * 16], eoC[r * 128:(r + 1) * 128, 512:528], reads=[exoC], writes=[self.uh_all],
                     chan_buf=self.uh_all, group=(r > 0))

    def l0_prefill_attn(self, cv, w_uq_d, w_uk_d, w_uv_d, cmask_d):
        S, A = self.S, self.A
        self.rot = [0, 1, 2, 3]
        PO = [self.PS[4], self.PS[6]]
        PL = [self.PS[5], self.PS[7]]
        cmask = A.alloc("cmask", 4 * 512, BF16, ckey="oncep")
        self.dma("pool", cmask[:, :], cmask_d[:, :], writes=[cmask])
        wq = [A.alloc(f"wqh{i}", 6 * 256, BF16) for i in range(2)]
        wuk = [A.alloc(f"wuk{i}", 4 * 128, BF16) for i in range(2)]
        wuv = [A.alloc(f"wuv{i}", 4 * 128, BF16) for i in range(2)]
        qn = A.alloc("qn_h", NTOK, BF16)
        qr = A.alloc("qr_h", NTOK, BF16)
        tq1 = A.alloc("tq1", 512)
        tq2 = A.alloc("tq2", 512)
        KT = A.alloc("KT_h", 4096, BF16)
        V = A.alloc("V_h", 4096, BF16)
        pT = [A.alloc(f"pT{i}", 512, BF16) for i in range(3)]
        Mt = [A.alloc(f"Mt{i}", 512, BF16) for i in range(2)]
        rl = A.alloc("rl", 512)
        self.qn_dec = A.alloc("qn_dec", 8 * 128, BF16)
        self.qr_dec = A.alloc("qr_dec", 8 * 128, BF16)
        self.merged[0:8] = [A.alloc(f"mg{h}", NTOK, BF16) for h in range(8)]
        wqv = w_uq_d.rearrange("(k p) n -> p k n", p=128)
        wukv = w_uk_d.rearrange("(c p) n -> p c n", p=128)
        wuvv = w_uv_d.rearrange("(c p) n -> p c n", p=128)
        rf = self.rope_fm
        pti = 0
        mti = 0
        ei = 0
        for h in range(8):
            w = wq[h % 2]
            w3 = w[:, :].rearrange("p (k n) -> p k n", k=6)
            b0 = h * 192
            self.dma("pool", w3[:, :, 0:192], wqv[:, :, b0:b0 + 192], writes=[w])
            self.dma("pool", w3[:, :, 192:224], wqv[:, :, b0 + 160:b0 + 192], writes=[w], group=True)
            self.dma("pool", w3[:, :, 224:256], wqv[:, :, b0 + 128:b0 + 160], writes=[w], group=True)
            uk = wuk[h % 2]
            uv = wuv[h % 2]
            self.dma("pool", uk[:, :].rearrange("p (c n) -> p c n", c=4), wukv[:, :, h * 128:(h + 1) * 128], writes=[uk])
            self.dma("pool", uv[:, :].rearrange("p (c n) -> p c n", c=4), wuvv[:, :, h * 128:(h + 1) * 128], writes=[uv])
            for (s, n) in TB:
                pn = self.nextps()
                p1 = self.nextps()
                p2 = self.nextps()
                for k in range(6):
                    self.mm(pn, pn[:, :n], w, w[:, k * 256:k * 256 + 128], self.cqn[k], self.cqn[k][:, s:s + n], k == 0, k == 5)
                for k in range(6):
                    self.mm(p1, p1[:64, :n], w, w[:, k * 256 + 128:k * 256 + 192], self.cqn[k], self.cqn[k][:, s:s + n], k == 0, k == 5)
                for k in range(6):
                    self.mm(p2, p2[:64, :n], w, w[:, k * 256 + 192:k * 256 + 256], self.cqn[k], self.cqn[k][:, s:s + n], k == 0, k == 5)
                S.op("act", lambda e, pn=pn, s=s, n=n: e.activation(out=qn[:, s:s + n], in_=pn[:, :n], func=AF.Copy), reads=[pn], writes=[qn])
                S.op("dve", lambda e, p2=p2, s=s, n=n: e.tensor_tensor(out=tq1[:64, :n], in0=p2[:64, :n], in1=rf[:64, NTOK + s:NTOK + s + n], op=ALU.mult),
                     reads=[p2, rf], writes=[tq1])
                S.op("dve", lambda e, p1=p1, s=s, n=n: e.tensor_tensor(out=tq2[:64, :n], in0=p1[:64, :n], in1=rf[:64, s:s + n], op=ALU.mult),
                     reads=[p1, rf], writes=[tq2])
                S.op("dve", lambda e, s=s, n=n: e.tensor_tensor(out=qr[:64, s:s + n], in0=tq1[:64, :n], in1=tq2[:64, :n], op=ALU.add),
                     reads=[tq1, tq2], writes=[qr])
            S.op("pool", lambda e, h=h: e.tensor_copy(out=self.qn_dec[:, h * 128:(h + 1) * 128], in_=qn[:, NP_:NTOK]), reads=[qn], writes=[self.qn_dec])
            S.op("pool", lambda e, h=h: e.tensor_copy(out=self.qr_dec[:64, h * 128:(h + 1) * 128], in_=qr[:64, NP_:NTOK]), reads=[qr], writes=[self.qr_dec])
            for r in range(4):
                for half in range(2):
                    ps = self.nextps()
                    for cc in range(4):
                        a = self.ckvT_all[r][cc]
                        self.mm(ps, ps[:, :512], uk, uk[:, cc * 128:(cc + 1) * 128], a, a[:, half * 512:(half + 1) * 512], cc == 0, cc == 3)
                    o0 = r * 1024 + half * 512
                    self.copy_evac(ei, KT[:, o0:o0 + 512], ps[:, :512], [ps], [KT])
                    ei += 1
                for q4 in range(2):
                    ps = self.nextps()
                    for sl in range(4):
                        st = q4 * 4 + sl
                        for cc in range(4):
                            a = self.ckvT_all[r][cc]
                            self.mm(ps, ps[:, sl * 128:(sl + 1) * 128], a, a[:, st * 128:(st + 1) * 128], uv, uv[:, cc * 128:(cc + 1) * 128], cc == 0, cc == 3)
                    o0 = (r * 8 + q4 * 4) * 128
                    self.copy_evac(ei, V[:, o0:o0 + 512], ps[:, :512], [ps], [V])
                    ei += 1
            for qb in range(2):
                q0 = qb * 512
                po, pl = PO[qb], PL[qb]
                first = True
                for kb in range(4):
                    for st in range(8):
                        rel = st - 4 * qb
                        ps = self.nextps()
                        kc0 = (kb * 8 + st) * 128
                        self.mm(ps, ps[:, :512], KT, KT[:, kc0:kc0 + 128], qn, qn[:, q0:q0 + 512], True, False)
                        ka = self.kropeT_all[kb]
                        self.mm(ps, ps[:, :512], ka, ka[:64, st * 128:(st + 1) * 128], qr, qr[:64, q0:q0 + 512], False, True)
                        bcol = (4 + kb) if rel > 3 else kb
                        p = pT[pti % 3]
                        pti += 1
                        S.op("act", lambda e, p=p, ps=ps, bcol=bcol: e.activation(out=p[:, :], in_=ps[:, :512], func=AF.Exp, bias=cv[:, bcol:bcol + 1], scale=SCALE),
                             reads=[ps, cv], writes=[p])
                        if 0 <= rel <= 3:
                            m = Mt[mti % 2]
                            mti += 1
                            S.op("pool", lambda e, m=m, rel=rel, kb=kb: e.tensor_scalar(out=m[:, :], in0=cmask[:, rel * 512:(rel + 1) * 512], scalar1=cv[:, 8 + kb:9 + kb],
                                                                                      scalar2=cv[:, 12 + kb:13 + kb], op0=ALU.mult, op1=ALU.add), reads=[cmask, cv], writes=[m])
                            S.op("dve", lambda e, p=p, m=m: e.tensor_tensor(out=p[:, :], in0=p[:, :], in1=m[:, :], op=ALU.mult), reads=[p, m], writes=[p])
                        last = (kb == 3 and st == 7)
                        self.mm(po, po[:, :512], V, V[:, kc0:kc0 + 128], p, p[:, :], first, last)
                        self.mm(pl, pl[:, :512], self.ones, self.ones[:, :], p, p[:, :], first, last)
                        first = False
                S.op("dve", lambda e, pl=pl: e.reciprocal(out=rl[:, :], in_=pl[:, :512]), reads=[pl], writes=[rl])
                mg = self.merged[h]
                S.op("dve", lambda e, po=po, mg=mg, q0=q0: e.tensor_tensor(out=mg[:, q0:q0 + 512], in0=po[:, :512], in1=rl[:, :], op=ALU.mult),
                     reads=[po, rl], writes=[mg])
        self.rot = None
        A.free(cmask, qn, qr, tq1, tq2, KT, V, rl, *wq, *wuk, *wuv, *pT, *Mt)
        for r in range(4):
            A.free(*self.ckvT_all[r])
        A.free(*self.kropeT_all)

    def l0_sconv_patch(self, pv, cv):
        S, A = self.S, self.A
        uh = A.alloc("uh", 16)
        S.op("dve", lambda e: e.tensor_scalar(out=uh[:, :], in0=self.uh_all[:, 0:16], scalar1=cv[:, 17:18], scalar2=None, op0=ALU.mult),
             reads=[self.uh_all, cv], writes=[uh])
        for r in range(1, 4):
            S.op("dve", lambda e, r=r: e.scalar_tensor_tensor(out=uh[:, :], in0=self.uh_all[:, r * 16:(r + 1) * 16], scalar=cv[:, 17 + r:18 + r], in1=uh[:, :],
                                                              op0=ALU.mult, op1=ALU.add), reads=[self.uh_all, cv, uh], writes=[uh])
        uh3 = uh[:, :].rearrange("p (c t) -> p c t", c=8)
        w3 = pv[:, PV_SCW:PV_SCW + 24].rearrange("p (c k) -> p c k", k=3)
        sv = self.sc_sav[:, :].rearrange("p (c t) -> p c t", c=8)
        fx = A.alloc("scfix", 32)
        f3 = fx[:, :].rearrange("p (c t) -> p c t", c=8)
        S.op("dve", lambda e: e.tensor_tensor(out=f3[:, :, 0], in0=w3[:, :, 0], in1=uh3[:, :, 0], op=ALU.mult), reads=[pv, uh], writes=[fx])
        S.op("dve", lambda e: e.tensor_tensor(out=f3[:, :, 1], in0=w3[:, :, 1], in1=uh3[:, :, 1], op=ALU.mult), reads=[pv, uh, fx], writes=[fx])
        S.op("dve", lambda e: e.tensor_tensor(out=f3[:, :, 0], in0=f3[:, :, 0], in1=f3[:, :, 1], op=ALU.add), reads=[fx], writes=[fx])
        S.op("dve", lambda e: e.tensor_tensor(out=f3[:, :, 1], in0=w3[:, :, 0], in1=uh3[:, :, 1], op=ALU.mult), reads=[pv, uh, fx], writes=[fx])
        S.op("dve", lambda e: e.tensor_tensor(out=f3[:, :, 0:2], in0=f3[:, :, 0:2], in1=sv[:, :, 0:2], op=ALU.add), reads=[fx, self.sc_sav], writes=[fx])
        S.op("dve", lambda e: e.tensor_tensor(out=f3[:, :, 2:4], in0=f3[:, :, 0:2], in1=sv[:, :, 2:4], op=ALU.mult), reads=[fx, self.sc_sav], writes=[fx])
        for cch in range(8):
            mg = self.merged[8 + cch]
            S.op("dve", lambda e, mg=mg, cch=cch: e.tensor_copy(out=mg[:, 0:2], in_=f3[:, cch, 2:4]), reads=[fx], writes=[mg])
        A.free(uh, fx, self.uh_all, self.sc_sav)

    def out_proj(self, w_out_d, nmc, xT):
        S, A = self.S, self.A
        wv = w_out_d.rearrange("(m p) n -> p m n", p=128)
        wo = [A.alloc(f"wo{i}", nmc * 256, BF16) for i in range(2)]
        for dcp in range(8):
            b = wo[dcp % 2]
            self.dma("pool", b[:, :].rearrange("p (m n) -> p m n", m=nmc), wv[:, :, dcp * 256:(dcp + 1) * 256], writes=[b])
            for c2 in range(2):
                dc = 2 * dcp + c2
                for (s, n) in TB:
                    ps = self.nextps()
                    for mc in range(nmc):
                        c0 = mc * 256 + c2 * 128
                        self.mm(ps, ps[:, :n], b, b[:, c0:c0 + 128], self.merged[mc], self.merged[mc][:, s:s + n], mc == 0, mc == nmc - 1)
                    S.op("dve", lambda e, ps=ps, dc=dc, s=s, n=n: e.tensor_tensor(out=xT[dc][:, s:s + n], in0=ps[:, :n], in1=xT[dc][:, s:s + n], op=ALU.add),
                         reads=[ps, xT[dc]], writes=[xT[dc]])
        A.free(*wo)

    def l0_decode_attn(self, cv, cache_ckv_d, cache_kr_d, ptx_d, w_ukT_d, w_uv_d, dmask_d, ident_d):
        S, A = self.S, self.A
        DS = 9
        identb = A.alloc("identb", 128, BF16, ckey="oncep")
        self.dma("pool", identb[:, :], ident_d[:, :], writes=[identb])
        onesf = A.alloc("onesf", 1)
        S.op("pool", lambda e: e.memset(onesf[:, :], 1.0), writes=[onesf])
        dmask = A.alloc("dmask", 16 * 64, ckey="once")
        self.dma("sp", dmask[:, :], dmask_d[:, :], writes=[dmask])
        wukT = A.alloc("wukT", 8 * 512, BF16)
        self.dma("pool", wukT[:, :].rearrange("p (h c) -> p h c", h=8), w_ukT_d.rearrange("(h n) c -> n h c", h=8), writes=[wukT])
        wuv = A.alloc("wuvd", 4 * 1024, BF16)
        self.dma("pool", wuv[:, :].rearrange("p (c n) -> p c n", c=4), w_uv_d.rearrange("(c p) n -> p c n", p=128), writes=[wuv])
        ptx = A.alloc("ptx", 128, I32, ckey="once")
        self.dma("sp", ptx[:, :], ptx_d[:, :], writes=[ptx])
        ptf = A.alloc("ptf", 128)
        idx = A.alloc("idx", 128, I32)
        S.op("dve", lambda e: e.tensor_copy(out=ptf[:, :], in_=ptx[:, :]), reads=[ptx], writes=[ptf])
        S.op("dve", lambda e: e.tensor_scalar(out=ptf[:, :], in0=ptf[:, :], scalar1=16.0, scalar2=cv[:, 16:17], op0=ALU.mult, op1=ALU.add),
             reads=[ptf, cv], writes=[ptf])
        S.op("dve", lambda e: e.tensor_copy(out=idx[:, :], in_=ptf[:, :]), reads=[ptf], writes=[idx])
        qlat = [A.alloc(f"qlat{cc}", 1024, BF16) for cc in range(4)]
        qrd = A.alloc("qrd", 1024, BF16)
        ei = 0
        for h in range(8):
            for cc in range(4):
                ps = self.nextps()
                self.mm(ps, ps[:, :128], wukT, wukT[:, h * 512 + cc * 128:h * 512 + (cc + 1) * 128], self.qn_dec, self.qn_dec[:, h * 128:(h + 1) * 128], True, True)
                dst = qlat[cc][:, :].rearrange("p (b h t) -> p b h t", b=16, h=8)[:, :, h, :]
                self.copy_evac(ei, dst, ps[:, :128].rearrange("p (b t) -> p b t", b=16), [ps], [qlat[cc]])
                ei += 1
            dst = qrd[:64, :].rearrange("p (b h t) -> p b h t", b=16, h=8)[:, :, h, :]
            S.op("pool", lambda e, dst=dst, h=h: e.tensor_copy(out=dst, in_=self.qr_dec[:64, h * 128:(h + 1) * 128].rearrange("p (b t) -> p b t", b=16)),
                 reads=[self.qr_dec], writes=[qrd])
        kv = [A.alloc(f"kvg{i}", 8 * 512, BF16) for i in range(2)]
        kr = [A.alloc(f"krg{i}", 8 * 64, BF16) for i in range(2)]
        ckT = [A.alloc(f"ckT{cc}", 1024, BF16) for cc in range(4)]
        krT = A.alloc("krT", 1024, BF16)
        pT = [A.alloc(f"pTd{i}", 512, BF16) for i in range(2)]
        lt = A.alloc("lt", 64)
        lacc = A.alloc("lacc", 64)
        pN = A.alloc("pN", 64)
        pNb = A.alloc("pNb", 64, BF16)
        rl = A.alloc("rld", 1)
        rlb = A.alloc("rlb", 64)
        on = A.alloc("on", 512, BF16)
        S.op("pool", lambda e: e.memset(on[:, :], 0.0), writes=[on])
        olat = [A.alloc(f"olat{cc}", 1024, BF16) for cc in range(4)]
        cview = cache_ckv_d.rearrange("r (a e) -> r a e", a=2)
        PT = self.PS[0:4]
        PTR = self.PS[4]
        PSS = self.PS[5]
        PSO = self.PS[6]
        PM = self.PS[7]
        gi = 0
        for b in range(16 if DS >= 2 else 0):
            for g in range(8):
                kvb = kv[gi % 2]
                krb = kr[gi % 2]
                col = b * 8 + g
                S.op("pool", lambda e, kvb=kvb, col=col: e.indirect_dma_start(
                    out=kvb[:, :], out_offset=None, in_=cache_ckv_d,
                    in_offset=bass.IndirectOffsetOnAxis(ap=idx[:, col:col + 1], axis=0)), reads=[idx], writes=[kvb], dma=True, chan_buf=kvb)
                S.op("pool", lambda e, krb=krb, col=col: e.indirect_dma_start(
                    out=krb[:, :], out_offset=None, in_=cache_kr_d,
                    in_offset=bass.IndirectOffsetOnAxis(ap=idx[:, col:col + 1], axis=0)), reads=[idx], writes=[krb], dma=True, chan_buf=krb)
                for cc in range(4):
                    pb = PT[cc]
                    pbv = pb[:, :].bitcast(BF16)
                    for jj in range(8):
                        S.op("pe", lambda e, pbv=pbv, kvb=kvb, jj=jj, cc=cc: e.transpose(pbv[:, jj * 128:(jj + 1) * 128], kvb[:, jj * 512 + cc * 128:jj * 512 + (cc + 1) * 128], identb[:, :]),
                             reads=[kvb, identb], writes=[pb])
                    self.copy_evac(cc, ckT[cc][:, :], pbv[:, 0:1024], [pb], [ckT[cc]])
                pbv = PTR[:, :].bitcast(BF16)
                for jj in range(8):
                    S.op("pe", lambda e, pbv=pbv, krb=krb, jj=jj: e.transpose(pbv[:64, jj * 128:(jj + 1) * 128], krb[:, jj * 64:(jj + 1) * 64], identb[:, :]),
                         reads=[krb, identb], writes=[PTR])
                self.copy_evac(1, krT[:64, :], pbv[:64, 0:1024], [PTR], [krT])
                if DS < 3:
                    gi += 1
                    continue
                for jj in range(8):
                    for cc in range(4):
                        self.mm(PSS, PSS[:, jj * 64:(jj + 1) * 64], ckT[cc], ckT[cc][:, jj * 128:(jj + 1) * 128], qlat[cc], qlat[cc][:, b * 64:(b + 1) * 64], cc == 0, False)
                    self.mm(PSS, PSS[:, jj * 64:(jj + 1) * 64], krT, krT[:64, jj * 128:(jj + 1) * 128], qrd, qrd[:64, b * 64:(b + 1) * 64], False, True)
                p = pT[gi % 2]
                S.op("act", lambda e, p=p: e.activation(out=p[:, :], in_=PSS[:, :512], func=AF.Exp, scale=SCALE), reads=[PSS], writes=[p])
                for jj in range(8):
                    first = (g == 0 and jj == 0)
                    for cc in range(4):
                        self.mm(PSO, PSO[:, cc * 64:(cc + 1) * 64], kvb, kvb[:, jj * 512 + cc * 128:jj * 512 + (cc + 1) * 128], p, p[:, jj * 64:(jj + 1) * 64], first, False)
                    self.mm(PSO, PSO[:, 256:320], self.ones, self.ones[:, :], p, p[:, jj * 64:(jj + 1) * 64], first, False)
                gi += 1
            if DS < 4:
                continue
            for cc in range(4):
                self.mm(PM, PM[:, 0:64], self.ckvT[cc], self.ckvT[cc][:, NP_:NTOK], qlat[cc], qlat[cc][:, b * 64:(b + 1) * 64], cc == 0, False)
            self.mm(PM, PM[:, 0:64], self.kropeT, self.kropeT[:64, NP_:NTOK], qrd, qrd[:64, b * 64:(b + 1) * 64], False, True)
            S.op("act", lambda e: e.activation(out=pN[:, :], in_=PM[:, 0:64], func=AF.Exp, scale=SCALE), reads=[PM], writes=[pN])
            S.op("dve", lambda e, b=b: e.tensor_tensor(out=pN[:, :], in0=pN[:, :], in1=dmask[:, b * 64:(b + 1) * 64], op=ALU.mult), reads=[pN, dmask], writes=[pN])
            S.op("dve", lambda e: e.tensor_copy(out=pNb[:, :], in_=pN[:, :]), reads=[pN], writes=[pNb])
            for cc in range(4):
                self.mm(PSO, PSO[:, cc * 64:(cc + 1) * 64], self.ckv_dec, self.ckv_dec[:, cc * 128:(cc + 1) * 128], pNb, pNb[:, :], False, True)
            self.mm(PSO, PSO[:, 256:320], self.ones, self.ones[:, :], pNb, pNb[:, :], False, True)
            S.op("dve", lambda e: e.reciprocal(out=rlb[:, :], in_=PSO[:, 256:320]), reads=[PSO], writes=[rlb])
            for cc in range(4):
                dst = olat[cc][:, :].rearrange("p (h b t) -> p h b t", h=8, b=16)[:, :, b, :]
                S.op("dve", lambda e, cc=cc, dst=dst: e.tensor_tensor(out=dst, in0=PSO[:, cc * 64:(cc + 1) * 64].rearrange("p (h t) -> p h t", h=8),
                                                                      in1=rlb[:, :].rearrange("p (h t) -> p h t", h=8), op=ALU.mult),
                     reads=[PSO, rlb], writes=[olat[cc]])
        self.rot = [0, 1, 2, 3]
        for h in range(8 if DS >= 5 else 0):
            ps = self.nextps()
            for cc in range(4):
                self.mm(ps, ps[:, :128], wuv, wuv[:, cc * 1024 + h * 128:cc * 1024 + (h + 1) * 128], olat[cc], olat[cc][:, h * 128:(h + 1) * 128], cc == 0, cc == 3)
            mg = self.merged[h]
            self.copy_evac(h, mg[:, NP_:NTOK], ps[:, :128], [ps], [mg])
        self.rot = None
        A.free(identb, onesf, dmask, wukT, wuv, ptx, ptf, idx, qrd, krT, lt, lacc, pN, pNb, rl, rlb, on, *qlat, *kv, *kr, *ckT, *pT, *olat)
        A.free(self.qn_dec, self.qr_dec, self.ckv_dec, self.kropeT, *self.ckvT, *self.cqn, self.rope_fm)


    def ssd_mixer(self, hT, xT, o_ssc, o_pssm, o_sssm):
        S, A, nc = self.S, self.A, self.nc
        cv = self.cv
        w_in = self.inp("ssm_w_in", [D, 10304])
        w_out = self.inp("ssm_w_out", [4096, D])
        self.ssm_wv = w_in.rearrange("(k p) n -> p k n", p=128)
        rows_d = self.inp("ssm_rows", [128, 128 + 8192])
        self.ssm_rows_d = rows_d
        kc_d = self.inp("ssm_consts", [128, SC_N])
        self.ssm_hist_d = self.inp("ssm_hist", [128, 48 * 48])
        st_d = self.inp("state_ssm", [16 * 4096, 128])
        ident_d = self.inp("ident", [128, 128])
        self.o_ssc = o_ssc
        KCB = A.alloc("ssm_kc", SC_N, ckey="once")
        self.dma("sp", KCB[:, :], kc_d[:, :], writes=[KCB])
        self.KCB = KCB
        cwb = A.alloc("ssm_cwb", 192 + 48, ckey="once")
        self.dma("sp", cwb[:, 0:192], self.inp("ssm_cw", [128, 192])[:, :], writes=[cwb])
        self.dma("sp", cwb[:, 192:240], self.inp("ssm_cb", [128, 48])[:, :], writes=[cwb])
        self.cwb = cwb
        identb = A.alloc("identb1", 128, BF16, ckey="oncep")
        self.dma("pool", identb[:, :], ident_d[:, :], writes=[identb])
        self.identb = identb
        onecol = A.alloc("onecol", 1)
        S.op("pool", lambda e: e.memset(onecol[:, :], 1.0), writes=[onecol])
        rows01 = A.alloc("rows01", 128, ckey="once")
        self.dma("sp", rows01[:, :], rows_d[:, 0:128], writes=[rows01])
        arow = A.alloc("arow", 64)
        S.op("act", lambda e: e.activation(out=arow[:, :], in_=rows01[:, 64:128], func=AF.Exp), reads=[rows01], writes=[arow])
        S.op("dve", lambda e: e.tensor_scalar(out=arow[:, :], in0=arow[:, :], scalar1=-1.0, scalar2=None, op0=ALU.mult), reads=[arow], writes=[arow])

        hl = A.alloc("hl", 48, ckey="stage")
        for k in range(KC):
            S.op("dve", lambda e, k=k: e.tensor_copy(out=hl[:, k * 3:(k + 1) * 3], in_=hT[k][:, NP_ - 3:NP_]), reads=[hT[k]], writes=[hl])
        def fill_h(ein, exi):
            self.dma("sp", ein[:, :], hl[:, :], reads=[hl], writes=[exi], chan_buf=hl)
        eoh, exo = self.allgather("exh", 48, fill_h)
        hall = A.alloc("hall", 4 * 48, ckey="exl")
        for r in range(4):
            self.dma("sp", hall[:, r * 48:(r + 1) * 48], eoh[r * 128:(r + 1) * 128, :], reads=[exo], writes=[hall],
                     chan_buf=hall, group=(r > 0))
        hh = A.alloc("hh", 48)
        S.op("dve", lambda e: e.tensor_scalar(out=hh[:, :], in0=hall[:, 0:48], scalar1=cv[:, 17:18], scalar2=None, op0=ALU.mult),
             reads=[hall, cv], writes=[hh])
        for r in range(1, 4):
            S.op("dve", lambda e, r=r: e.scalar_tensor_tensor(out=hh[:, :], in0=hall[:, r * 48:(r + 1) * 48], scalar=cv[:, 17 + r:18 + r], in1=hh[:, :],
                                                              op0=ALU.mult, op1=ALU.add), reads=[hall, cv, hh], writes=[hh])
        hTh = A.alloc("hTh", 48, BF16)
        S.op("dve", lambda e: e.tensor_copy(out=hTh[:, :], in_=hh[:, :]), reads=[hh], writes=[hTh])
        self.hTh = hTh
        A.free(hl, hall, hh)

        U_p, U_d, T_d, ONESF = KCB[:, 0:128], KCB[:, 128:256], KCB[:, 256:384], KCB[:, 384:512]
        Wdt = A.alloc("wdt", KC * 64, BF16)
        self.dma("pool", Wdt[:, :].rearrange("p (k n) -> p k n", k=KC), self.ssm_wv[:, :, 10240:10304], writes=[Wdt])
        PA = {n: A.alloc("pa_" + n, 9 * 64) for n in ("dt", "da", "nacum", "dtd", "cdec")}
        self.PA = PA
        self.ea_d = A.alloc("ea_d", 64)
        self.totd = A.alloc("totd", 64)
        totc = A.alloc("totc", 64)
        t64 = A.alloc("t64s", 64)
        for ti in range(9):
            c0 = ti * 128
            sl = slice(ti * 64, (ti + 1) * 64)
            ps = self.nextps()
            for k in range(KC):
                self.mm(ps, ps[:, :64], hT[k], hT[k][:, c0:c0 + 128], Wdt, Wdt[:, k * 64:(k + 1) * 64], k == 0, k == KC - 1)
            S.op("dve", lambda e, ps=ps: e.tensor_tensor(out=t64[:, :], in0=ps[:, :64], in1=rows01[:, 0:64], op=ALU.add), reads=[ps, rows01], writes=[t64])
            S.op("act", lambda e: e.activation(out=t64[:, :], in_=t64[:, :], func=AF.Exp), reads=[t64], writes=[t64])
            S.op("act", lambda e, sl=sl: e.activation(out=PA["dt"][:, sl], in_=t64[:, :], func=AF.Ln, bias=onecol[:, :], scale=1.0),
                 reads=[t64, onecol], writes=[PA["dt"]])
            S.op("dve", lambda e, sl=sl: e.tensor_tensor(out=PA["da"][:, sl], in0=PA["dt"][:, sl], in1=arow[:, :], op=ALU.mult),
                 reads=[PA["dt"], arow], writes=[PA["da"]])
            Um = U_p if ti < 8 else U_d
            Tm = ONESF if ti < 8 else T_d
            ps2 = self.nextps()
            self.mm(ps2, ps2[:, 0:64], KCB, Um, PA["da"], PA["da"][:, sl], True, True)
            self.mm(ps2, ps2[:, 64:128], KCB, Tm, PA["da"], PA["da"][:, sl], True, True)
            S.op("dve", lambda e, ps2=ps2, sl=sl: e.tensor_scalar(out=PA["nacum"][:, sl], in0=ps2[:, 0:64], scalar1=-1.0, scalar2=None, op0=ALU.mult),
                 reads=[ps2], writes=[PA["nacum"]])
            S.op("dve", lambda e, ps2=ps2, sl=sl: e.tensor_tensor(out=t64[:, :], in0=ps2[:, 64:128], in1=PA["nacum"][:, sl], op=ALU.add),
                 reads=[ps2, PA["nacum"]], writes=[t64])
            S.op("act", lambda e: e.activation(out=t64[:, :], in_=t64[:, :], func=AF.Exp), reads=[t64], writes=[t64])
            S.op("dve", lambda e, sl=sl: e.tensor_tensor(out=PA["dtd"][:, sl], in0=t64[:, :], in1=PA["dt"][:, sl], op=ALU.mult),
                 reads=[t64, PA["dt"]], writes=[PA["dtd"]])
            S.op("act", lambda e, ps2=ps2, sl=sl: e.activation(out=PA["cdec"][:, sl], in_=ps2[:, 64:128], func=AF.Exp), reads=[ps2], writes=[PA["cdec"]])
            if ti == 0:
                S.op("dve", lambda e, ps2=ps2: e.tensor_copy(out=totc[:, :], in_=ps2[:, 64:128]), reads=[ps2], writes=[totc])
            elif ti < 8:
                S.op("dve", lambda e, ps2=ps2: e.tensor_tensor(out=totc[:, :], in0=totc[:, :], in1=ps2[:, 64:128], op=ALU.add), reads=[ps2, totc], writes=[totc])
            else:
                S.op("act", lambda e, ps2=ps2: e.activation(out=self.ea_d[:, :], in_=ps2[:, 0:64], func=AF.Exp), reads=[ps2], writes=[self.ea_d])
                S.op("dve", lambda e, ps2=ps2: e.tensor_copy(out=self.totd[:, :], in_=ps2[:, 64:128]), reads=[ps2], writes=[self.totd])
        S.op("act", lambda e: e.activation(out=totc[:, :], in_=totc[:, :], func=AF.Exp), reads=[totc], writes=[totc])
        A.free(Wdt, t64, rows01, arow)

        self.wsl = [A.alloc(f"ssw{i}", KC * 256, BF16) for i in range(2)]
        self.wji = 0
        self.rot = [0, 1, 2, 3, 4]
        stT = A.alloc("stT", 512)
        st2 = [A.alloc(f"st2_{i}", 1024, ckey="stage") for i in range(2)]
        eo_st = []
        for g in range(8):
            xc = self.ssd_inproj_conv(g, hT, with_c=False)
            S.op("pool", lambda e: e.memset(stT[:, :], 0.0), writes=[stT])
            for ti in range(8):
                xtm, xdtd, xdt = self.ssd_tile_common(ti, g, xc, need_xdt=False)
                self.ssd_state_step(ti, g, xtm, xdtd, stT)
                A.free(xtm, xdtd, xdt)
            sb = st2[(g // 2) % 2]
            S.op("pool", lambda e, sb=sb, g=g: e.tensor_copy(out=sb[:, (g % 2) * 512:(g % 2 + 1) * 512], in_=stT[:, :]), reads=[stT], writes=[sb])
            A.free(*xc)
            if g % 2 == 1:
                def fill_s(ein, exi, sb=sb):
                    self.dma("sp", ein[:, :], sb[:, :], reads=[sb], writes=[exi], chan_buf=sb)
                eo_st.append(self.allgather(f"ex1_{g // 2}", 1024, fill_s))

        def fill_t(ein, exi):
            self.dma("sp", ein[:, :], totc[:, :], reads=[totc], writes=[exi], chan_buf=totc)
        eot, exot = self.allgather("ex1t", 64, fill_t)
        decall = A.alloc("decall", 4 * 64, ckey="exl")
        for r in range(4):
            self.dma("sp", decall[:, r * 64:(r + 1) * 64], eot[r * 128:(r + 1) * 128, :], reads=[exot], writes=[decall],
                     chan_buf=decall, group=(r > 0))
        dd = A.alloc("ddsel", 4 * 64)
        for r in range(4):
            S.op("dve", lambda e, r=r: e.tensor_scalar(out=dd[:, r * 64:(r + 1) * 64], in0=decall[:, r * 64:(r + 1) * 64], scalar1=cv[:, 24 + r:25 + r],
                                                       scalar2=cv[:, 28 + r:29 + r], op0=ALU.mult, op1=ALU.add), reads=[decall, cv], writes=[dd])
        A.free(decall, totc, *st2)

        stTb = A.alloc("stTb", 512, BF16)
        rowsg = A.alloc("rowsg", 1024, ckey="once")
        gT = A.alloc("gT", 4 * NTOK, BF16)
        wov = w_out.rearrange("(m p) n -> p m n", p=128)
        for g in range(8):
            xc = self.ssd_inproj_conv(g, hT, with_c=True)
            for i in range(2):
                b = self.wsl[i]
                self.dma("pool", b[:, :].rearrange("p (k n) -> p k n", k=KC), self.ssm_wv[:, :, g * 512 + i * 256:g * 512 + (i + 1) * 256], writes=[b])
            self.dma("sp", rowsg[:, 0:512], rows_d[:, 128 + g * 512:128 + (g + 1) * 512], writes=[rowsg])
            self.dma("sp", rowsg[:, 512:1024], rows_d[:, 128 + 4096 + g * 512:128 + 4096 + (g + 1) * 512], writes=[rowsg], group=True)
            S.op("pool", lambda e: e.memset(stT[:, :], 0.0), writes=[stT])
            st3 = stT[:, :].rearrange("p (r q) -> p r q", r=8)
            Sr = [A.alloc(f"Sr{i}", 512, ckey="exl") for i in range(2)]
            for r in range(4):
                sr = Sr[r % 2]
                eo, ex1o = eo_st[g // 2]
                self.dma("sp", sr[:, :], eo[r * 128:(r + 1) * 128, (g % 2) * 512:(g % 2 + 1) * 512], reads=[ex1o], writes=[sr], chan_buf=sr)
                S.op("dve", lambda e, r=r, g=g: e.tensor_tensor(out=st3, in0=st3, in1=dd[:, r * 64 + 8 * g:r * 64 + 8 * g + 8].unsqueeze(2).broadcast_to([128, 8, 64]),
                                                                op=ALU.mult), reads=[stT, dd], writes=[stT])
                S.op("dve", lambda e, r=r, sr=sr: e.scalar_tensor_tensor(out=stT[:, :], in0=sr[:, :], scalar=cv[:, 24 + r:25 + r], in1=stT[:, :],
                                                                         op0=ALU.mult, op1=ALU.add), reads=[sr, cv, stT], writes=[stT])
            A.free(*Sr)
            S.op("act", lambda e: e.activation(out=stTb[:, :], in_=stT[:, :], func=AF.Copy), reads=[stT], writes=[stTb])
            pend = None
            for ti in range(9):
                xtm, xdtd, xdt = self.ssd_tile_common(ti, g, xc, need_xdt=True)
                yoff_tm = None
                if ti == 8:
                    yoff_tm = self.ssd_decode_states(g, xc, xtm, xdtd, st_d, o_sssm)
                self.ssd_tile_y(ti, g, xc, xtm, xdt, stTb, yoff_tm, rowsg, gT, hT)
                if pend is not None:
                    self.ssd_tile_b(pend[0], g, pend[1], None, rowsg, gT, hT)
                    A.free(pend[1])
                if ti < 8:
                    self.ssd_state_step(ti, g, xtm, xdtd, stT)
                    S.op("act", lambda e: e.activation(out=stTb[:, :], in_=stT[:, :], func=AF.Copy), reads=[stT], writes=[stTb])
                A.free(xdtd, xdt)
                pend = (ti, xtm)
                if ti == 8:
                    self.ssd_tile_b(8, g, xtm, yoff_tm, rowsg, gT, hT)
                    A.free(xtm, yoff_tm)
                if ti == 7:
                    ps = self.nextps()
                    for q in range(4):
                        S.op("pe", lambda e, ps=ps, q=q: e.transpose(ps[:, q * 128:(q + 1) * 128], stT[:, q * 128:(q + 1) * 128], KCB[:, 512:640]),
                             reads=[stT, KCB], writes=[ps])
                    fo = A.alloc("pssm_o", 512, ckey="ost")
                    self.copy_evac(g, fo[:, :], ps[:, :512], [ps], [fo])
                    self.dma("sp", o_pssm[:, g * 512:(g + 1) * 512], fo[:, :], reads=[fo])
                    A.free(fo)
            A.free(*xc)
            gT3 = gT[:, :].rearrange("p (q t) -> p q t", q=4)
            for dp in range(4):
                b = self.wsl[self.wji % 2]
                self.wji += 1
                self.dma("pool", b[:, 0:2048].rearrange("p (q n) -> p q n", q=4), wov[:, g * 4:(g + 1) * 4, dp * 512:(dp + 1) * 512], writes=[b])
                for dcl in range(4):
                    dc = dp * 4 + dcl
                    for (s, n) in TB:
                        ps = self.nextps()
                        for q in range(4):
                            self.mm(ps, ps[:, :n], b, b[:, q * 512 + dcl * 128:q * 512 + (dcl + 1) * 128], gT, gT3[:, q, s:s + n], q == 0, q == 3)
                        S.op("dve", lambda e, ps=ps, dc=dc, s=s, n=n: e.tensor_tensor(out=xT[dc][:, s:s + n], in0=ps[:, :n], in1=xT[dc][:, s:s + n], op=ALU.add),
                             reads=[ps, xT[dc]], writes=[xT[dc]])
        self.rot = None
        A.free(stT, stTb, rowsg, gT, dd, *self.wsl, KCB, cwb, identb, onecol, hTh, self.ea_d, self.totd, *PA.values())

    def ssd_inproj_conv(self, g, hT, with_c):
        S, A = self.S, self.A
        wv = self.ssm_wv
        cwb = self.cwb
        jobs = [[g * 4, g * 4 + 1], [g * 4 + 2, g * 4 + 3], [32 + g] + ([40 + g] if with_c else [])]
        uext = [A.alloc(f"suext{i}", 3 + NP_ + 16 * 11) for i in range(2)]
        acc = A.alloc("sacc", NTOK)
        out = []
        ci = 0
        for chs in jobs:
            b = self.wsl[self.wji % 2]
            self.wji += 1
            b3 = b[:, :].rearrange("p (k n) -> p k n", k=KC)
            if len(chs) == 2 and chs[1] == chs[0] + 1:
                self.dma("pool", b3[:, :, 0:256], wv[:, :, 4096 + chs[0] * 128:4096 + chs[0] * 128 + 256], writes=[b])
            else:
                for i, ch in enumerate(chs):
                    self.dma("pool", b3[:, :, i * 128:(i + 1) * 128], wv[:, :, 4096 + ch * 128:4096 + (ch + 1) * 128], writes=[b], group=(i > 0))
            for i, ch in enumerate(chs):
                ue = uext[ci % 2]
                ci += 1
                udec = ue[:, 3 + NP_:].rearrange("p (b t) -> p b t", b=16)
                self.dma_nc("sp", udec[:, :, 0:3], self.ssm_hist_d[:, ch * 48:(ch + 1) * 48].rearrange("p (b t) -> p b t", b=16), writes=[ue])
                ps = self.nextps()
                for k in range(KC):
                    self.mm(ps, ps[:, :3], b, b[:, k * 256 + i * 128:k * 256 + (i + 1) * 128], self.hTh, self.hTh[:, k * 3:(k + 1) * 3], k == 0, k == KC - 1)
                S.op("act", lambda e, ps=ps, ue=ue: e.activation(out=ue[:, 0:3], in_=ps[:, :3], func=AF.Copy), reads=[ps], writes=[ue])
                ei = 0
                for (s, n) in TB:
                    ps = self.nextps()
                    for k in range(KC):
                        self.mm(ps, ps[:, :n], b, b[:, k * 256 + i * 128:k * 256 + (i + 1) * 128], hT[k], hT[k][:, s:s + n], k == 0, k == KC - 1)
                    if s < NP_:
                        self.copy_evac(ei, ue[:, 3 + s:3 + s + n], ps[:, :n], [ps], [ue])
                    else:
                        self.copy_evac(ei, udec[:, :, 3:11], ps[:, :n].rearrange("p (b t) -> p b t", b=16), [ps], [ue])
                    ei += 1
                wc = [cwb[:, ch * 4 + k:ch * 4 + k + 1] for k in range(4)]
                accd = acc[:, NP_:].rearrange("p (b t) -> p b t", b=16)
                S.op("dve", lambda e, ue=ue, wc=wc: e.tensor_scalar(out=acc[:, 0:NP_], in0=ue[:, 3:3 + NP_], scalar1=wc[3], scalar2=None, op0=ALU.mult),
                     reads=[ue, cwb], writes=[acc])
                S.op("dve", lambda e, udec=udec, wc=wc, accd=accd: e.tensor_scalar(out=accd, in0=udec[:, :, 3:11], scalar1=wc[3], scalar2=None, op0=ALU.mult),
                     reads=[ue, cwb], writes=[acc])
                for k in range(3):
                    S.op("dve", lambda e, ue=ue, wc=wc, k=k: e.scalar_tensor_tensor(out=acc[:, 0:NP_], in0=ue[:, k:k + NP_], scalar=wc[k], in1=acc[:, 0:NP_],
                                                                                    op0=ALU.mult, op1=ALU.add), reads=[ue, cwb, acc], writes=[acc])
                    S.op("dve", lambda e, udec=udec, wc=wc, k=k, accd=accd: e.scalar_tensor_tensor(out=accd, in0=udec[:, :, k:k + 8], scalar=wc[k], in1=accd,
                                                                                                   op0=ALU.mult, op1=ALU.add), reads=[ue, cwb, acc], writes=[acc])
                xo = A.alloc(f"sxc{len(out)}", NTOK, BF16)
                S.op("act", lambda e, xo=xo, ch=ch: e.activation(out=xo[:, :], in_=acc[:, :], func=AF.Silu, bias=cwb[:, 192 + ch:193 + ch], scale=1.0),
                     reads=[acc, cwb], writes=[xo])
                out.append(xo)
                if with_c:
                    osv = self.o_ssc[:, ch * 51:(ch + 1) * 51]
                    self.dma_nc("sp", osv[:, 0:3], ue[:, NP_:NP_ + 3], reads=[ue], chan_buf=ue)
                    self.dma_nc("sp", osv[:, 3:51].rearrange("p (b t) -> p b t", b=16), udec[:, :, 8:11], reads=[ue], chan_buf=ue)
        A.free(acc, *uext)
        return out

    def ssd_tile_common(self, ti, g, xc, need_xdt):
        S, A = self.S, self.A
        PA = self.PA
        c0 = ti * 128
        ps = self.nextps()
        pb = ps[:, :].bitcast(BF16)
        for q in range(5):
            S.op("pe", lambda e, pb=pb, q=q: e.transpose(pb[:, q * 128:(q + 1) * 128], xc[q][:, c0:c0 + 128], self.identb[:, :]),
                 reads=[xc[q], self.identb], writes=[ps])
        xtm = A.alloc("xtm", 640, BF16)
        S.op("act", lambda e: e.activation(out=xtm[:, :], in_=pb[:, 0:640], func=AF.Copy), reads=[ps], writes=[xtm])
        x3 = xtm[:, 0:512].rearrange("p (r q) -> p r q", r=8)
        h0 = ti * 64 + 8 * g
        xdtd = A.alloc("xdtd", 512, BF16)
        S.op("dve", lambda e: e.tensor_tensor(out=xdtd[:, :].rearrange("p (r q) -> p r q", r=8), in0=x3,
                                              in1=PA["dtd"][:, h0:h0 + 8].unsqueeze(2).broadcast_to([128, 8, 64]), op=ALU.mult),
             reads=[xtm, PA["dtd"]], writes=[xdtd])
        xdt = A.alloc("xdt", 512, BF16)
        if need_xdt:
            S.op("pool", lambda e: e.tensor_tensor(out=xdt[:, :].rearrange("p (r q) -> p r q", r=8), in0=x3,
                                                   in1=PA["dt"][:, h0:h0 + 8].unsqueeze(2).broadcast_to([128, 8, 64]), op=ALU.mult),
                 reads=[xtm, PA["dt"]], writes=[xdt])
        return xtm, xdtd, xdt

    def ssd_state_step(self, ti, g, xtm, xdtd, stT):
        S = self.S
        PA = self.PA
        h0 = ti * 64 + 8 * g
        ps = self.nextps()
        self.mm(ps, ps[:, :512], xtm, xtm[:, 512:640], xdtd, xdtd[:, :], True, True)
        st3 = stT[:, :].rearrange("p (r q) -> p r q", r=8)
        S.op("dve", lambda e: e.tensor_tensor(out=st3, in0=st3, in1=PA["cdec"][:, h0:h0 + 8].unsqueeze(2).broadcast_to([128, 8, 64]), op=ALU.mult),
             reads=[stT, PA["cdec"]], writes=[stT])
        S.op("dve", lambda e, ps=ps: e.tensor_tensor(out=stT[:, :], in0=stT[:, :], in1=ps[:, :512], op=ALU.add), reads=[stT, ps], writes=[stT])

    def ssd_tile_y(self, ti, g, xc, xtm, xdt, stTb, yoff_tm, rowsg, gT, hT):
        S, A = self.S, self.A
        PA, KCB = self.PA, self.KCB
        c0 = ti * 128
        dec_t = ti == 8
        Um = KCB[:, 128:256] if dec_t else KCB[:, 0:128]
        NEGM = KCB[:, 1152:1664] if dec_t else KCB[:, 640:1152]
        ONESF, IDF = KCB[:, 384:512], KCB[:, 512:640]
        xcB, xcC = xc[4], xc[5]
        pY = self.PS[6 + ti % 2]
        pc = self.nextps()
        self.mm(pc, pc[:, :128], xcB, xcB[:, c0:c0 + 128], xcC, xcC[:, c0:c0 + 128], True, True)
        cbT = A.alloc("cbT", 128, BF16)
        S.op("act", lambda e: e.activation(out=cbT[:, :], in_=pc[:, :128], func=AF.Copy), reads=[pc], writes=[cbT])
        R = A.alloc("Rda", 512)
        ea = A.alloc("ea", 512, BF16)
        CsT = A.alloc("CsT", 512, BF16)
        dec = A.alloc("decm", 512, BF16)
        MT = A.alloc("MT", 512, BF16)
        for hq in range(2):
            hb = ti * 64 + 8 * g + 4 * hq
            S.op("pool", lambda e, hb=hb: e.tensor_tensor(out=R[:, :].rearrange("p (r i) -> p r i", r=4), in0=Um.unsqueeze(1).broadcast_to([128, 4, 128]),
                                                          in1=PA["da"][:, hb:hb + 4].unsqueeze(2).broadcast_to([128, 4, 128]), op=ALU.mult),
                 reads=[KCB, PA["da"]], writes=[R])
            pA = self.nextps()
            self.mm(pA, pA[:, :512], KCB, ONESF, R, R[:, :], True, False)
            if not dec_t:
                S.op("act", lambda e, pA=pA: e.activation(out=ea[:, :], in_=pA[:, :512], func=AF.Exp), reads=[pA], writes=[ea])
                S.op("dve", lambda e: e.tensor_tensor(out=CsT[:, :].rearrange("p (r i) -> p r i", r=4), in0=ea[:, :].rearrange("p (r i) -> p r i", r=4),
                                                      in1=xcC[:, c0:c0 + 128].unsqueeze(1).broadcast_to([128, 4, 128]), op=ALU.mult),
                     reads=[ea, xcC], writes=[CsT])
            self.mm(pA, pA[:, :512], KCB, IDF, KCB, NEGM, False, True)
            for r4 in range(4):
                S.op("act", lambda e, pA=pA, r4=r4, hb=hb: e.activation(out=dec[:, r4 * 128:(r4 + 1) * 128], in_=pA[:, r4 * 128:(r4 + 1) * 128], func=AF.Exp,
                                                                        bias=PA["nacum"][:, hb + r4:hb + r4 + 1], scale=1.0),
                     reads=[pA, PA["nacum"]], writes=[dec])
            S.op("dve", lambda e: e.tensor_tensor(out=MT[:, :].rearrange("p (r i) -> p r i", r=4), in0=dec[:, :].rearrange("p (r i) -> p r i", r=4),
                                                  in1=cbT[:, :].unsqueeze(1).broadcast_to([128, 4, 128]), op=ALU.mult), reads=[dec, cbT], writes=[MT])
            for r4 in range(4):
                h = 4 * hq + r4
                self.mm(pY, pY[:, h * 64:(h + 1) * 64], MT, MT[:, r4 * 128:(r4 + 1) * 128], xdt, xdt[:, h * 64:(h + 1) * 64], True, dec_t)
                if not dec_t:
                    self.mm(pY, pY[:, h * 64:(h + 1) * 64], CsT, CsT[:, r4 * 128:(r4 + 1) * 128], stTb, stTb[:, h * 64:(h + 1) * 64], False, True)
        A.free(R, ea, CsT, dec, MT, cbT)

    def ssd_tile_b(self, ti, g, xtm, yoff_tm, rowsg, gT, hT):
        S, A = self.S, self.A
        c0 = ti * 128
        pY = self.PS[6 + ti % 2]
        ysb = A.alloc("ysb", 512)
        S.op("pool", lambda e: e.tensor_tensor(out=ysb[:, :], in0=xtm[:, 0:512], in1=rowsg[:, 0:512], op=ALU.mult), reads=[xtm, rowsg], writes=[ysb])
        S.op("dve", lambda e: e.tensor_tensor(out=ysb[:, :], in0=ysb[:, :], in1=pY[:, :512], op=ALU.add), reads=[ysb, pY], writes=[ysb])
        if yoff_tm is not None:
            S.op("dve", lambda e: e.tensor_tensor(out=ysb[:, :], in0=ysb[:, :], in1=yoff_tm[:, :], op=ALU.add), reads=[ysb, yoff_tm], writes=[ysb])
        pZ = self.nextps()
        for i in range(2):
            w = self.wsl[i]
            for k in range(KC):
                self.mm(pZ, pZ[:, i * 256:(i + 1) * 256], hT[k], hT[k][:, c0:c0 + 128], w, w[:, k * 256:(k + 1) * 256], k == 0, k == KC - 1)
        sz = A.alloc("sz", 512)
        S.op("act", lambda e: e.activation(out=sz[:, :], in_=pZ[:, :512], func=AF.Silu), reads=[pZ], writes=[sz])
        S.op("dve", lambda e: e.tensor_tensor(out=ysb[:, :], in0=ysb[:, :], in1=sz[:, :], op=ALU.mult), reads=[ysb, sz], writes=[ysb])
        ssq = A.alloc("ssq1", 2)
        S.op("act", lambda e: e.activation(out=sz[:, :], in_=ysb[:, :], func=AF.Square, accum_out=ssq[:, 0:1]), reads=[ysb], writes=[sz, ssq])
        S.op("act", lambda e: e.activation(out=ssq[:, 1:2], in_=ssq[:, 0:1], func=AF.Sqrt, bias=self.epsb[:, :], scale=1.0 / 512),
             reads=[ssq, self.epsb], writes=[ssq])
        S.op("dve", lambda e: e.reciprocal(out=ssq[:, 1:2], in_=ssq[:, 1:2]), reads=[ssq], writes=[ssq])
        gn = A.alloc("gn", 512, BF16)
        S.op("dve", lambda e: e.scalar_tensor_tensor(out=gn[:, :], in0=ysb[:, :], scalar=ssq[:, 1:2], in1=rowsg[:, 512:1024], op0=ALU.mult, op1=ALU.mult),
             reads=[ysb, ssq, rowsg], writes=[gn])
        pt = self.nextps()
        ptb = pt[:, :].bitcast(BF16)
        for q in range(4):
            S.op("pe", lambda e, q=q: e.transpose(ptb[:, q * 128:(q + 1) * 128], gn[:, q * 128:(q + 1) * 128], self.identb[:, :]),
                 reads=[gn, self.identb], writes=[pt])
        gT3 = gT[:, :].rearrange("p (q t) -> p q t", q=4)
        S.op("act", lambda e: e.activation(out=gT3[:, :, c0:c0 + 128], in_=ptb[:, 0:512].rearrange("p (q t) -> p q t", q=4), func=AF.Copy),
             reads=[pt], writes=[gT])
        A.free(ysb, sz, ssq, gn)

    def ssd_decode_states(self, g, xc, xtm, xdtd, st_d, o_sssm):
        S, A = self.S, self.A
        KCB = self.KCB
        IDF, SEQM, SEL = KCB[:, 512:640], KCB[:, 1664:1680], KCB[:, 1680:1696]
        xcC = xc[5]
        pYo = self.PS[5]
        stv = st_d.rearrange("(b t p) n -> p b t n", b=16, t=32)
        aexp = A.alloc("aexp", 512)
        S.op("dve", lambda e: e.tensor_copy(out=aexp[:, :].rearrange("p (r q) -> p r q", r=8), in_=self.totd[:, 8 * g:8 * g + 8].unsqueeze(2).broadcast_to([128, 8, 64])),
             reads=[self.totd], writes=[aexp])
        pD = self.nextps()
        for q in range(4):
            self.mm(pD, pD[:, q * 16:(q + 1) * 16], aexp, aexp[:, q * 128:(q + 1) * 128], KCB, SEL, True, True)
        decfm = A.alloc("decfm", 64)
        S.op("act", lambda e: e.activation(out=decfm[:, :], in_=pD[:, 0:64], func=AF.Exp), reads=[pD], writes=[decfm])
        h0b = [A.alloc(f"h0b{i}", 512, ckey=f"h0b{i}") for i in range(2)]
        nsb = [A.alloc(f"nsb{i}", 512, ckey=f"nsb{i}") for i in range(2)]
        h0Ts = [A.alloc(f"h0T{i}", 512, BF16) for i in range(2)]
        Bms = [A.alloc(f"Bm{i}", 128, BF16) for i in range(2)]
        for b in range(16):
            h0 = h0b[b % 2]
            ns = nsb[b % 2]
            h0T = h0Ts[b % 2]
            Bm = Bms[b % 2]
            if b == 0:
                self.dma("sp", h0[:, :].rearrange("p (t n) -> p t n", t=4), stv[:, 0, g * 4:(g + 1) * 4, :], writes=[h0])
            if b + 1 < 16:
                hn = h0b[(b + 1) % 2]
                self.dma("sp", hn[:, :].rearrange("p (t n) -> p t n", t=4), stv[:, b + 1, g * 4:(g + 1) * 4, :], writes=[hn])
            pH = self.nextps()
            for q in range(4):
                S.op("pe", lambda e, pH=pH, q=q, h0=h0: e.transpose(pH[:, q * 128:(q + 1) * 128], h0[:, q * 128:(q + 1) * 128], IDF),
                     reads=[h0, KCB], writes=[pH])
            self.copy_evac(b, h0T[:, :], pH[:, :512], [pH], [h0T])
            for q in range(4):
                self.mm(pYo, pYo[:, q * 128 + b * 8:q * 128 + b * 8 + 8], h0T, h0T[:, q * 128:(q + 1) * 128], xcC, xcC[:, NP_ + b * 8:NP_ + b * 8 + 8], True, True)
            S.op("pool", lambda e, b=b, Bm=Bm: e.tensor_scalar(out=Bm[:, :], in0=xtm[:, 512:640], scalar1=SEQM[:, b:b + 1], scalar2=None, op0=ALU.mult),
                 reads=[xtm, KCB], writes=[Bm])
            pN = self.nextps()
            for q in range(4):
                self.mm(pN, pN[:, q * 128:(q + 1) * 128], xdtd, xdtd[:, q * 128:(q + 1) * 128], Bm, Bm[:, :], True, True)
            for q in range(4):
                S.op("dve", lambda e, q=q, b=b, h0=h0, ns=ns, pN=pN: e.scalar_tensor_tensor(
                    out=ns[:, q * 128:(q + 1) * 128], in0=h0[:, q * 128:(q + 1) * 128], scalar=decfm[:, q * 16 + b:q * 16 + b + 1],
                    in1=pN[:, q * 128:(q + 1) * 128], op0=ALU.mult, op1=ALU.add), reads=[h0, decfm, pN], writes=[ns])
            self.dma("sp", o_sssm[b * 128:(b + 1) * 128, g * 512:(g + 1) * 512], ns[:, :], reads=[ns])
        yoT = A.alloc("yoT", 512, BF16)
        S.op("act", lambda e: e.activation(out=yoT[:, :], in_=pYo[:, :512], func=AF.Copy), reads=[pYo], writes=[yoT])
        pt = self.nextps()
        ptb = pt[:, :].bitcast(BF16)
        for q in range(4):
            S.op("pe", lambda e, q=q: e.transpose(ptb[:, q * 128:(q + 1) * 128], yoT[:, q * 128:(q + 1) * 128], self.identb[:, :]),
                 reads=[yoT, self.identb], writes=[pt])
        yoff = A.alloc("yoff_tm", 512)
        S.op("dve", lambda e: e.tensor_tensor(out=yoff[:, :].rearrange("p (r q) -> p r q", r=8), in0=ptb[:, 0:512].rearrange("p (r q) -> p r q", r=8),
                                              in1=self.ea_d[:, 8 * g:8 * g + 8].unsqueeze(2).broadcast_to([128, 8, 64]), op=ALU.mult),
             reads=[pt, self.ea_d], writes=[yoff])
        A.free(aexp, decfm, yoT, *h0Ts, *Bms, *h0b, *nsb)
        return yoff


PV_NORM_FFN = 0
PV_NORM_MIX = 64
PV_NORM_FIN = 96
PV_QNORM = 112
PV_SCW = 120
PV_KVN = 144
PV_N = 160


def _fm(v, nchunk):
    return np.ascontiguousarray(np.asarray(v, np.float32).reshape(nchunk, 128).T)


def _rope_tables(pos):
    inv = np.power(np.float32(10000.0), -np.arange(0, 64, 2, dtype=np.float32) / np.float32(64))
    ang = pos.astype(np.float32)[:, None] * inv[None, :]
    c = np.cos(ang).astype(np.float32)
    s = np.sin(ang).astype(np.float32)
    cos2 = np.concatenate([c, c], axis=1)
    sin2 = np.concatenate([-s, s], axis=1)
    return cos2, sin2


def _bc(v, n=128):
    v = np.asarray(v, np.float32).reshape(1, -1)
    return np.ascontiguousarray(np.broadcast_to(v, (n, v.shape[1])))


SC_N = 1696


def _ssd_consts():
    f = np.float32
    k = np.arange(128)[:, None]
    i = np.arange(128)[None, :]
    same = (k // 8 == i // 8)
    U_p = (k <= i).astype(f)
    U_d = (same & (k <= i)).astype(f)
    T_d = same.astype(f)
    ones = np.ones((128, 128), f)
    idf = np.eye(128, dtype=f)
    negp = np.where(i < k, NEG, 0.0).astype(f)
    negd = np.where(same & (i >= k), 0.0, NEG).astype(f)
    seqm = (k // 8 == np.arange(16)[None, :]).astype(f)
    sel = (k == 8 * np.arange(16)[None, :]).astype(f)
    return np.ascontiguousarray(np.concatenate([U_p, U_d, T_d, ones, idf, np.tile(negp, (1, 4)), np.tile(negd, (1, 4)), seqm, sel], axis=1))


def _ssd_host_shared(inp):
    f = np.float32
    g = lambda k: np.asarray(inp[k], f)
    cwv = g("ssm_conv_w")[0]
    rows = np.concatenate([g("ssm_dt_bias")[0], g("ssm_a_log")[0], np.repeat(g("ssm_d")[0], 64), g("ssm_norm")[0]])
    return {"ssm_w_in": g("ssm_w_in")[0], "ssm_w_out": g("ssm_w_out")[0],
            "ssm_cw": np.ascontiguousarray(cwv.reshape(4, 48, 128).transpose(2, 1, 0).reshape(128, 192)),
            "ssm_cb": _fm(g("ssm_conv_b")[0], 48), "ssm_rows": _bc(rows), "ssm_consts": _ssd_consts()}


def _ssd_host_core(inp, c):
    f = np.float32
    hs = np.asarray(inp["state_ssm_conv"], f)[0, 16 * c:16 * c + 16]
    hist = np.ascontiguousarray(hs.reshape(16, 3, 48, 128).transpose(3, 2, 0, 1).reshape(128, 48 * 48))
    st = np.asarray(inp["state_ssm"], f)[0, 16 * c:16 * c + 16].reshape(16 * 4096, 128)
    return {"ssm_hist": hist, "state_ssm": st}


def run_step(inp, mode="full", stop=99):
    f = np.float32
    g = lambda k: np.asarray(inp[k], f)
    npool = int(np.asarray(inp["cache_ckv"]).shape[1]) if "cache_ckv" in inp else 10240
    prog = Prog(mode=mode, npool=npool, stop=stop)
    nc = prog.build()
    need = set(prog.din)
    shared = {}
    pvh = np.zeros((128, PV_N), f)
    nf = g("norm_ffn").reshape(4, D)
    for i in range(4):
        pvh[:, PV_NORM_FFN + 16 * i:PV_NORM_FFN + 16 * (i + 1)] = _fm(nf[i], 16)
    nm = g("norm_mix")
    for i in range(2):
        pvh[:, PV_NORM_MIX + 16 * i:PV_NORM_MIX + 16 * (i + 1)] = _fm(nm[i], 16)
    pvh[:, PV_NORM_FIN:PV_NORM_FIN + 16] = _fm(g("norm_final"), 16)
    pvh[:, PV_QNORM:PV_QNORM + 6] = _fm(g("mla_q_norm")[0], 6)
    pvh[:, PV_KVN:PV_KVN + 4] = _fm(g("mla_kv_norm")[0], 4)
    scw = g("sconv_w")[0]
    pvh[:, PV_SCW:PV_SCW + 24] = scw.reshape(3, 8, 128).transpose(2, 1, 0).reshape(128, 24)
    shared["pv"] = pvh
    if "ffn_w_gate" in need:
        shared["ffn_w_gate"] = g("ffn_w_gate").reshape(4, D, FF)
        shared["ffn_w_up"] = g("ffn_w_up").reshape(4, D, FF)
        shared["ffn_w_down"] = g("ffn_w_down").reshape(4, FF, D)
    if "ab_w_in" in need:
        shared["ab_w_in"] = g("ab_w_in")[0]
        shared["kvn"] = _bc(g("mla_kv_norm")[0])
        w_uk = g("mla_w_uk")[0]
        shared["mla_w_uq"] = g("mla_w_uq")[0]
        shared["mla_w_uk"] = w_uk.reshape(KVL, 1024)
        shared["mla_w_uv"] = g("mla_w_uv")[0].reshape(KVL, 1024)
        shared["w_ukT"] = np.ascontiguousarray(w_uk.transpose(1, 2, 0).reshape(1024, KVL))
        shared["ab_w_out"] = g("ab_w_out")[0]
        shared["cache_ckv"] = g("cache_ckv").reshape(npool * 16, 4096)
        shared["cache_krope"] = g("cache_krope").reshape(npool * 16, 512)
        sidx = np.arange(128)[:, None]
        qidx = np.arange(512)[None, :]
        shared["cmask"] = np.concatenate([(qidx >= r * 128 + sidx).astype(f) for r in range(4)], axis=1)
        kb_, kt_ = np.arange(128) // 8, np.arange(128) % 8
        dmask = np.zeros((128, 16, 8, 8), f)
        for b in range(16):
            for t in range(8):
                dmask[:, b, :, t] = ((kb_ == b) & (kt_ <= t)).astype(f)[:, None]
        shared["dmask"] = dmask.reshape(128, 1024)
        ssc = g("state_sconv")[0]
        pt = np.asarray(inp["page_table"], np.int32)
    shared["ident"] = np.eye(128, dtype=f)
    if "ssm_w_in" in need:
        shared.update(_ssd_host_shared(inp))
    xp = g("x_prompt")
    xs = g("x_sample")
    in_maps = []
    for c in range(8):
        b, j = c // 4, c % 4
        xt = np.concatenate([xp[b, j * 1024:(j + 1) * 1024], xs[16 * c:16 * c + 16].reshape(128, D)], axis=0)
        m = dict(shared)
        m["xT"] = np.ascontiguousarray(xt.T.reshape(KC, 128, NTOK).transpose(1, 0, 2).reshape(128, KC * NTOK))
        cvh = np.zeros((128, 32), f)
        for kb in range(4):
            cvh[:, kb] = 0.0 if kb <= j else NEG
            cvh[:, 4 + kb] = 0.0 if kb < j else NEG
            cvh[:, 8 + kb] = 1.0 if kb == j else 0.0
            cvh[:, 12 + kb] = 1.0 if kb < j else 0.0
            cvh[:, 17 + kb] = 1.0 if kb == j - 1 else 0.0
            cvh[:, 24 + kb] = 1.0 if kb < j else 0.0
            cvh[:, 28 + kb] = 0.0 if kb < j else 1.0
        cvh[:, 16] = np.arange(128) % 16
        m["cv"] = cvh
        if "ab_w_in" in need:
            pos = np.concatenate([np.arange(j * 1024, (j + 1) * 1024), np.tile(8192 + np.arange(8), 16)])
            cos2, sin2 = _rope_tables(pos)
            rt = np.concatenate([cos2, sin2], axis=1)
            m["rope_tm"] = np.ascontiguousarray(rt.reshape(9, 128, 128).transpose(1, 0, 2).reshape(128, 9 * 128))
            rope_fm = np.zeros((128, 2 * NTOK), f)
            rope_fm[:64, :NTOK] = cos2.T
            rope_fm[:64, NTOK:] = sin2.T
            m["rope_fm"] = rope_fm
            h = ssc[16 * c:16 * c + 16]
            m["sc_hist"] = np.ascontiguousarray(h.reshape(16, 2, 8, 128).transpose(3, 2, 0, 1).reshape(128, 8 * 32))
            ptc = pt[16 * c:16 * c + 16]
            m["ptx"] = np.ascontiguousarray(ptc.reshape(16, 8, 8)[:, :, np.arange(128) // 16].transpose(2, 0, 1).reshape(128, 128)).astype(np.int32)
        if "ssm_w_in" in need:
            m.update(_ssd_host_core(inp, c))
        in_maps.append({k: v for k, v in m.items() if k in need})
    res = run_bass_kernel_spmd(nc, in_maps, core_ids=list(range(8)))
    R = res.results
    y_prompt = np.zeros((2, 4096, D), f)
    y_sample = np.zeros((128, 8, D), f)
    p_ckv = np.zeros((1, 2, 4096, KVL), f)
    p_kr = np.zeros((1, 2, 4096, ROPE), f)
    s_ckv = np.zeros((1, 128, 8, KVL), f)
    s_kr = np.zeros((1, 128, 8, ROPE), f)
    p_sc = np.zeros((1, 2, 2, SCD), f)
    s_sc = np.zeros((1, 128, 2, SCD), f)
    p_ssc = np.zeros((1, 2, 3, 6144), f)
    s_ssc = np.zeros((1, 128, 3, 6144), f)
    p_ssm = np.zeros((1, 2, 64, 64, 128), f)
    s_ssm = np.zeros((1, 128, 64, 64, 128), f)
    for c in range(8):
        b, j = c // 4, c % 4
        r = R[c]
        yt = r["o_yT"].reshape(128, KC, NTOK).transpose(2, 1, 0).reshape(NTOK, D)
        y_prompt[b, j * 1024:(j + 1) * 1024] = yt[:1024]
        y_sample[16 * c:16 * c + 16] = yt[1024:].reshape(16, 8, D)
        if "o_ssc" in r:
            ossc = r["o_ssc"].reshape(128, 48, 51)
            if j == 3:
                p_ssc[0, b] = ossc[:, :, 0:3].transpose(2, 1, 0).reshape(3, 6144)
                p_ssm[0, b] = r["o_pssm"].reshape(2, 64, 32, 128).transpose(2, 0, 1, 3).reshape(64, 64, 128)
            s_ssc[0, 16 * c:16 * c + 16] = ossc[:, :, 3:51].reshape(128, 48, 16, 3).transpose(2, 3, 1, 0).reshape(16, 3, 6144)
            s_ssm[0, 16 * c:16 * c + 16] = r["o_sssm"].reshape(16, 2, 64, 32, 128).transpose(0, 3, 1, 2, 4).reshape(16, 64, 64, 128)
        if "o_ckv" not in r:
            continue
        p_ckv[0, b, j * 1024:(j + 1) * 1024] = r["o_ckv"][:1024]
        s_ckv[0, 16 * c:16 * c + 16] = r["o_ckv"][1024:].reshape(16, 8, KVL)
        p_kr[0, b, j * 1024:(j + 1) * 1024] = r["o_kr"][:1024]
        s_kr[0, 16 * c:16 * c + 16] = r["o_kr"][1024:].reshape(16, 8, ROPE)
        osc = r["o_sc"].reshape(128, 8, 34)
        if j == 3:
            p_sc[0, b] = osc[:, :, 0:2].transpose(2, 1, 0).reshape(2, SCD)
        s_sc[0, 16 * c:16 * c + 16] = osc[:, :, 2:34].reshape(128, 8, 16, 2).transpose(2, 3, 1, 0).reshape(16, 2, SCD)
    return (y_prompt, y_sample, p_ckv, p_kr, p_sc, p_ssc, p_ssm, s_ckv, s_kr, s_sc, s_ssc, s_ssm)


def kernel(x_prompt, x_sample, cache_ckv, cache_krope, state_sconv, state_ssm_conv, state_ssm, page_table,
           norm_ffn, ffn_w_gate, ffn_w_up, ffn_w_down, norm_mix, ab_w_in, mla_q_norm, mla_w_uq, mla_kv_norm,
           mla_w_uk, mla_w_uv, sconv_w, ab_w_out, ssm_w_in, ssm_conv_w, ssm_conv_b, ssm_dt_bias, ssm_a_log,
           ssm_d, ssm_norm, ssm_w_out, norm_final):
    inp = dict(x_prompt=x_prompt, x_sample=x_sample, cache_ckv=cache_ckv, cache_krope=cache_krope, state_sconv=state_sconv,
               state_ssm_conv=state_ssm_conv, state_ssm=state_ssm, page_table=page_table, norm_ffn=norm_ffn,
               ffn_w_gate=ffn_w_gate, ffn_w_up=ffn_w_up, ffn_w_down=ffn_w_down, norm_mix=norm_mix, ab_w_in=ab_w_in,
               mla_q_norm=mla_q_norm, mla_w_uq=mla_w_uq, mla_kv_norm=mla_kv_norm, mla_w_uk=mla_w_uk, mla_w_uv=mla_w_uv,
               sconv_w=sconv_w, ab_w_out=ab_w_out, ssm_w_in=ssm_w_in, ssm_conv_w=ssm_conv_w, ssm_conv_b=ssm_conv_b,
               ssm_dt_bias=ssm_dt_bias, ssm_a_log=ssm_a_log, ssm_d=ssm_d, ssm_norm=ssm_norm, ssm_w_out=ssm_w_out,
               norm_final=norm_final)
    return run_step(inp, "full")
```
